# Optimizing a Trainium2 kernel written in Bass

```python
import math
import jax, jax.numpy as jnp
from jax import lax
import numpy as np

D_MODEL = 1024
BATCH = 32
SEQ = 2048
DEPTH = 1
DEC_BATCH = 8
DEC_SEQ = 16
PAST_LEN = 2048

CHUNK = 64
D_SSM = D_MODEL // 2
SSM_GROUP = 16
SSM_GROUPS = D_SSM // SSM_GROUP
SSM_STATE = 64
D_CONV = D_MODEL // 2
CONV_WIDTH = 31
D_FF = ((8 * D_MODEL // 3 + 127) // 128) * 128
FFN_RES = 0.5
EPS = 1e-6
DT_MIN = 0.001
DT_MAX = 0.1
N_IN = D_SSM + 2 * D_CONV + 2 * D_MODEL

kernel_name = 'streaming_s5_conformer_hybrid_step'


def _rmsnorm(x, g):
    xf = x.astype(jnp.float32)
    y = xf * lax.rsqrt(jnp.mean(xf * xf, axis=-1, keepdims=True) + EPS)
    return (y * g.astype(jnp.float32)).astype(x.dtype)


def _layernorm(x, g, b):
    xf = x.astype(jnp.float32)
    xc = xf - jnp.mean(xf, axis=-1, keepdims=True)
    var = jnp.mean(xc * xc, axis=-1, keepdims=True)
    y = xc * lax.rsqrt(var + EPS) * g.astype(jnp.float32) + b.astype(jnp.float32)
    return y.astype(x.dtype)


def _swiglu(x, w1, w3, w2):
    return (jax.nn.silu(x @ w1) * (x @ w3)) @ w2


def _s5_discretize(lam_re, lam_im, log_step, b_re, b_im):
    f32 = jnp.float32
    lam_re = lam_re.astype(f32)
    lam_im = lam_im.astype(f32)
    dt = jnp.exp(log_step.astype(f32))[:, None]
    mag = jnp.exp(lam_re * dt)
    a_re = mag * jnp.cos(lam_im * dt)
    a_im = mag * jnp.sin(lam_im * dt)
    num_re = a_re - 1.0
    inv_den = 1.0 / (lam_re * lam_re + lam_im * lam_im)
    k_re = (num_re * lam_re + a_im * lam_im) * inv_den
    k_im = (a_im * lam_re - num_re * lam_im) * inv_den
    b_re = b_re.astype(f32)
    b_im = b_im.astype(f32)
    bb_re = k_re[..., None] * b_re - k_im[..., None] * b_im
    bb_im = k_re[..., None] * b_im + k_im[..., None] * b_re
    return a_re, a_im, bb_re, bb_im


def _complex_affine_combine(e1, e2):
    a1r, a1i, b1r, b1i = e1
    a2r, a2i, b2r, b2i = e2
    return (a2r * a1r - a2i * a1i,
            a2r * a1i + a2i * a1r,
            a2r * b1r - a2i * b1i + b2r,
            a2r * b1i + a2i * b1r + b2i)


def _s5(u, p, h0_re, h0_im):
    f32 = jnp.float32
    bsz, length, _ = u.shape
    a_re, a_im, bb_re, bb_im = _s5_discretize(p['ssm_lambda_re'], p['ssm_lambda_im'], p['ssm_log_step'], p['ssm_b_re'], p['ssm_b_im'])
    uf = u.astype(f32)
    ug = uf.reshape(bsz, length, SSM_GROUPS, SSM_GROUP)
    bu_re = jnp.einsum('blgc,gpc->blgp', ug, bb_re)
    bu_im = jnp.einsum('blgc,gpc->blgp', ug, bb_im)
    shape = bu_re.shape
    pr, pi, xr, xi = lax.associative_scan(
        _complex_affine_combine,
        (jnp.broadcast_to(a_re, shape), jnp.broadcast_to(a_im, shape), bu_re, bu_im),
        axis=1)
    if h0_re is not None:
        h0r = h0_re.astype(f32)[:, None]
        h0i = h0_im.astype(f32)[:, None]
        xr, xi = xr + pr * h0r - pi * h0i, xi + pr * h0i + pi * h0r
    c_re = p['ssm_c_re'].astype(f32)
    c_im = p['ssm_c_im'].astype(f32)
    y = jnp.einsum('blgp,gcp->blgc', xr, c_re) - jnp.einsum('blgp,gcp->blgc', xi, c_im)
    y = y.reshape(bsz, length, D_SSM) + p['ssm_d'].astype(f32) * uf
    return y.astype(u.dtype), xr[:, -1], xi[:, -1]


def _mixer(h, p, h0_re, h0_im, conv_buf):
    proj = h @ p['w_in']
    u, conv_a, conv_g, gate_s, gate_c = jnp.split(
        proj, [D_SSM, D_SSM + D_CONV, D_SSM + 2 * D_CONV, D_SSM + 2 * D_CONV + D_MODEL], axis=-1)
    y_ssm, new_re, new_im = _s5(u, p, h0_re, h0_im)
    z = jax.nn.gelu(y_ssm)
    z = z * jax.nn.sigmoid(z @ p['ssm_glu_w'] + p['ssm_glu_b'])
    br_s = z @ p['ssm_out_w']
    v = conv_a * jax.nn.sigmoid(conv_g)
    if conv_buf is None:
        conv_buf = jnp.zeros((v.shape[0], CONV_WIDTH - 1, D_CONV), v.dtype)
    vp = jnp.concatenate([conv_buf.astype(v.dtype), v], axis=1)
    new_buf = vp[:, vp.shape[1] - (CONV_WIDTH - 1):]
    c = lax.conv_general_dilated(
        vp, p['conv_dw_w'][:, None, :].astype(v.dtype), (1,), 'VALID',
        dimension_numbers=('NWC', 'WIO', 'NWC'), feature_group_count=D_CONV)
    c = c + p['conv_dw_b']
    c = jax.nn.silu(_layernorm(c, p['conv_ln_g'], p['conv_ln_b']))
    br_c = c @ p['conv_out_w']
    mix = jax.nn.sigmoid(gate_s) * br_s + jax.nn.sigmoid(gate_c) * br_c
    return mix @ p['w_o'], new_re, new_im, new_buf


def _layer(x, p, h0_re, h0_im, conv_buf):
    x = x + FFN_RES * _swiglu(_rmsnorm(x, p['ffn1_norm']), p['ffn1_w1'], p['ffn1_w3'], p['ffn1_w2'])
    m, new_re, new_im, new_buf = _mixer(_rmsnorm(x, p['mix_norm']), p, h0_re, h0_im, conv_buf)
    x = x + m
    x = x + FFN_RES * _swiglu(_rmsnorm(x, p['ffn2_norm']), p['ffn2_w1'], p['ffn2_w3'], p['ffn2_w2'])
    return x, new_re, new_im, new_buf


def _trunk(x, layer_params, final_norm, ssm_re, ssm_im, conv_cache):
    res_re, res_im, res_buf = [], [], []
    for l in range(DEPTH):
        p = {k: v[l] for k, v in layer_params.items()}
        h0r = None if ssm_re is None else ssm_re[l]
        h0i = None if ssm_im is None else ssm_im[l]
        cb = None if conv_cache is None else conv_cache[l]
        x, nr, ni, nb = _layer(x, p, h0r, h0i, cb)
        res_re.append(nr)
        res_im.append(ni)
        res_buf.append(nb)
    return _rmsnorm(x, final_norm), jnp.stack(res_re), jnp.stack(res_im), jnp.stack(res_buf)


def setup_inputs(seed: int = 0) -> dict:
    key = jax.random.key(seed)
    ks = jax.random.split(key, 40)
    f32 = jnp.float32

    def nrm(k, shape, scale):
        return jax.random.normal(k, shape, f32) * scale

    def gain(k, shape):
        return 1.0 + 0.01 * jax.random.normal(k, shape, f32)

    G, P, Hc = SSM_GROUPS, SSM_STATE, SSM_GROUP
    lam_im_base = jnp.pi * jnp.arange(P, dtype=f32)
    return {
        'x_prompt': nrm(ks[0], (BATCH, SEQ, D_MODEL), 1.0),
        'x_sample': nrm(ks[1], (DEC_BATCH, DEC_SEQ, D_MODEL), 1.0),
        'state_ssm_re': nrm(ks[2], (DEPTH, DEC_BATCH, G, P), 0.1),
        'state_ssm_im': nrm(ks[3], (DEPTH, DEC_BATCH, G, P), 0.1),
        'cache_conv': nrm(ks[4], (DEPTH, DEC_BATCH, CONV_WIDTH - 1, D_CONV), 0.5),
        'ffn1_norm': gain(ks[5], (DEPTH, D_MODEL)),
        'ffn1_w1': nrm(ks[6], (DEPTH, D_MODEL, D_FF), D_MODEL ** -0.5),
        'ffn1_w3': nrm(ks[7], (DEPTH, D_MODEL, D_FF), D_MODEL ** -0.5),
        'ffn1_w2': nrm(ks[8], (DEPTH, D_FF, D_MODEL), D_FF ** -0.5),
        'mix_norm': gain(ks[9], (DEPTH, D_MODEL)),
        'w_in': nrm(ks[10], (DEPTH, D_MODEL, N_IN), D_MODEL ** -0.5),
        'ssm_lambda_re': -0.5 + nrm(ks[11], (DEPTH, G, P), 0.01),
        'ssm_lambda_im': lam_im_base + nrm(ks[12], (DEPTH, G, P), 0.01),
        'ssm_log_step': jax.random.uniform(ks[13], (DEPTH, G), f32, math.log(DT_MIN), math.log(DT_MAX)),
        'ssm_b_re': nrm(ks[14], (DEPTH, G, P, Hc), (2 * Hc) ** -0.5),
        'ssm_b_im': nrm(ks[15], (DEPTH, G, P, Hc), (2 * Hc) ** -0.5),
        'ssm_c_re': nrm(ks[16], (DEPTH, G, Hc, P), P ** -0.5),
        'ssm_c_im': nrm(ks[17], (DEPTH, G, Hc, P), P ** -0.5),
        'ssm_d': nrm(ks[18], (DEPTH, D_SSM), 1.0),
        'ssm_glu_w': nrm(ks[19], (DEPTH, D_SSM, D_SSM), D_SSM ** -0.5),
        'ssm_glu_b': nrm(ks[20], (DEPTH, D_SSM), 0.01),
        'ssm_out_w': nrm(ks[21], (DEPTH, D_SSM, D_MODEL), D_SSM ** -0.5),
        'conv_dw_w': nrm(ks[22], (DEPTH, CONV_WIDTH, D_CONV), CONV_WIDTH ** -0.5),
        'conv_dw_b': nrm(ks[23], (DEPTH, D_CONV), 0.01),
        'conv_ln_g': gain(ks[24], (DEPTH, D_CONV)),
        'conv_ln_b': nrm(ks[25], (DEPTH, D_CONV), 0.01),
        'conv_out_w': nrm(ks[26], (DEPTH, D_CONV, D_MODEL), D_CONV ** -0.5),
        'w_o': nrm(ks[27], (DEPTH, D_MODEL, D_MODEL), D_MODEL ** -0.5),
        'ffn2_norm': gain(ks[28], (DEPTH, D_MODEL)),
        'ffn2_w1': nrm(ks[29], (DEPTH, D_MODEL, D_FF), D_MODEL ** -0.5),
        'ffn2_w3': nrm(ks[30], (DEPTH, D_MODEL, D_FF), D_MODEL ** -0.5),
        'ffn2_w2': nrm(ks[31], (DEPTH, D_FF, D_MODEL), D_FF ** -0.5),
        'final_norm': gain(ks[32], (D_MODEL,)),
    }


def reference(x_prompt, x_sample, state_ssm_re, state_ssm_im, cache_conv,
              ffn1_norm, ffn1_w1, ffn1_w3, ffn1_w2, mix_norm, w_in,
              ssm_lambda_re, ssm_lambda_im, ssm_log_step, ssm_b_re, ssm_b_im,
              ssm_c_re, ssm_c_im, ssm_d, ssm_glu_w, ssm_glu_b, ssm_out_w,
              conv_dw_w, conv_dw_b, conv_ln_g, conv_ln_b, conv_out_w, w_o,
              ffn2_norm, ffn2_w1, ffn2_w3, ffn2_w2, final_norm):
    layer_params = dict(
        ffn1_norm=ffn1_norm, ffn1_w1=ffn1_w1, ffn1_w3=ffn1_w3, ffn1_w2=ffn1_w2,
        mix_norm=mix_norm, w_in=w_in,
        ssm_lambda_re=ssm_lambda_re, ssm_lambda_im=ssm_lambda_im, ssm_log_step=ssm_log_step,
        ssm_b_re=ssm_b_re, ssm_b_im=ssm_b_im, ssm_c_re=ssm_c_re, ssm_c_im=ssm_c_im,
        ssm_d=ssm_d, ssm_glu_w=ssm_glu_w, ssm_glu_b=ssm_glu_b, ssm_out_w=ssm_out_w,
        conv_dw_w=conv_dw_w, conv_dw_b=conv_dw_b, conv_ln_g=conv_ln_g, conv_ln_b=conv_ln_b,
        conv_out_w=conv_out_w, w_o=w_o,
        ffn2_norm=ffn2_norm, ffn2_w1=ffn2_w1, ffn2_w3=ffn2_w3, ffn2_w2=ffn2_w2)
    y_prompt, new_ssm_re_prompt, new_ssm_im_prompt, new_conv_prompt = _trunk(
        x_prompt, layer_params, final_norm, None, None, None)
    y_sample, new_ssm_re_sample, new_ssm_im_sample, new_conv_sample = _trunk(
        x_sample, layer_params, final_norm, state_ssm_re, state_ssm_im, cache_conv)
    return (y_prompt, y_sample, new_ssm_re_prompt, new_ssm_im_prompt, new_conv_prompt,
            new_ssm_re_sample, new_ssm_im_sample, new_conv_sample)
```

```python
import numpy as np
from contextlib import ExitStack
import concourse.bass as bass
import concourse.mybir as mybir
from concourse.bass_utils import run_bass_kernel_spmd

F32 = mybir.dt.float32
BF16 = mybir.dt.bfloat16
I32 = mybir.dt.int32
ALU = mybir.AluOpType
AF = mybir.ActivationFunctionType

D = 1024
DFF = 2816
NIN = 3584
DS = 512
NPAIR = 16
CW = 31
HALO = 30
SEQ = 2048
BPC = 4
NCORES = 8
DSEQ = 16
EPS = 1e-6
TWO_PI = 6.283185307179586
PI = 3.141592653589793

MODE = "bf16"
ENGS = ("sync", "scalar", "vector", "gpsimd", "tensor")
import os
SAME_ENGINE_SYNC = os.environ.get("KSES", "1") == "1"
DBG = os.environ.get("KDBG", "")


class StopBuild(Exception):
    pass


class Prog:
    def __init__(self, nc):
        self.nc = nc
        self.ops = {e: [] for e in ENGS}
        self.count = {}
        self.waited = {e: {} for e in ENGS}
        self.res = {}
        self.sems = {}
        self.ninst = {e: 0 for e in ENGS}

    def _need(self, eng, toks):
        need = {}
        for t in toks:
            if t is None:
                continue
            s, v, te = t
            if te == eng and s.startswith("e_"):
                if eng == "tensor" or not SAME_ENGINE_SYNC:
                    continue
            if need.get(s, 0) < v:
                need[s] = v
        for s, v in need.items():
            if self.waited[eng].get(s, 0) >= v:
                continue
            self.waited[eng][s] = v
            self.ops[eng].append(("wait", s, v))

    def emit(self, eng, fn, reads=(), writes=(), sem=None, inc=1):
        toks = []
        for r in reads:
            st = self.res.get(r)
            if st is not None:
                toks.append(st[0])
        for w in writes:
            st = self.res.get(w)
            if st is not None:
                toks.append(st[0])
                toks.extend(st[1])
        self._need(eng, toks)
        if sem is None:
            sem = "e_" + eng
        v = self.count.get(sem, 0) + inc
        self.count[sem] = v
        tok = (sem, v, eng)
        self.ops[eng].append(("op", fn, sem, inc))
        self.ninst[eng] += 1
        for r in reads:
            st = self.res.setdefault(r, [None, []])
            st[1].append(tok)
        for w in writes:
            self.res[w] = [tok, []]
        return tok

    def dma(self, eng, out, in_, reads=(), writes=(), sem=None, **kw):
        prev = self.count.get("d_" + sem, 0)
        if prev and self.waited[eng].get("d_" + sem, 0) < prev:
            self.waited[eng]["d_" + sem] = prev
            self.ops[eng].append(("wait", "d_" + sem, prev))
        return self.emit(eng, lambda e: e.dma_start(out=out, in_=in_, **kw),
                         reads=reads, writes=writes, sem="d_" + sem, inc=16)

    def wait_all(self, eng):
        for s, v in self.count.items():
            if self.waited[eng].get(s, 0) < v:
                self.waited[eng][s] = v
                self.ops[eng].append(("wait", s, v))

    def build(self):
        nc = self.nc
        with ExitStack() as es:
            for s in self.count:
                self.sems[s] = es.enter_context(nc.semaphore(s))
            block = es.enter_context(nc.Block())
            prog = self

            def runner(name):
                def run(e):
                    for op in prog.ops[name]:
                        if op[0] == "wait":
                            e.wait_ge(prog.sems[op[1]], op[2])
                        else:
                            _, fn, s, inc = op
                            fn(e).then_inc(prog.sems[s], inc)
                return run

            block.sync(runner("sync"))
            block.scalar(runner("scalar"))
            block.vector(runner("vector"))
            block.gpsimd(runner("gpsimd"))
            block.tensor(runner("tensor"))


WEIGHTS = [
    ("ffn1_w1", D, DFF), ("ffn1_w3", D, DFF), ("ffn1_w2", DFF, D),
    ("w_in", D, NIN), ("ssm_glu_w", DS, DS), ("ssm_out_w", DS, D), ("conv_out_w", DS, D),
    ("w_o", D, D),
    ("ffn2_w1", D, DFF), ("ffn2_w3", D, DFF), ("ffn2_w2", DFF, D),
]
WSLAB_NC = {"ffn1_w2": 128, "ffn2_w2": 128}
SMALL = [
    ("ffn1_norm", [D]), ("mix_norm", [D]), ("ffn2_norm", [D]), ("final_norm", [D]),
    ("ssm_d", [DS]), ("ssm_glu_b", [DS]), ("conv_dw_b", [DS]), ("conv_ln_g", [DS]), ("conv_ln_b", [DS]),
    ("conv_dw_w", [CW, DS]),
    ("ssm_lambda_re", [32, 64]), ("ssm_lambda_im", [32, 64]), ("ssm_log_step", [32]),
    ("ssm_b_re", [32, 64, 16]), ("ssm_b_im", [32, 64, 16]),
    ("ssm_c_re", [32, 16, 64]), ("ssm_c_im", [32, 16, 64]),
]


def build_program(n_prompt_seq=BPC, seq_len=SEQ, do_sample=True, mode=MODE, stop_after=None):
    bf = mode == "bf16"
    assert bf
    MDT = BF16
    T = 256
    SBLK = 128
    NSLOT = 4
    SLOT_EL = 2048
    NSTREAM = 2
    nc = bass.Bass("TRN2", target_bir_lowering=False)
    P = Prog(nc)
    dr = {}

    def din(name, shape, dt=F32):
        dr[name] = nc.dram_tensor(name, list(shape), dt, kind="ExternalInput").ap()
        return dr[name]

    def dout(name, shape, dt=F32):
        dr[name] = nc.dram_tensor(name, list(shape), dt, kind="ExternalOutput").ap()
        return dr[name]

    din("xp", [BPC, SEQ, D]); din("xs", [DSEQ, D])
    din("sre", [32, 64]); din("sim", [32, 64]); din("cc", [HALO, DS])
    for nm, K, N in WEIGHTS:
        din(nm, [K, N])
    for nm, shp in SMALL:
        din(nm, shp)
    dout("yp", [BPC, SEQ, D]); dout("ys", [DSEQ, D])
    dout("nrp", [BPC, 32, 64]); dout("nip", [BPC, 32, 64]); dout("ncp", [BPC, HALO, DS])
    dout("nrs", [32, 64]); dout("nis", [32, 64]); dout("ncs", [HALO, DS])

    winfo = {}
    for nm, K, N in WEIGHTS:
        if nm.endswith("_w2"):
            kc, ncol, ns = 11, 128, 16
        else:
            ncol = 256
            kc = K // 128
            ns = N // ncol
        winfo[nm] = (kc, ncol, ns)
        dr["scr_" + nm] = nc.dram_tensor("scr_" + nm, [ns, 128, kc * ncol], BF16, kind="Internal").ap()

    es = ExitStack()
    with es:
        def sb(name, shape, dt=F32):
            return es.enter_context(nc.sbuf_tensor(name, list(shape), dt))

        def psb(name):
            return es.enter_context(nc.psum_tensor(name, [128, 512], F32))

        ps = [psb(f"ps{i}") for i in range(8)]
        ident = sb("ident", [128, 128])
        ones = sb("ones", [128, 128], MDT)
        wslab = sb("wslab", [128, NSTREAM * NSLOT * SLOT_EL], MDT)
        wsl32 = wslab[:].bitcast(F32)
        Ec = sb("Ec", [128, NPAIR, SBLK]); Es = sb("Es", [128, NPAIR, SBLK])
        WBr = sb("WBr", [128, NPAIR, 128], MDT); WBi = sb("WBi", [128, NPAIR, 128], MDT)
        WCr = sb("WCr", [128, NPAIR, 128], MDT); WCi = sb("WCi", [128, NPAIR, 128], MDT)
        amag = sb("amag", [128, NPAIR])
        nEsL = sb("nEsL", [128, NPAIR, 2])
        colsA = sb("colsA", [128, 8, 4])
        colsB = sb("colsB", [128, 4, 36])
        ncst = sb("ncst", [HALO, DS])
        sst = sb("sst", [16, 256])
        xin = sb("xinr", [128, 2048])
        identb = sb("identb", [128, 128], MDT)
        Dg = sb("Dg", [128, 4, CW, 128], MDT)
        ti = sb("ti", [128, 16], I32)

        def V(fn, r, w): return P.emit("vector", fn, reads=r, writes=w)
        def A(fn, r, w): return P.emit("scalar", fn, reads=r, writes=w)
        def G(fn, r, w): return P.emit("gpsimd", fn, reads=r, writes=w)
        def PE(fn, r, w): return P.emit("tensor", fn, reads=r, writes=w)

        def mm(out, lhsT, rhs, start, stop, r, w):
            return PE(lambda e: e.matmul(out, lhsT=lhsT, rhs=rhs, start=start, stop=stop), r, w)

        def tr(out, in_, n, r, w):
            return PE(lambda e: e.transpose(out, in_, ident[:n, :n]), r, w)

        xin_real = xin
        xin = wsl32[:, 0:3584]
        gb32 = wsl32[:, 3584:7168]
        gres = lambda i: f"g{i}"
        stg = gb32[0:36, 2560:3584]

        G(lambda e: e.memset(ident[:], 0.0), [], ["ident"])
        G(lambda e: e.affine_select(out=ident[:], in_=ident[:], pattern=[[-1, 128]], compare_op=ALU.not_equal,
                                    fill=1.0, base=0, channel_multiplier=1), ["ident"], ["ident"])
        V(lambda e: e.memset(ones[:], 1.0), [], ["ones"])

        def wsrc(nm, s):
            kc, ncol, ns = winfo[nm]
            if nm.endswith("_w2"):
                n_, kh = s // 2, s % 2
                return dr[nm][kh * 1408:(kh + 1) * 1408, n_ * 128:(n_ + 1) * 128].rearrange("(kc p) n -> p kc n", p=128)
            return dr[nm][:, s * ncol:(s + 1) * ncol].rearrange("(kc p) n -> p kc n", p=128)

        if bf:
            for nm, K, N in WEIGHTS:
                kc, ncol, ns = winfo[nm]
                for s in range(ns):
                    P.dma("gpsimd", dr["scr_" + nm][s].rearrange("p (kc n) -> p kc n", kc=kc), wsrc(nm, s),
                          writes=[f"scr_{nm}_{s}"], sem=f"cast{s % 4}")

        STG = [f"g{i}" for i in range(22)]
        for i, nm in enumerate(["ffn1_norm", "mix_norm", "ffn2_norm", "final_norm"]):
            P.dma("sync", stg[i:i + 1, :], dr[nm].rearrange("(o d) -> o d", o=1), writes=STG, sem="su")
        for c in range(8):
            tr(ps[7][:, c * 4:(c + 1) * 4], stg[0:4, c * 128:(c + 1) * 128], 4, STG + ["ident"], ["ps7"])
        V(lambda e: e.tensor_copy(colsA[:].rearrange("p c k -> p (c k)"), ps[7][:, 0:32]), ["ps7"], ["colsA"])
        for i, nm in enumerate(["ssm_d", "ssm_glu_b", "conv_dw_b", "conv_ln_g", "conv_ln_b"]):
            P.dma("sync", stg[i:i + 1, 0:DS], dr[nm].rearrange("(o d) -> o d", o=1), reads=[], writes=STG, sem="su")
        P.dma("sync", stg[5:36, 0:DS], dr["conv_dw_w"], writes=STG, sem="su")
        for c in range(4):
            tr(ps[7][:, c * 36:(c + 1) * 36], stg[0:36, c * 128:(c + 1) * 128], 36, STG + ["ident"], ["ps7"])
        V(lambda e: e.tensor_copy(colsB[:].rearrange("p c k -> p (c k)"), ps[7][:, 0:144]), ["ps7"], ["colsB"])
        gcol = {"ffn1": lambda c: colsA[:, c, 0:1], "mix": lambda c: colsA[:, c, 1:2],
                "ffn2": lambda c: colsA[:, c, 2:3], "final": lambda c: colsA[:, c, 3:4]}
        dcol = lambda c: colsB[:, c, 0:1]
        glubcol = lambda c: colsB[:, c, 1:2]
        dwbcol = lambda c: colsB[:, c, 2:3]
        lngcol = lambda c: colsB[:, c, 3:4]
        lnbcol = lambda c: colsB[:, c, 4:5]
        tapcol = lambda c, j: colsB[:, c, 5 + j:6 + j]

        xs32 = xin
        LR = xs32[:, 0:16]; LI = xs32[:, 16:32]; DT = xs32[:, 32:48]; TH = xs32[:, 48:64]
        AR = xs32[:, 64:80]; AI = xs32[:, 80:96]; KR = xs32[:, 96:112]; KI = xs32[:, 112:128]
        t0 = xs32[:, 128:144]; t1 = xs32[:, 144:160]; t2 = xs32[:, 160:176]; t3 = xs32[:, 176:192]
        SN = xs32[:, 192:208]; CS = xs32[:, 208:224]
        XR = "xin"

        def loadT16(dst, src_ap):
            P.dma("sync", sst[0:16, 0:128], src_ap, writes=["sst"], sem="su")
            tr(ps[7][:, 0:16], sst[0:16, 0:128], 16, ["sst", "ident"], ["ps7"])
            V(lambda e: e.tensor_copy(dst, ps[7][:, 0:16]), ["ps7"], [XR])

        loadT16(LR, dr["ssm_lambda_re"].rearrange("(j gl) p -> j (gl p)", gl=2))
        loadT16(LI, dr["ssm_lambda_im"].rearrange("(j gl) p -> j (gl p)", gl=2))
        lsh = dr["ssm_log_step"].tensor
        for gl in range(2):
            P.dma("sync", xs32[gl * 64:(gl + 1) * 64, 32:48], bass.AP(tensor=lsh, offset=gl, ap=[[0, 64], [2, 16]]),
                  writes=[XR], sem="su", allow_slow_non_contiguous=True)
        A(lambda e: e.activation(DT, DT, AF.Exp), [XR], [XR])
        V(lambda e: e.tensor_tensor(t0, LR, DT, ALU.mult), [XR], [XR])
        A(lambda e: e.activation(amag[:], t0, AF.Exp), [XR], ["amag"])
        V(lambda e: e.tensor_tensor(TH, LI, DT, ALU.mult), [XR], [XR])

        V(lambda e: e.tensor_scalar(t1, TH, 1.0 / TWO_PI, None, ALU.mult), [XR], [XR])
        V(lambda e: e.tensor_copy(ti[:], t1), [XR], ["ti"])
        V(lambda e: e.tensor_copy(t2, ti[:]), ["ti"], [XR])
        V(lambda e: e.scalar_tensor_tensor(t1, t2, -TWO_PI, TH, ALU.mult, ALU.add), [XR], [XR])
        V(lambda e: e.tensor_scalar(t2, t1, PI, TWO_PI, ALU.is_gt, ALU.mult), [XR], [XR])
        V(lambda e: e.tensor_tensor(t1, t1, t2, ALU.subtract), [XR], [XR])
        V(lambda e: e.tensor_scalar(t2, t1, -PI, TWO_PI, ALU.is_lt, ALU.mult), [XR], [XR])
        V(lambda e: e.tensor_tensor(t1, t1, t2, ALU.add), [XR], [XR])
        V(lambda e: e.tensor_scalar(t1, t1, 0.25, None, ALU.mult), [XR], [XR])
        V(lambda e: e.tensor_scalar(t2, t1, PI / 2, None, ALU.add), [XR], [XR])
        A(lambda e: e.activation(SN, t1, AF.Sin), [XR], [XR])
        A(lambda e: e.activation(CS, t2, AF.Sin), [XR], [XR])
        for _ in range(2):
            V(lambda e: e.tensor_tensor(t1, CS, CS, ALU.mult), [XR], [XR])
            V(lambda e: e.tensor_tensor(t2, SN, SN, ALU.mult), [XR], [XR])
            V(lambda e: e.tensor_tensor(t3, CS, SN, ALU.mult), [XR], [XR])
            V(lambda e: e.tensor_tensor(CS, t1, t2, ALU.subtract), [XR], [XR])
            V(lambda e: e.tensor_scalar(SN, t3, 2.0, None, ALU.mult), [XR], [XR])
        V(lambda e: e.tensor_tensor(AR, amag[:], CS, ALU.mult), [XR, "amag"], [XR])
        V(lambda e: e.tensor_tensor(AI, amag[:], SN, ALU.mult), [XR, "amag"], [XR])
        V(lambda e: e.tensor_scalar(t0, AR, -1.0, None, ALU.add), [XR], [XR])
        V(lambda e: e.tensor_tensor(t1, LR, LR, ALU.mult), [XR], [XR])
        V(lambda e: e.tensor_tensor(t2, LI, LI, ALU.mult), [XR], [XR])
        V(lambda e: e.tensor_tensor(t1, t1, t2, ALU.add), [XR], [XR])
        V(lambda e: e.reciprocal(t1, t1), [XR], [XR])
        V(lambda e: e.tensor_tensor(t2, t0, LR, ALU.mult), [XR], [XR])
        V(lambda e: e.tensor_tensor(t3, AI, LI, ALU.mult), [XR], [XR])
        V(lambda e: e.tensor_tensor(t2, t2, t3, ALU.add), [XR], [XR])
        V(lambda e: e.tensor_tensor(KR, t2, t1, ALU.mult), [XR], [XR])
        V(lambda e: e.tensor_tensor(t2, AI, LR, ALU.mult), [XR], [XR])
        V(lambda e: e.tensor_tensor(t3, t0, LI, ALU.mult), [XR], [XR])
        V(lambda e: e.tensor_tensor(t2, t2, t3, ALU.subtract), [XR], [XR])
        V(lambda e: e.tensor_tensor(KI, t2, t1, ALU.mult), [XR], [XR])
        V(lambda e: e.tensor_copy(Ec[:, :, 0:1], CS.unsqueeze(2)), [XR], ["E"])
        V(lambda e: e.tensor_copy(Es[:, :, 0:1], SN.unsqueeze(2)), [XR], ["E"])
        TA = xs32[:, 1024:1024 + 16 * 64].rearrange("p (j m) -> p j m", j=16)
        TB = xs32[:, 2048:2048 + 16 * 64].rearrange("p (j m) -> p j m", j=16)
        n = 1
        while n < SBLK:
            cn = Ec[:, :, n - 1:n].to_broadcast([128, NPAIR, n])
            sn = Es[:, :, n - 1:n].to_broadcast([128, NPAIR, n])
            c_lo = Ec[:, :, 0:n]; s_lo = Es[:, :, 0:n]
            c_hi = Ec[:, :, n:2 * n]; s_hi = Es[:, :, n:2 * n]
            ta = TA[:, :, 0:n]; tb = TB[:, :, 0:n]
            V(lambda e, a=ta, x=c_lo, y=cn: e.tensor_tensor(a, x, y, ALU.mult), ["E"], [XR])
            V(lambda e, a=tb, x=s_lo, y=sn: e.tensor_tensor(a, x, y, ALU.mult), ["E"], [XR])
            V(lambda e, o=c_hi, a=ta, b=tb: e.tensor_tensor(o, a, b, ALU.subtract), [XR, "E"], ["E"])
            V(lambda e, a=ta, x=s_lo, y=cn: e.tensor_tensor(a, x, y, ALU.mult), ["E"], [XR])
            V(lambda e, a=tb, x=c_lo, y=sn: e.tensor_tensor(a, x, y, ALU.mult), ["E"], [XR])
            V(lambda e, o=s_hi, a=ta, b=tb: e.tensor_tensor(o, a, b, ALU.add), [XR, "E"], ["E"])
            n *= 2
        BR = xs32[:, 256:512].rearrange("p (j c) -> p j c", j=16)
        BI = xs32[:, 512:768].rearrange("p (j c) -> p j c", j=16)
        BBR = xs32[:, 768:1024].rearrange("p (j c) -> p j c", j=16)
        BBI = xs32[:, 3072:3328].rearrange("p (j c) -> p j c", j=16)
        P.dma("sync", BR, dr["ssm_b_re"].rearrange("(j gl) p c -> (gl p) j c", gl=2), writes=[XR], sem="su")
        P.dma("sync", BI, dr["ssm_b_im"].rearrange("(j gl) p c -> (gl p) j c", gl=2), writes=[XR], sem="su")
        krb = KR.unsqueeze(2).to_broadcast([128, 16, 16]); kib = KI.unsqueeze(2).to_broadcast([128, 16, 16])
        ta = TA[:, :, 0:16]; tb = TB[:, :, 0:16]
        V(lambda e: e.tensor_tensor(ta, BR, krb, ALU.mult), [XR], [XR])
        V(lambda e: e.tensor_tensor(tb, BI, kib, ALU.mult), [XR], [XR])
        V(lambda e: e.tensor_tensor(BBR, ta, tb, ALU.subtract), [XR], [XR])
        V(lambda e: e.tensor_tensor(ta, BI, krb, ALU.mult), [XR], [XR])
        V(lambda e: e.tensor_tensor(tb, BR, kib, ALU.mult), [XR], [XR])
        V(lambda e: e.tensor_tensor(BBI, ta, tb, ALU.add), [XR], [XR])
        Z = gb32[:, 0:2048].rearrange("p (j r) -> p j r", j=16)
        ZR = [gres(i) for i in range(22)]
        for (BB, WB, nm) in ((BBR, WBr, "WBr"), (BBI, WBi, "WBi")):
            V(lambda e: e.memset(Z, 0.0), [], ZR)
            for jj in range(4):
                for gl in range(2):
                    lo = gl * 64
                    V(lambda e, jj=jj, gl=gl, lo=lo, BB=BB: e.tensor_copy(
                        Z[lo:lo + 64, jj::4, 32 * jj + 16 * gl:32 * jj + 16 * gl + 16], BB[lo:lo + 64, jj::4, :]),
                      [XR], ZR)
            for j in range(16):
                pb = ps[j % 2]
                tr(pb[:, 0:128], Z[:, j, :], 128, ZR + ["ident"], [f"ps{j % 2}"])
                V(lambda e, j=j, pb=pb, WB=WB: e.tensor_copy(WB[:, j, :], pb[:, 0:128]), [f"ps{j % 2}"], [nm])
        CX = gb32[:, 2048:2048 + 512].rearrange("p (q m) -> p q m", q=4)
        for (cname, WC, nm, sgn) in (("ssm_c_re", WCr, "WCr", 1.0), ("ssm_c_im", WCi, "WCi", -1.0)):
            csrc = dr[cname].rearrange("(q g) c p -> (g c) q p", q=4)
            P.dma("sync", CX[:, :, 0:64], csrc, writes=ZR, sem="su")
            P.dma("sync", CX[:, :, 64:128], csrc, writes=[gres(21)], sem="su")
            V(lambda e, WC=WC: e.memset(WC[:], 0.0), [], [nm])
            for q in range(4):
                pb = ps[q % 2]
                tr(pb[:, 0:128], CX[:, q, :], 128, ZR + ["ident"], [f"ps{q % 2}"])
                for jj in range(4):
                    j = 4 * q + jj
                    for gl in range(2):
                        lo = gl * 64
                        c0 = 32 * jj + 16 * gl
                        V(lambda e, WC=WC, j=j, lo=lo, c0=c0, pb=pb, sgn=sgn: e.tensor_scalar(
                            WC[lo:lo + 64, j, c0:c0 + 16], pb[lo:lo + 64, c0:c0 + 16], sgn, None, ALU.mult),
                          [f"ps{q % 2}"], [nm])

        V(lambda e: e.tensor_copy(identb[:], ident[:]), ["ident"], ["identb"])
        for c in range(4):
            for jt in range(CW):
                V(lambda e, c=c, jt=jt: e.tensor_scalar(Dg[:, c, jt, :], identb[:], colsB[:, c, 5 + jt:6 + jt], None, ALU.mult),
                  ["identb", "colsB"], ["Dg"])
        V(lambda e: e.tensor_scalar(nEsL[:, :, 0], Es[:, :, SBLK - 1], -1.0, None, ALU.mult), ["E"], ["E2"])
        V(lambda e: e.tensor_scalar(nEsL[:, :, 1], Es[:, :, DSEQ - 1], -1.0, None, ALU.mult), ["E"], ["E2"])
        slot_names = [f"{sid}:wsl{k}" for sid in range(NSTREAM) for k in range(NSLOT)]
        V(lambda e: e.memset(ti[:, 0:1], 0), [], ["xin", "ti"] + [f"g{i}" for i in range(22)] + slot_names)
        xin = xin_real

        gcol = {"ffn1": lambda c: colsA[:, c, 0:1], "mix": lambda c: colsA[:, c, 1:2],
                "ffn2": lambda c: colsA[:, c, 2:3], "final": lambda c: colsA[:, c, 3:4]}

        def tile_slabs():
            L = []
            def ffn(f):
                r = []
                for s in range(11):
                    r.append((f + "_w1", s)); r.append((f + "_w3", s))
                for s in range(16):
                    r.append((f + "_w2", s))
                return r
            L += ffn("ffn1")
            for s in (0, 1, 4, 5, 2, 3, 6, 7, 8, 9, 10, 11, 12, 13):
                L.append(("w_in", s))
            L += [("ssm_glu_w", 0), ("ssm_glu_w", 1)]
            for s in range(4):
                L.append(("ssm_out_w", s)); L.append(("conv_out_w", s))
            for s in range(4):
                L.append(("w_o", s))
            L += ffn("ffn2")
            return L

        class Stream:
            pass

        SHARED_PREFIX = ("u32_", "um", "sg", "tt", "m0_", "m1_", "r0_", "r1_", "s0_", "s1_", "mean", "msq", "yssm")
        sh = Stream()
        sh.tt = [sb(f"tt{i}", [128, T]) for i in range(2)]
        sh.mrmi = [[sb(f"m{b}_{i}", [128, T]) for i in range(2)] for b in range(2)]
        sh.rri = [[sb(f"r{b}_{i}", [128, T]) for i in range(2)] for b in range(2)]
        sh.srsi = [[sb(f"s{b}_{i}", [128, T], MDT) for i in range(2)] for b in range(2)]
        sh.u32 = sb("u32", [128, 4, T]); sh.um = sb("um", [128, 4, T], MDT)
        sh.sg = sb("sg", [128, 4, T])
        sh.mean = sb("mean", [128, T]); sh.msq = sb("msq", [128, T])
        sh.yssm = [sb(f"yssm{i}", [128, T]) for i in range(2)]
        sh.pt = [sb(f"ptmp{i}", [128, T]) for i in range(2)]

        def make_stream(sid):
            S = Stream()
            S.sid = sid
            n = lambda x: x if x.startswith(SHARED_PREFIX) else f"{sid}:{x}"
            S.n = n
            pre = f"s{sid}_"
            S.xT = sb(pre + "xT", [128, 8, T])
            S.hbuf = sb(pre + "hbuf", [128, 8, T], MDT)
            S.gbuf = sb(pre + "gbuf", [128, 22 * T], MDT)
            S.gb32 = S.gbuf[:].bitcast(F32)
            S.Sre = sb(pre + "Sre", [128, NPAIR]); S.Sim = sb(pre + "Sim", [128, NPAIR])
            S.tt = sh.tt; S.mrmi = sh.mrmi; S.rri = sh.rri; S.srsi = sh.srsi
            S.s5c = sb(pre + "s5c", [128, 4])
            S.u32 = sh.u32; S.um = sh.um
            S.vp = sb(pre + "vp", [128, 4, HALO + T])
            S.vpb = sb(pre + "vpb", [128, 4, HALO + T], MDT)
            S.sg = sh.sg
            S.rstd = sb(pre + "rstd", [128, T]); S.mean = sh.mean; S.msq = sh.msq
            S.sq = [sb(pre + f"sq{i}", [128, T], MDT) for i in range(2)]
            S.ftmp = [sb(pre + f"ftmp{i}", [128, T]) for i in range(2)]
            S.yssm = sh.yssm
            S.wsl = [wslab[:, (sid * NSLOT + k) * SLOT_EL:(sid * NSLOT + k + 1) * SLOT_EL] for k in range(NSLOT)]
            S.pb = ps[4 * sid:4 * sid + 4]
            S.pbn = [f"ps{4 * sid + k}" for k in range(4)]
            S.rot = {"A": 0, "B": 0, "Y": 0, "sq": 0, "ft": 0}
            S.slab_seq = []
            S.state = {"next_load": 0, "cons": 0}
            return S

        def bankk(S, k, Tt=T):
            return S.pb[k][:, 0:Tt], S.pbn[k]

        def hb(S, kind, Tt=T):
            if kind == "A":
                k = 0 + 2 * S.rot["A"]
                S.rot["A"] ^= 1
            else:
                k = 1 + 2 * S.rot["B"]
                S.rot["B"] ^= 1
            return bankk(S, k, Tt)

        def issue_loads(S, upto):
            st = S.state
            while st["next_load"] <= min(upto, len(S.slab_seq) - 1):
                i = st["next_load"]
                nm, s = S.slab_seq[i]
                kc, ncol, ns = winfo[nm]
                slot = i % NSLOT
                dst = S.wsl[slot][:, 0:kc * ncol].rearrange("p (kc n) -> p kc n", kc=kc)
                src = dr["scr_" + nm][s].rearrange("p (kc n) -> p kc n", kc=kc)
                P.dma("sync", dst, src, reads=[f"scr_{nm}_{s}"], writes=[S.n(f"wsl{slot}")], sem=f"w{S.sid}_{slot}")
                st["next_load"] += 1

        def next_slab(S, nm_expect, s_expect):
            st = S.state
            i = st["cons"]
            assert S.slab_seq[i] == (nm_expect, s_expect), (S.slab_seq[i], nm_expect, s_expect)
            issue_loads(S, i + NSLOT - 2)
            st["cons"] += 1
            kc, ncol, ns = winfo[nm_expect]
            slot = i % NSLOT
            return S.wsl[slot][:, 0:kc * ncol].rearrange("p (kc n) -> p kc n", kc=kc), S.n(f"wsl{slot}")

        MMC = 0.14

        def rmsnorm_stats(S, Tt, nchunks, src, src_res, scale):
            n = S.n
            pst, rst = bankk(S, 3, Tt)
            for c in range(nchunks):
                k = S.rot["sq"]; S.rot["sq"] ^= 1
                A(lambda e, c=c, k=k: e.activation(S.sq[k][:, :Tt], src(c), AF.Square), [src_res(c)], [n(f"sq{k}")])
                mm(pst, ones[:], S.sq[k][:, :Tt], c == 0, c == nchunks - 1, ["ones", n(f"sq{k}")], [rst])
            V(lambda e: e.tensor_scalar(S.rstd[:, :Tt], pst, scale, EPS, ALU.mult, ALU.add), [rst], [n("rstd")])
            A(lambda e: e.activation(S.rstd[:, :Tt], S.rstd[:, :Tt], AF.Ln), [n("rstd")], [n("rstd")])
            A(lambda e: e.activation(S.rstd[:, :Tt], S.rstd[:, :Tt], AF.Exp, scale=-0.5), [n("rstd")], [n("rstd")])

        def norm_to_h(S, Tt, which):
            n = S.n
            rmsnorm_stats(S, Tt, 8, lambda c: S.xT[:, c, :Tt], lambda c: n(f"xT{c}"), 1.0 / D)
            for c in range(8):
                V(lambda e, c=c: e.scalar_tensor_tensor(S.hbuf[:, c, :Tt], S.xT[:, c, :Tt], gcol[which](c), S.rstd[:, :Tt],
                                                        ALU.mult, ALU.mult), [n(f"xT{c}"), "colsA", n("rstd")], [n(f"h{c}")])

        def gch(S, i, Tt):
            return S.gbuf[:, i * T:i * T + Tt]

        def ffn(S, Tt, f):
            n = S.n
            norm_to_h(S, Tt, f)
            yield 6.0
            hres = [n(f"h{c}") for c in range(8)]
            for s in range(11):
                w1, r1 = next_slab(S, f + "_w1", s)
                w3, r3 = next_slab(S, f + "_w3", s)
                for i2 in range(2):
                    i = 2 * s + i2
                    pa, ra = hb(S, "A", Tt); pb_, rb = hb(S, "B", Tt)
                    for k in range(8):
                        mm(pa, w1[:, k, i2 * 128:(i2 + 1) * 128], S.hbuf[:, k, :Tt], k == 0, k == 7, [r1] + hres, [ra])
                    for k in range(8):
                        mm(pb_, w3[:, k, i2 * 128:(i2 + 1) * 128], S.hbuf[:, k, :Tt], k == 0, k == 7, [r3] + hres, [rb])
                    kt = S.rot["ft"]; S.rot["ft"] ^= 1
                    A(lambda e, pa=pa, kt=kt: e.activation(S.ftmp[kt][:, :Tt], pa, AF.Silu), [ra], [n(f"ftmp{kt}")])
                    V(lambda e, pb_=pb_, kt=kt, i=i: e.tensor_tensor(gch(S, i, Tt), S.ftmp[kt][:, :Tt], pb_, ALU.mult),
                      [n(f"ftmp{kt}"), rb], [n(f"g{i}")])
                    yield 16 * MMC
            gall = [n(f"g{i}") for i in range(22)]
            for n_ in range(8):
                py, ry = hb(S, "Y", Tt)
                for kh in range(2):
                    w2, r2 = next_slab(S, f + "_w2", 2 * n_ + kh)
                    for k in range(11):
                        mm(py, w2[:, k, :], gch(S, kh * 11 + k, Tt), kh == 0 and k == 0, kh == 1 and k == 10, [r2] + gall, [ry])
                V(lambda e, py=py, n_=n_: e.scalar_tensor_tensor(S.xT[:, n_, :Tt], py, 0.5, S.xT[:, n_, :Tt],
                                                                 ALU.mult, ALU.add), [ry, n(f"xT{n_}")], [n(f"xT{n_}")])
                yield 22 * MMC

        def mixer(S, Tt):
            n = S.n
            norm_to_h(S, Tt, "mix")
            yield 6.0
            yield "lock"
            hres = [n(f"h{c}") for c in range(8)]
            gs = lambda n_: gch(S, n_, Tt)
            gc = lambda n_: gch(S, 8 + n_, Tt)
            cact = lambda c: gch(S, 16 + c, Tt)
            zz = lambda c: S.hbuf[:, c, :Tt]
            z2 = lambda c: S.hbuf[:, 4 + c, :Tt]
            u32, um, vp, sg = S.u32, S.um, S.vp, S.sg

            def proj_chunks(s, kind):
                w, rw = next_slab(S, "w_in", s)
                outs = []
                for i2 in range(2):
                    pa, ra = hb(S, kind, Tt)
                    for k in range(8):
                        mm(pa, w[:, k, i2 * 128:(i2 + 1) * 128], S.hbuf[:, k, :Tt], k == 0, k == 7, [rw] + hres, [ra])
                    outs.append((pa, ra))
                return outs

            for s in (0, 1):
                for i2, (pa, ra) in enumerate(proj_chunks(s, "A")):
                    c = 2 * s + i2
                    A(lambda e, pa=pa, c=c: e.copy(u32[:, c, :Tt], pa), [ra], [n(f"u32_{c}")])
                    G(lambda e, c=c: e.tensor_copy(um[:, c, :Tt], u32[:, c, :Tt]), [n(f"u32_{c}")], [n(f"um{c}")])
                yield 16 * MMC
            for s in (4, 5):
                for i2, (pa, ra) in enumerate(proj_chunks(s, "A")):
                    c = 2 * (s - 4) + i2
                    A(lambda e, pa=pa, c=c: e.activation(sg[:, c, :Tt], pa, AF.Sigmoid), [ra], [n(f"sg{c}")])
                yield 16 * MMC
            for s in (2, 3):
                for i2, (pa, ra) in enumerate(proj_chunks(s, "B")):
                    c = 2 * (s - 2) + i2
                    V(lambda e, pa=pa, c=c: e.tensor_tensor(vp[:, c, HALO:HALO + Tt], pa, sg[:, c, :Tt], ALU.mult),
                      [ra, n(f"sg{c}")], [n(f"vp{c}")])
                yield 16 * MMC
            for s in range(6, 14):
                for i2, (pa, ra) in enumerate(proj_chunks(s, "A")):
                    n_ = 2 * (s - 6) + i2
                    A(lambda e, pa=pa, n_=n_: e.activation(gch(S, n_, Tt), pa, AF.Sigmoid), [ra], [n(f"g{n_}")])
                yield 16 * MMC

            nblk = max(1, Tt // SBLK)
            L = min(SBLK, Tt)
            Sre, Sim, s5c = S.Sre, S.Sim, S.s5c
            tt1, tt2 = S.tt
            rtt = [n("tt0"), n("tt1")]
            pbr, rbr = bankk(S, 0, Tt)
            pbi, rbi = bankk(S, 2, Tt)

            def emit_bu(j):
                q = j // 4
                mm(pbr, WBr[:, j, :], um[:, q, :Tt], True, True, ["WBr", n(f"um{q}")], [rbr])
                mm(pbi, WBi[:, j, :], um[:, q, :Tt], True, True, ["WBi", n(f"um{q}")], [rbi])

            def emit_c(j):
                q, jj = j // 4, j % 4
                par = j % 2
                sr, si = S.srsi[par]
                py, ry = bankk(S, 1 + 2 * (q % 2), Tt)
                mm(py, WCr[:, j, :], sr[:, :Tt], jj == 0, False, ["WCr", n(f"s{par}_0")], [ry])
                mm(py, WCi[:, j, :], si[:, :Tt], False, jj == 3, ["WCi", n(f"s{par}_1")], [ry])
                if jj == 3:
                    k = q % 2
                    V(lambda e, q=q, py=py, k=k: e.scalar_tensor_tensor(S.yssm[k][:, :Tt], u32[:, q, :Tt], colsB[:, q, 0:1], py,
                                                                        ALU.mult, ALU.add), [n(f"u32_{q}"), "colsB", ry], [n(f"yssm{k}")])
                    A(lambda e, q=q, k=k: e.activation(zz(q), S.yssm[k][:, :Tt], AF.Gelu_apprx_tanh), [n(f"yssm{k}")], [n(f"h{q}")])

            v3 = lambda ap: ap[:, :Tt].rearrange("p (b l) -> p b l", l=L)

            def pair_ctx(j):
                par = j % 2
                c = {"j": j, "par": par}
                c["mr"], c["mi"] = S.mrmi[par]
                c["rr"], c["ri"] = S.rri[par]
                c["sr"], c["si"] = S.srsi[par]
                c["rm"] = [n(f"m{par}_0"), n(f"m{par}_1")]
                c["rrs"] = [n(f"r{par}_0"), n(f"r{par}_1")]
                c["rss"] = [n(f"s{par}_0"), n(f"s{par}_1")]
                c["ecb"] = Ec[:, j:j + 1, 0:L].to_broadcast([128, nblk, L])
                c["esb"] = Es[:, j:j + 1, 0:L].to_broadcast([128, nblk, L])
                c["ab"] = amag[:, j:j + 1].to_broadcast([128, L])
                c["ecl"] = Ec[:, j, L - 1:L]; c["esl"] = Es[:, j, L - 1:L]
                return c

            def mod_ops(c):
                rr, ri, mr, mi, ecb, esb, rrs, rm = c["rr"], c["ri"], c["mr"], c["mi"], c["ecb"], c["esb"], c["rrs"], c["rm"]
                return [
                    lambda: V(lambda e: e.tensor_tensor(v3(tt1), v3(rr), ecb, ALU.mult), [rrs[0], "E"], [rtt[0]]),
                    lambda: V(lambda e: e.tensor_tensor(v3(tt2), v3(ri), esb, ALU.mult), [rrs[1], "E"], [rtt[1]]),
                    lambda: V(lambda e: e.tensor_tensor(mr[:, :Tt], tt1[:, :Tt], tt2[:, :Tt], ALU.add), rtt, [rm[0]]),
                    lambda: V(lambda e: e.tensor_tensor(v3(tt1), v3(ri), ecb, ALU.mult), [rrs[1], "E"], [rtt[0]]),
                    lambda: V(lambda e: e.tensor_tensor(v3(tt2), v3(rr), esb, ALU.mult), [rrs[0], "E"], [rtt[1]]),
                    lambda: V(lambda e: e.tensor_tensor(mi[:, :Tt], tt1[:, :Tt], tt2[:, :Tt], ALU.subtract), rtt, [rm[1]]),
                ]

            def scan_ops(c):
                j, par = c["j"], c["par"]
                rr, ri, mr, mi, ab, ecl, esl, rrs, rm = c["rr"], c["ri"], c["mr"], c["mi"], c["ab"], c["ecl"], c["esl"], c["rrs"], c["rm"]
                sc0, sc1 = s5c[:, 2 * par:2 * par + 1], s5c[:, 2 * par + 1:2 * par + 2]
                rsc = n(f"s5c{par}")
                nesl = nEsL[:, j, (0 if L == SBLK else 1):(1 if L == SBLK else 2)]
                ops = []
                for b in range(nblk):
                    c0 = b * L
                    l1 = c0 + L - 1
                    ops += [
                        lambda c0=c0: V(lambda e: e.tensor_tensor_scan(rr[:, c0:c0 + L], ab, mr[:, c0:c0 + L], Sre[:, j:j + 1], ALU.mult, ALU.add),
                                        [rm[0], "amag", n(f"S{j}")], [rrs[0]]),
                        lambda c0=c0: V(lambda e: e.tensor_tensor_scan(ri[:, c0:c0 + L], ab, mi[:, c0:c0 + L], Sim[:, j:j + 1], ALU.mult, ALU.add),
                                        [rm[1], "amag", n(f"S{j}")], [rrs[1]]),
                        lambda l1=l1: A(lambda e: e.activation(sc0, ri[:, l1:l1 + 1], AF.Identity, scale=nesl), [rrs[1], "E2"], [rsc]),
                        lambda l1=l1: A(lambda e: e.activation(sc1, rr[:, l1:l1 + 1], AF.Identity, scale=esl), [rrs[0], "E"], [rsc]),
                        lambda l1=l1: A(lambda e: e.activation(Sre[:, j:j + 1], rr[:, l1:l1 + 1], AF.Identity, scale=ecl, bias=sc0),
                                        [rrs[0], "E", rsc], [n(f"S{j}")]),
                        lambda l1=l1: A(lambda e: e.activation(Sim[:, j:j + 1], ri[:, l1:l1 + 1], AF.Identity, scale=ecl, bias=sc1),
                                        [rrs[1], "E", rsc], [n(f"S{j}")]),
                    ]
                return ops

            def demod(c):
                p1, p2 = sh.pt
                p3, p4 = c["mr"], c["mi"]
                rp = ["ptmp0", "ptmp1"]
                rr, ri, sr, si, ecb, esb, rrs, rss, rm = c["rr"], c["ri"], c["sr"], c["si"], c["ecb"], c["esb"], c["rrs"], c["rss"], c["rm"]
                G(lambda e: e.tensor_tensor(v3(p1), v3(rr), ecb, ALU.mult), [rrs[0], "E"], [rp[0]])
                G(lambda e: e.tensor_tensor(v3(p2), v3(ri), esb, ALU.mult), [rrs[1], "E"], [rp[1]])
                G(lambda e: e.tensor_tensor(v3(p3), v3(ri), ecb, ALU.mult), [rrs[1], "E"], [rm[0]])
                G(lambda e: e.tensor_tensor(v3(p4), v3(rr), esb, ALU.mult), [rrs[0], "E"], [rm[1]])
                G(lambda e: e.tensor_tensor(sr[:, :Tt], p1[:, :Tt], p2[:, :Tt], ALU.subtract), rp, [rss[0]])
                G(lambda e: e.tensor_tensor(si[:, :Tt], p3[:, :Tt], p4[:, :Tt], ALU.add), rm, [rss[1]])

            emit_bu(0)
            prev = None
            for j in range(NPAIR + 1):
                cur = None
                mods = []
                if j < NPAIR:
                    cur = pair_ctx(j)
                    A(lambda e, o=cur["rr"]: e.copy(o[:, :Tt], pbr), [rbr], [cur["rrs"][0]])
                    A(lambda e, o=cur["ri"]: e.copy(o[:, :Tt], pbi), [rbi], [cur["rrs"][1]])
                    if j + 1 < NPAIR:
                        emit_bu(j + 1)
                    mods = mod_ops(cur)
                scans = scan_ops(prev) if prev is not None else []
                lead = min(4, len(scans)) if mods else len(scans)
                for op in scans[:lead]:
                    op()
                rest = scans[lead:]
                while mods or rest:
                    if mods:
                        mods.pop(0)()
                    if rest:
                        rest.pop(0)()
                if prev is not None:
                    if j >= 3:
                        emit_c(j - 3)
                    demod(prev)
                prev = cur
                yield 6.0
            emit_c(NPAIR - 2)
            emit_c(NPAIR - 1)
            yield 1.0

            zres = [n(f"h{c}") for c in range(4)]
            for s in range(2):
                w, rw = next_slab(S, "ssm_glu_w", s)
                for i2 in range(2):
                    n_ = 2 * s + i2
                    pa, ra = hb(S, "A", Tt)
                    for k in range(4):
                        mm(pa, w[:, k, i2 * 128:(i2 + 1) * 128], zz(k), k == 0, k == 3, [rw] + zres, [ra])
                    kt = S.rot["ft"]; S.rot["ft"] ^= 1
                    A(lambda e, pa=pa, kt=kt, n_=n_: e.activation(S.ftmp[kt][:, :Tt], pa, AF.Sigmoid, bias=colsB[:, n_, 1:2]),
                      [ra, "colsB"], [n(f"ftmp{kt}")])
                    V(lambda e, kt=kt, n_=n_: e.tensor_tensor(z2(n_), zz(n_), S.ftmp[kt][:, :Tt], ALU.mult),
                      [n(f"ftmp{kt}"), n(f"h{n_}")], [n(f"h{4 + n_}")])
                yield 8 * MMC + 0.5

            vpb = S.vpb
            for c in range(4):
                A(lambda e, c=c: e.copy(vpb[:, c, 0:HALO + Tt], vp[:, c, 0:HALO + Tt]), [n(f"vp{c}")], [n(f"vpb{c}")])
            for c in range(4):
                pa, ra = hb(S, "A", Tt)
                for jt in range(CW):
                    mm(pa, Dg[:, c, jt, :], vpb[:, c, jt:jt + Tt], jt == 0, jt == CW - 1, ["Dg", n(f"vpb{c}")], [ra])
                A(lambda e, c=c, pa=pa: e.activation(sg[:, c, :Tt], pa, AF.Identity, bias=colsB[:, c, 2:3]), [ra, "colsB"], [n(f"sg{c}")])
                yield CW * MMC
            psm, rsm = bankk(S, 1, Tt)
            pst, rst = bankk(S, 3, Tt)
            for c in range(4):
                A(lambda e, c=c: e.copy(um[:, c, :Tt], sg[:, c, :Tt]), [n(f"sg{c}")], [n(f"um{c}")])
                mm(psm, ones[:], um[:, c, :Tt], c == 0, c == 3, ["ones", n(f"um{c}")], [rsm])
            V(lambda e: e.tensor_scalar(S.mean[:, :Tt], psm, 1.0 / DS, None, ALU.mult), [rsm], [n("mean")])
            for c in range(4):
                k = S.rot["sq"]; S.rot["sq"] ^= 1
                A(lambda e, c=c, k=k: e.activation(S.sq[k][:, :Tt], sg[:, c, :Tt], AF.Square), [n(f"sg{c}")], [n(f"sq{k}")])
                mm(pst, ones[:], S.sq[k][:, :Tt], c == 0, c == 3, ["ones", n(f"sq{k}")], [rst])
            V(lambda e: e.tensor_tensor(S.msq[:, :Tt], S.mean[:, :Tt], S.mean[:, :Tt], ALU.mult), [n("mean")], [n("msq")])
            V(lambda e: e.scalar_tensor_tensor(S.rstd[:, :Tt], pst, 1.0 / DS, S.msq[:, :Tt], ALU.mult, ALU.subtract),
              [rst, n("msq")], [n("rstd")])
            V(lambda e: e.tensor_scalar(S.rstd[:, :Tt], S.rstd[:, :Tt], EPS, None, ALU.add), [n("rstd")], [n("rstd")])
            A(lambda e: e.activation(S.rstd[:, :Tt], S.rstd[:, :Tt], AF.Ln), [n("rstd")], [n("rstd")])
            A(lambda e: e.activation(S.rstd[:, :Tt], S.rstd[:, :Tt], AF.Exp, scale=-0.5), [n("rstd")], [n("rstd")])
            for c in range(4):
                V(lambda e, c=c: e.tensor_tensor(sg[:, c, :Tt], sg[:, c, :Tt], S.mean[:, :Tt], ALU.subtract), [n(f"sg{c}"), n("mean")], [n(f"sg{c}")])
                V(lambda e, c=c: e.tensor_tensor(sg[:, c, :Tt], sg[:, c, :Tt], S.rstd[:, :Tt], ALU.mult), [n(f"sg{c}"), n("rstd")], [n(f"sg{c}")])
                A(lambda e, c=c: e.activation(cact(c), sg[:, c, :Tt], AF.Silu, bias=colsB[:, c, 4:5], scale=colsB[:, c, 3:4]),
                  [n(f"sg{c}"), "colsB"], [n(f"g{16 + c}")])
            yield "unlock"
            yield 8.0
            z2res = [n(f"h{4 + c}") for c in range(4)]
            cres = [n(f"g{16 + c}") for c in range(4)]
            for s in range(4):
                ws, rws = next_slab(S, "ssm_out_w", s)
                wc, rwc = next_slab(S, "conv_out_w", s)
                for i2 in range(2):
                    n_ = 2 * s + i2
                    pa, ra = bankk(S, 1, Tt); pb_, rb = bankk(S, 3, Tt)
                    for k in range(4):
                        mm(pa, ws[:, k, i2 * 128:(i2 + 1) * 128], z2(k), k == 0, k == 3, [rws] + z2res, [ra])
                    for k in range(4):
                        mm(pb_, wc[:, k, i2 * 128:(i2 + 1) * 128], cact(k), k == 0, k == 3, [rwc] + cres, [rb])
                    V(lambda e, pa=pa, n_=n_: e.tensor_tensor(S.ftmp[0][:, :Tt], pa, gs(n_), ALU.mult), [ra, n(f"g{n_}")], [n("ftmp0")])
                    V(lambda e, pb_=pb_, n_=n_: e.tensor_tensor(S.ftmp[1][:, :Tt], pb_, gc(n_), ALU.mult), [rb, n(f"g{8 + n_}")], [n("ftmp1")])
                    V(lambda e, n_=n_: e.tensor_tensor(gs(n_), S.ftmp[0][:, :Tt], S.ftmp[1][:, :Tt], ALU.add), [n("ftmp0"), n("ftmp1")], [n(f"g{n_}")])
                yield 16 * MMC + 1.0
            mres = [n(f"g{n_}") for n_ in range(8)]
            for s in range(4):
                w, rw = next_slab(S, "w_o", s)
                for i2 in range(2):
                    n_ = 2 * s + i2
                    py, ry = hb(S, "Y", Tt)
                    for k in range(8):
                        mm(py, w[:, k, i2 * 128:(i2 + 1) * 128], gs(k), k == 0, k == 7, [rw] + mres, [ry])
                    V(lambda e, py=py, n_=n_: e.tensor_tensor(S.xT[:, n_, :Tt], S.xT[:, n_, :Tt], py, ALU.add), [ry, n(f"xT{n_}")], [n(f"xT{n_}")])
                yield 16 * MMC

        def emit_conv_state(S, Tt, dst_ap):
            n = S.n
            bank = S.pb[0]
            rb = [S.pbn[0]]
            for c in range(4):
                tr(bank[:HALO, c * 128:(c + 1) * 128], S.vp[:, c, Tt:Tt + HALO], 128, [n(f"vp{c}"), "ident"], rb)
            A(lambda e: e.copy(ncst[:, :], bank[:HALO, 0:512]), rb, ["ncst"])
            P.dma("scalar", dst_ap, ncst[:, :], reads=["ncst"], sem="o_nc")

        def emit_ssm_state(S, dst_re, dst_im):
            n = S.n
            bank = S.pb[0]
            rb = [S.pbn[0]]
            Sall = [n(f"S{j}") for j in range(16)]
            tr(bank[:16, 0:128], S.Sre[:, :], 128, Sall + ["ident"], rb)
            tr(bank[:16, 128:256], S.Sim[:, :], 128, Sall + ["ident"], rb)
            A(lambda e: e.copy(sst[:, :], bank[:16, 0:256]), rb, ["sst"])
            P.dma("scalar", dst_re.rearrange("(j gl) p -> j (gl p)", gl=2), sst[:, 0:128], reads=["sst"], sem="o_st")
            P.dma("scalar", dst_im.rearrange("(j gl) p -> j (gl p)", gl=2), sst[:, 128:256], reads=["sst"], sem="o_st")

        def load_x(Tt, src_ap):
            nb = (Tt + 127) // 128
            tb = min(128, Tt)
            P.dma("sync", xin[:tb, 0:nb * 1024].rearrange("p (b d) -> p b d", b=nb),
                  src_ap.rearrange("(b p) d -> p b d", p=tb), writes=["xin"], sem="xin")

        def transpose_in(S, Tt):
            n = S.n
            nb = (Tt + 127) // 128
            tb = min(128, Tt)
            for c in range(8):
                py, ry = hb(S, "A", Tt)
                for b in range(nb):
                    tr(py[:, b * tb:(b + 1) * tb], xin[:tb, b * 1024 + c * 128: b * 1024 + (c + 1) * 128], tb, ["xin", "ident"], [ry])
                A(lambda e, c=c, py=py: e.copy(S.xT[:, c, :Tt], py), [ry], [n(f"xT{c}")])

        def emit_output(S, Tt, dst_ap):
            n = S.n
            nb = (Tt + 127) // 128
            tb = min(128, Tt)
            rmsnorm_stats(S, Tt, 8, lambda c: S.xT[:, c, :Tt], lambda c: n(f"xT{c}"), 1.0 / D)
            for c in range(8):
                V(lambda e, c=c: e.scalar_tensor_tensor(S.xT[:, c, :Tt], S.xT[:, c, :Tt], gcol["final"](c), S.rstd[:, :Tt],
                                                        ALU.mult, ALU.mult), [n(f"xT{c}"), "colsA", n("rstd")], [n(f"xT{c}")])
            bank = S.pb[0]
            rb = [S.pbn[0]]
            for b in range(nb):
                slot = b % 2
                ysl = S.gb32[:tb, slot * 1024:(slot + 1) * 1024]
                yres = [n(f"g{i}") for i in range(slot * 8, slot * 8 + 9)]
                for hf in range(2):
                    for c4 in range(4):
                        c = hf * 4 + c4
                        tr(bank[:tb, c4 * 128:(c4 + 1) * 128], S.xT[:, c, b * tb:(b + 1) * tb], 128, [n(f"xT{c}"), "ident"], rb)
                    A(lambda e, hf=hf, ysl=ysl: e.copy(ysl[:, hf * 512:(hf + 1) * 512], bank[:tb, 0:512]), rb, yres)
                P.dma("scalar", dst_ap[b * tb:(b + 1) * tb, :], ysl, reads=yres, sem=f"o_y{S.sid}_{slot}")

        def stream_gen(S, tiles, tps):
            n = S.n
            for idx, t in enumerate(tiles):
                kind, sq_i, ti_ = t
                if kind == "p":
                    Tt = T
                    src = dr["xp"][sq_i, ti_ * T:(ti_ + 1) * T, :]
                else:
                    Tt = DSEQ
                    src = dr["xs"][:, :]
                first = ti_ == 0
                last = (kind == "s") or (ti_ == tps - 1)
                load_x(Tt, src)
                transpose_in(S, Tt)
                yield 4.0
                if first:
                    Sall = [n(f"S{j}") for j in range(16)]
                    if kind == "p":
                        V(lambda e: e.memset(S.Sre[:], 0.0), [], Sall)
                        V(lambda e: e.memset(S.Sim[:], 0.0), [], Sall)
                        for c in range(4):
                            G(lambda e, c=c: e.memset(S.vp[:, c, 0:HALO], 0.0), [], [n(f"vp{c}")])
                    else:
                        bank = S.pb[0]
                        rb = [S.pbn[0]]
                        for (srcS, dstS) in ((dr["sre"], S.Sre), (dr["sim"], S.Sim)):
                            P.dma("sync", sst[0:16, 0:128], srcS.rearrange("(j gl) p -> j (gl p)", gl=2), writes=["sst"], sem="su")
                            tr(bank[:, 0:16], sst[0:16, 0:128], 16, ["sst", "ident"], rb)
                            A(lambda e, dstS=dstS: e.copy(dstS[:], bank[:, 0:16]), rb, Sall)
                        P.dma("sync", ncst[:, :], dr["cc"], writes=["ncst"], sem="su")
                        for c in range(4):
                            tr(bank[:, c * HALO:(c + 1) * HALO], ncst[:, c * 128:(c + 1) * 128], HALO, ["ncst", "ident"], rb)
                        for c in range(4):
                            A(lambda e, c=c: e.copy(S.vp[:, c, 0:HALO], bank[:, c * HALO:(c + 1) * HALO]), rb, [n(f"vp{c}")])
                yield from ffn(S, Tt, "ffn1")
                yield from mixer(S, Tt)
                if last:
                    if kind == "p":
                        emit_conv_state(S, Tt, dr["ncp"][sq_i])
                        emit_ssm_state(S, dr["nrp"][sq_i], dr["nip"][sq_i])
                    else:
                        emit_conv_state(S, Tt, dr["ncs"])
                        emit_ssm_state(S, dr["nrs"], dr["nis"])
                else:
                    for c in range(4):
                        G(lambda e, c=c, Tt=Tt: e.tensor_copy(S.vp[:, c, 0:HALO], S.vp[:, c, Tt:Tt + HALO]), [n(f"vp{c}")], [n(f"vp{c}")])
                yield from ffn(S, Tt, "ffn2")
                if kind == "p":
                    emit_output(S, Tt, dr["yp"][sq_i, ti_ * T:(ti_ + 1) * T, :])
                else:
                    emit_output(S, Tt, dr["ys"])
                yield 4.0

        tps = seq_len // T
        streams = [make_stream(0), make_stream(1)]
        tile_lists = [[], []]
        for sq_i in range(n_prompt_seq):
            sid = 0 if sq_i < (n_prompt_seq + 1) // 2 else 1
            for ti_ in range(tps):
                tile_lists[sid].append(("p", sq_i, ti_))
        if do_sample:
            tile_lists[1].append(("s", 0, 0))
        gens = []
        for S, tl in zip(streams, tile_lists):
            for _ in tl:
                S.slab_seq.extend(tile_slabs())
            gens.append(stream_gen(S, tl, tps))
        clocks = [0.0, 230.0]
        alive = [len(tile_lists[0]) > 0, len(tile_lists[1]) > 0]
        waiting = [False, False]
        owner = None
        while any(alive):
            cands = [k for k in range(2) if alive[k] and not (waiting[k] and owner is not None)]
            i = min(cands, key=lambda k: clocks[k])
            if waiting[i]:
                waiting[i] = False
                owner = i
                clocks[i] = max(clocks[i], clocks[1 - i])
                continue
            try:
                r = next(gens[i])
            except StopIteration:
                alive[i] = False
                if owner == i:
                    owner = None
                continue
            if r == "lock":
                if owner is None:
                    owner = i
                else:
                    waiting[i] = True
            elif r == "unlock":
                owner = None
            else:
                clocks[i] += r
        P.wait_all("sync")
        P.build()
    return nc, P


_CACHE = {}


def kernel(**inputs):
    inp = {k: np.ascontiguousarray(np.asarray(v)) for k, v in inputs.items()}
    if "nc" not in _CACHE:
        _CACHE["nc"] = build_program()[0]
    nc = _CACHE["nc"]
    in_maps = []
    for c in range(NCORES):
        m = {
            "xp": inp["x_prompt"][c * BPC:(c + 1) * BPC],
            "xs": inp["x_sample"][c],
            "sre": inp["state_ssm_re"][0, c],
            "sim": inp["state_ssm_im"][0, c],
            "cc": inp["cache_conv"][0, c],
        }
        for nm, K, N in WEIGHTS:
            m[nm] = inp[nm][0]
        for nm, shp in SMALL:
            m[nm] = inp[nm] if nm == "final_norm" else inp[nm][0]
        in_maps.append({k: np.ascontiguousarray(v, dtype=np.float32) for k, v in m.items()})
    res = run_bass_kernel_spmd(nc, in_maps, core_ids=list(range(NCORES)))
    R = res.results
    cat = lambda k: np.concatenate([r[k] for r in R], axis=0)
    stk = lambda k: np.stack([r[k] for r in R], axis=0)
    y_prompt = cat("yp")
    y_sample = stk("ys")
    nrp = cat("nrp")[None]; nip = cat("nip")[None]; ncp = cat("ncp")[None]
    nrs = stk("nrs")[None]; nis = stk("nis")[None]; ncs = stk("ncs")[None]
    return (y_prompt, y_sample, nrp, nip, ncp, nrs, nis, ncs)
```

```python
import numpy as np
from contextlib import ExitStack
import concourse.bass as bass
import concourse.mybir as mybir
from concourse.bass_utils import run_bass_kernel_spmd

F32 = mybir.dt.float32
BF16 = mybir.dt.bfloat16
I32 = mybir.dt.int32
ALU = mybir.AluOpType
AF = mybir.ActivationFunctionType

D = 1024
DFF = 2816
NIN = 3584
DS = 512
NPAIR = 16
CW = 31
HALO = 30
SEQ = 2048
BPC = 4
NCORES = 8
DSEQ = 16
EPS = 1e-6
TWO_PI = 6.283185307179586
PI = 3.141592653589793

MODE = "bf16"
ENGS = ("sync", "scalar", "vector", "gpsimd", "tensor")
import os
SAME_ENGINE_SYNC = os.environ.get("KSES", "1") == "1"
DBG = os.environ.get("KDBG", "")


class StopBuild(Exception):
    pass


class Prog:
    def __init__(self, nc):
        self.nc = nc
        self.ops = {e: [] for e in ENGS}
        self.count = {}
        self.waited = {e: {} for e in ENGS}
        self.res = {}
        self.sems = {}
        self.ninst = {e: 0 for e in ENGS}

    def _need(self, eng, toks):
        need = {}
        for t in toks:
            if t is None:
                continue
            s, v, te = t
            if te == eng and s.startswith("e_"):
                if eng == "tensor" or not SAME_ENGINE_SYNC:
                    continue
            if need.get(s, 0) < v:
                need[s] = v
        for s, v in need.items():
            if self.waited[eng].get(s, 0) >= v:
                continue
            self.waited[eng][s] = v
            self.ops[eng].append(("wait", s, v))

    def emit(self, eng, fn, reads=(), writes=(), sem=None, inc=1):
        toks = []
        for r in reads:
            st = self.res.get(r)
            if st is not None:
                toks.append(st[0])
        for w in writes:
            st = self.res.get(w)
            if st is not None:
                toks.append(st[0])
                toks.extend(st[1])
        self._need(eng, toks)
        if sem is None:
            sem = "e_" + eng
        v = self.count.get(sem, 0) + inc
        self.count[sem] = v
        tok = (sem, v, eng)
        self.ops[eng].append(("op", fn, sem, inc))
        self.ninst[eng] += 1
        for r in reads:
            st = self.res.setdefault(r, [None, []])
            st[1].append(tok)
        for w in writes:
            self.res[w] = [tok, []]
        return tok

    def dma(self, eng, out, in_, reads=(), writes=(), sem=None, **kw):
        prev = self.count.get("d_" + sem, 0)
        if prev and self.waited[eng].get("d_" + sem, 0) < prev:
            self.waited[eng]["d_" + sem] = prev
            self.ops[eng].append(("wait", "d_" + sem, prev))
        return self.emit(eng, lambda e: e.dma_start(out=out, in_=in_, **kw),
                         reads=reads, writes=writes, sem="d_" + sem, inc=16)

    def wait_all(self, eng):
        for s, v in self.count.items():
            if self.waited[eng].get(s, 0) < v:
                self.waited[eng][s] = v
                self.ops[eng].append(("wait", s, v))

    def build(self):
        nc = self.nc
        with ExitStack() as es:
            for s in self.count:
                self.sems[s] = es.enter_context(nc.semaphore(s))
            block = es.enter_context(nc.Block())
            prog = self

            def runner(name):
                def run(e):
                    for op in prog.ops[name]:
                        if op[0] == "wait":
                            e.wait_ge(prog.sems[op[1]], op[2])
                        else:
                            _, fn, s, inc = op
                            fn(e).then_inc(prog.sems[s], inc)
                return run

            block.sync(runner("sync"))
            block.scalar(runner("scalar"))
            block.vector(runner("vector"))
            block.gpsimd(runner("gpsimd"))
            block.tensor(runner("tensor"))


WEIGHTS = [
    ("ffn1_w1", D, DFF), ("ffn1_w3", D, DFF), ("ffn1_w2", DFF, D),
    ("w_in", D, NIN), ("ssm_glu_w", DS, DS), ("ssm_out_w", DS, D), ("conv_out_w", DS, D),
    ("w_o", D, D),
    ("ffn2_w1", D, DFF), ("ffn2_w3", D, DFF), ("ffn2_w2", DFF, D),
]
WSLAB_NC = {"ffn1_w2": 128, "ffn2_w2": 128}
SMALL = [
    ("ffn1_norm", [D]), ("mix_norm", [D]), ("ffn2_norm", [D]), ("final_norm", [D]),
    ("ssm_d", [DS]), ("ssm_glu_b", [DS]), ("conv_dw_b", [DS]), ("conv_ln_g", [DS]), ("conv_ln_b", [DS]),
    ("conv_dw_w", [CW, DS]),
    ("ssm_lambda_re", [32, 64]), ("ssm_lambda_im", [32, 64]), ("ssm_log_step", [32]),
    ("ssm_b_re", [32, 64, 16]), ("ssm_b_im", [32, 64, 16]),
    ("ssm_c_re", [32, 16, 64]), ("ssm_c_im", [32, 16, 64]),
]


def build_program(n_prompt_seq=BPC, seq_len=SEQ, do_sample=True, mode=MODE, stop_after=None):
    bf = mode == "bf16"
    assert bf
    MDT = BF16
    T = 256
    SBLK = 128
    NSLOT = 4
    SLOT_EL = 2048
    NSTREAM = 2
    nc = bass.Bass("TRN2", target_bir_lowering=False)
    P = Prog(nc)
    dr = {}

    def din(name, shape, dt=F32):
        dr[name] = nc.dram_tensor(name, list(shape), dt, kind="ExternalInput").ap()
        return dr[name]

    def dout(name, shape, dt=F32):
        dr[name] = nc.dram_tensor(name, list(shape), dt, kind="ExternalOutput").ap()
        return dr[name]

    din("xp", [BPC, SEQ, D]); din("xs", [DSEQ, D])
    din("sre", [32, 64]); din("sim", [32, 64]); din("cc", [HALO, DS])
    for nm, K, N in WEIGHTS:
        din(nm, [K, N])
    for nm, shp in SMALL:
        din(nm, shp)
    dout("yp", [BPC, SEQ, D]); dout("ys", [DSEQ, D])
    dout("nrp", [BPC, 32, 64]); dout("nip", [BPC, 32, 64]); dout("ncp", [BPC, HALO, DS])
    dout("nrs", [32, 64]); dout("nis", [32, 64]); dout("ncs", [HALO, DS])

    winfo = {}
    for nm, K, N in WEIGHTS:
        if nm.endswith("_w2"):
            kc, ncol, ns = 11, 128, 16
        else:
            ncol = 256
            kc = K // 128
            ns = N // ncol
        winfo[nm] = (kc, ncol, ns)
        dr["scr_" + nm] = nc.dram_tensor("scr_" + nm, [ns, 128, kc * ncol], BF16, kind="Internal").ap()

    es = ExitStack()
    with es:
        def sb(name, shape, dt=F32):
            return es.enter_context(nc.sbuf_tensor(name, list(shape), dt))

        def psb(name):
            return es.enter_context(nc.psum_tensor(name, [128, 512], F32))

        ps = [psb(f"ps{i}") for i in range(8)]
        ident = sb("ident", [128, 128])
        ones = sb("ones", [128, 128], MDT)
        wslab = sb("wslab", [128, NSTREAM * NSLOT * SLOT_EL], MDT)
        wsl32 = wslab[:].bitcast(F32)
        Ec = sb("Ec", [128, NPAIR, SBLK]); Es = sb("Es", [128, NPAIR, SBLK])
        WBr = sb("WBr", [128, NPAIR, 128], MDT); WBi = sb("WBi", [128, NPAIR, 128], MDT)
        WCr = sb("WCr", [128, NPAIR, 128], MDT); WCi = sb("WCi", [128, NPAIR, 128], MDT)
        amag = sb("amag", [128, NPAIR])
        nEsL = sb("nEsL", [128, NPAIR, 2])
        colsA = sb("colsA", [128, 8, 4])
        colsB = sb("colsB", [128, 4, 36])
        ncst = sb("ncst", [HALO, DS])
        sst = sb("sst", [16, 256])
        xin = sb("xinr", [128, 2048])
        identb = sb("identb", [128, 128], MDT)
        Dg = sb("Dg", [128, 4, CW, 128], MDT)
        ti = sb("ti", [128, 16], I32)

        def V(fn, r, w): return P.emit("vector", fn, reads=r, writes=w)
        def A(fn, r, w): return P.emit("scalar", fn, reads=r, writes=w)
        def G(fn, r, w): return P.emit("gpsimd", fn, reads=r, writes=w)
        def PE(fn, r, w): return P.emit("tensor", fn, reads=r, writes=w)

        def mm(out, lhsT, rhs, start, stop, r, w):
            return PE(lambda e: e.matmul(out, lhsT=lhsT, rhs=rhs, start=start, stop=stop), r, w)

        def tr(out, in_, n, r, w):
            return PE(lambda e: e.transpose(out, in_, ident[:n, :n]), r, w)

        xin_real = xin
        xin = wsl32[:, 0:3584]
        gb32 = wsl32[:, 3584:7168]
        gres = lambda i: f"g{i}"
        stg = gb32[0:36, 2560:3584]

        G(lambda e: e.memset(ident[:], 0.0), [], ["ident"])
        G(lambda e: e.affine_select(out=ident[:], in_=ident[:], pattern=[[-1, 128]], compare_op=ALU.not_equal,
                                    fill=1.0, base=0, channel_multiplier=1), ["ident"], ["ident"])
        V(lambda e: e.memset(ones[:], 1.0), [], ["ones"])

        def wsrc(nm, s):
            kc, ncol, ns = winfo[nm]
            if nm.endswith("_w2"):
                n_, kh = s // 2, s % 2
                return dr[nm][kh * 1408:(kh + 1) * 1408, n_ * 128:(n_ + 1) * 128].rearrange("(kc p) n -> p kc n", p=128)
            return dr[nm][:, s * ncol:(s + 1) * ncol].rearrange("(kc p) n -> p kc n", p=128)

        if bf:
            for nm, K, N in WEIGHTS:
                kc, ncol, ns = winfo[nm]
                for s in range(ns):
                    P.dma("gpsimd", dr["scr_" + nm][s].rearrange("p (kc n) -> p kc n", kc=kc), wsrc(nm, s),
                          writes=[f"scr_{nm}_{s}"], sem=f"cast{s % 4}")

        STG = [f"g{i}" for i in range(22)]
        for i, nm in enumerate(["ffn1_norm", "mix_norm", "ffn2_norm", "final_norm"]):
            P.dma("sync", stg[i:i + 1, :], dr[nm].rearrange("(o d) -> o d", o=1), writes=STG, sem="su")
        for c in range(8):
            tr(ps[7][:, c * 4:(c + 1) * 4], stg[0:4, c * 128:(c + 1) * 128], 4, STG + ["ident"], ["ps7"])
        V(lambda e: e.tensor_copy(colsA[:].rearrange("p c k -> p (c k)"), ps[7][:, 0:32]), ["ps7"], ["colsA"])
        for i, nm in enumerate(["ssm_d", "ssm_glu_b", "conv_dw_b", "conv_ln_g", "conv_ln_b"]):
            P.dma("sync", stg[i:i + 1, 0:DS], dr[nm].rearrange("(o d) -> o d", o=1), reads=[], writes=STG, sem="su")
        P.dma("sync", stg[5:36, 0:DS], dr["conv_dw_w"], writes=STG, sem="su")
        for c in range(4):
            tr(ps[7][:, c * 36:(c + 1) * 36], stg[0:36, c * 128:(c + 1) * 128], 36, STG + ["ident"], ["ps7"])
        V(lambda e: e.tensor_copy(colsB[:].rearrange("p c k -> p (c k)"), ps[7][:, 0:144]), ["ps7"], ["colsB"])
        gcol = {"ffn1": lambda c: colsA[:, c, 0:1], "mix": lambda c: colsA[:, c, 1:2],
                "ffn2": lambda c: colsA[:, c, 2:3], "final": lambda c: colsA[:, c, 3:4]}
        dcol = lambda c: colsB[:, c, 0:1]
        glubcol = lambda c: colsB[:, c, 1:2]
        dwbcol = lambda c: colsB[:, c, 2:3]
        lngcol = lambda c: colsB[:, c, 3:4]
        lnbcol = lambda c: colsB[:, c, 4:5]
        tapcol = lambda c, j: colsB[:, c, 5 + j:6 + j]

        xs32 = xin
        LR = xs32[:, 0:16]; LI = xs32[:, 16:32]; DT = xs32[:, 32:48]; TH = xs32[:, 48:64]
        AR = xs32[:, 64:80]; AI = xs32[:, 80:96]; KR = xs32[:, 96:112]; KI = xs32[:, 112:128]
        t0 = xs32[:, 128:144]; t1 = xs32[:, 144:160]; t2 = xs32[:, 160:176]; t3 = xs32[:, 176:192]
        SN = xs32[:, 192:208]; CS = xs32[:, 208:224]
        XR = "xin"

        def loadT16(dst, src_ap):
            P.dma("sync", sst[0:16, 0:128], src_ap, writes=["sst"], sem="su")
            tr(ps[7][:, 0:16], sst[0:16, 0:128], 16, ["sst", "ident"], ["ps7"])
            V(lambda e: e.tensor_copy(dst, ps[7][:, 0:16]), ["ps7"], [XR])

        loadT16(LR, dr["ssm_lambda_re"].rearrange("(j gl) p -> j (gl p)", gl=2))
        loadT16(LI, dr["ssm_lambda_im"].rearrange("(j gl) p -> j (gl p)", gl=2))
        lsh = dr["ssm_log_step"].tensor
        for gl in range(2):
            P.dma("sync", xs32[gl * 64:(gl + 1) * 64, 32:48], bass.AP(tensor=lsh, offset=gl, ap=[[0, 64], [2, 16]]),
                  writes=[XR], sem="su", allow_slow_non_contiguous=True)
        A(lambda e: e.activation(DT, DT, AF.Exp), [XR], [XR])
        V(lambda e: e.tensor_tensor(t0, LR, DT, ALU.mult), [XR], [XR])
        A(lambda e: e.activation(amag[:], t0, AF.Exp), [XR], ["amag"])
        V(lambda e: e.tensor_tensor(TH, LI, DT, ALU.mult), [XR], [XR])

        V(lambda e: e.tensor_scalar(t1, TH, 1.0 / TWO_PI, None, ALU.mult), [XR], [XR])
        V(lambda e: e.tensor_copy(ti[:], t1), [XR], ["ti"])
        V(lambda e: e.tensor_copy(t2, ti[:]), ["ti"], [XR])
        V(lambda e: e.scalar_tensor_tensor(t1, t2, -TWO_PI, TH, ALU.mult, ALU.add), [XR], [XR])
        V(lambda e: e.tensor_scalar(t2, t1, PI, TWO_PI, ALU.is_gt, ALU.mult), [XR], [XR])
        V(lambda e: e.tensor_tensor(t1, t1, t2, ALU.subtract), [XR], [XR])
        V(lambda e: e.tensor_scalar(t2, t1, -PI, TWO_PI, ALU.is_lt, ALU.mult), [XR], [XR])
        V(lambda e: e.tensor_tensor(t1, t1, t2, ALU.add), [XR], [XR])
        V(lambda e: e.tensor_scalar(t1, t1, 0.25, None, ALU.mult), [XR], [XR])
        V(lambda e: e.tensor_scalar(t2, t1, PI / 2, None, ALU.add), [XR], [XR])
        A(lambda e: e.activation(SN, t1, AF.Sin), [XR], [XR])
        A(lambda e: e.activation(CS, t2, AF.Sin), [XR], [XR])
        for _ in range(2):
            V(lambda e: e.tensor_tensor(t1, CS, CS, ALU.mult), [XR], [XR])
            V(lambda e: e.tensor_tensor(t2, SN, SN, ALU.mult), [XR], [XR])
            V(lambda e: e.tensor_tensor(t3, CS, SN, ALU.mult), [XR], [XR])
            V(lambda e: e.tensor_tensor(CS, t1, t2, ALU.subtract), [XR], [XR])
            V(lambda e: e.tensor_scalar(SN, t3, 2.0, None, ALU.mult), [XR], [XR])
        V(lambda e: e.tensor_tensor(AR, amag[:], CS, ALU.mult), [XR, "amag"], [XR])
        V(lambda e: e.tensor_tensor(AI, amag[:], SN, ALU.mult), [XR, "amag"], [XR])
        V(lambda e: e.tensor_scalar(t0, AR, -1.0, None, ALU.add), [XR], [XR])
        V(lambda e: e.tensor_tensor(t1, LR, LR, ALU.mult), [XR], [XR])
        V(lambda e: e.tensor_tensor(t2, LI, LI, ALU.mult), [XR], [XR])
        V(lambda e: e.tensor_tensor(t1, t1, t2, ALU.add), [XR], [XR])
        V(lambda e: e.reciprocal(t1, t1), [XR], [XR])
        V(lambda e: e.tensor_tensor(t2, t0, LR, ALU.mult), [XR], [XR])
        V(lambda e: e.tensor_tensor(t3, AI, LI, ALU.mult), [XR], [XR])
        V(lambda e: e.tensor_tensor(t2, t2, t3, ALU.add), [XR], [XR])
        V(lambda e: e.tensor_tensor(KR, t2, t1, ALU.mult), [XR], [XR])
        V(lambda e: e.tensor_tensor(t2, AI, LR, ALU.mult), [XR], [XR])
        V(lambda e: e.tensor_tensor(t3, t0, LI, ALU.mult), [XR], [XR])
        V(lambda e: e.tensor_tensor(t2, t2, t3, ALU.subtract), [XR], [XR])
        V(lambda e: e.tensor_tensor(KI, t2, t1, ALU.mult), [XR], [XR])
        V(lambda e: e.tensor_copy(Ec[:, :, 0:1], CS.unsqueeze(2)), [XR], ["E"])
        V(lambda e: e.tensor_copy(Es[:, :, 0:1], SN.unsqueeze(2)), [XR], ["E"])
        TA = xs32[:, 1024:1024 + 16 * 64].rearrange("p (j m) -> p j m", j=16)
        TB = xs32[:, 2048:2048 + 16 * 64].rearrange("p (j m) -> p j m", j=16)
        n = 1
        while n < SBLK:
            cn = Ec[:, :, n - 1:n].to_broadcast([128, NPAIR, n])
            sn = Es[:, :, n - 1:n].to_broadcast([128, NPAIR, n])
            c_lo = Ec[:, :, 0:n]; s_lo = Es[:, :, 0:n]
            c_hi = Ec[:, :, n:2 * n]; s_hi = Es[:, :, n:2 * n]
            ta = TA[:, :, 0:n]; tb = TB[:, :, 0:n]
            V(lambda e, a=ta, x=c_lo, y=cn: e.tensor_tensor(a, x, y, ALU.mult), ["E"], [XR])
            V(lambda e, a=tb, x=s_lo, y=sn: e.tensor_tensor(a, x, y, ALU.mult), ["E"], [XR])
            V(lambda e, o=c_hi, a=ta, b=tb: e.tensor_tensor(o, a, b, ALU.subtract), [XR, "E"], ["E"])
            V(lambda e, a=ta, x=s_lo, y=cn: e.tensor_tensor(a, x, y, ALU.mult), ["E"], [XR])
            V(lambda e, a=tb, x=c_lo, y=sn: e.tensor_tensor(a, x, y, ALU.mult), ["E"], [XR])
            V(lambda e, o=s_hi, a=ta, b=tb: e.tensor_tensor(o, a, b, ALU.add), [XR, "E"], ["E"])
            n *= 2
        BR = xs32[:, 256:512].rearrange("p (j c) -> p j c", j=16)
        BI = xs32[:, 512:768].rearrange("p (j c) -> p j c", j=16)
        BBR = xs32[:, 768:1024].rearrange("p (j c) -> p j c", j=16)
        BBI = xs32[:, 3072:3328].rearrange("p (j c) -> p j c", j=16)
        P.dma("sync", BR, dr["ssm_b_re"].rearrange("(j gl) p c -> (gl p) j c", gl=2), writes=[XR], sem="su")
        P.dma("sync", BI, dr["ssm_b_im"].rearrange("(j gl) p c -> (gl p) j c", gl=2), writes=[XR], sem="su")
        krb = KR.unsqueeze(2).to_broadcast([128, 16, 16]); kib = KI.unsqueeze(2).to_broadcast([128, 16, 16])
        ta = TA[:, :, 0:16]; tb = TB[:, :, 0:16]
        V(lambda e: e.tensor_tensor(ta, BR, krb, ALU.mult), [XR], [XR])
        V(lambda e: e.tensor_tensor(tb, BI, kib, ALU.mult), [XR], [XR])
        V(lambda e: e.tensor_tensor(BBR, ta, tb, ALU.subtract), [XR], [XR])
        V(lambda e: e.tensor_tensor(ta, BI, krb, ALU.mult), [XR], [XR])
        V(lambda e: e.tensor_tensor(tb, BR, kib, ALU.mult), [XR], [XR])
        V(lambda e: e.tensor_tensor(BBI, ta, tb, ALU.add), [XR], [XR])
        Z = gb32[:, 0:2048].rearrange("p (j r) -> p j r", j=16)
        ZR = [gres(i) for i in range(22)]
        for (BB, WB, nm) in ((BBR, WBr, "WBr"), (BBI, WBi, "WBi")):
            V(lambda e: e.memset(Z, 0.0), [], ZR)
            for jj in range(4):
                for gl in range(2):
                    lo = gl * 64
                    V(lambda e, jj=jj, gl=gl, lo=lo, BB=BB: e.tensor_copy(
                        Z[lo:lo + 64, jj::4, 32 * jj + 16 * gl:32 * jj + 16 * gl + 16], BB[lo:lo + 64, jj::4, :]),
                      [XR], ZR)
            for j in range(16):
                pb = ps[j % 2]
                tr(pb[:, 0:128], Z[:, j, :], 128, ZR + ["ident"], [f"ps{j % 2}"])
                V(lambda e, j=j, pb=pb, WB=WB: e.tensor_copy(WB[:, j, :], pb[:, 0:128]), [f"ps{j % 2}"], [nm])
        CX = gb32[:, 2048:2048 + 512].rearrange("p (q m) -> p q m", q=4)
        for (cname, WC, nm, sgn) in (("ssm_c_re", WCr, "WCr", 1.0), ("ssm_c_im", WCi, "WCi", -1.0)):
            csrc = dr[cname].rearrange("(q g) c p -> (g c) q p", q=4)
            P.dma("sync", CX[:, :, 0:64], csrc, writes=ZR, sem="su")
            P.dma("sync", CX[:, :, 64:128], csrc, writes=[gres(21)], sem="su")
            V(lambda e, WC=WC: e.memset(WC[:], 0.0), [], [nm])
            for q in range(4):
                pb = ps[q % 2]
                tr(pb[:, 0:128], CX[:, q, :], 128, ZR + ["ident"], [f"ps{q % 2}"])
                for jj in range(4):
                    j = 4 * q + jj
                    for gl in range(2):
                        lo = gl * 64
                        c0 = 32 * jj + 16 * gl
                        V(lambda e, WC=WC, j=j, lo=lo, c0=c0, pb=pb, sgn=sgn: e.tensor_scalar(
                            WC[lo:lo + 64, j, c0:c0 + 16], pb[lo:lo + 64, c0:c0 + 16], sgn, None, ALU.mult),
                          [f"ps{q % 2}"], [nm])

        V(lambda e: e.tensor_copy(identb[:], ident[:]), ["ident"], ["identb"])
        for c in range(4):
            for jt in range(CW):
                V(lambda e, c=c, jt=jt: e.tensor_scalar(Dg[:, c, jt, :], identb[:], colsB[:, c, 5 + jt:6 + jt], None, ALU.mult),
                  ["identb", "colsB"], ["Dg"])
        V(lambda e: e.tensor_scalar(nEsL[:, :, 0], Es[:, :, SBLK - 1], -1.0, None, ALU.mult), ["E"], ["E2"])
        V(lambda e: e.tensor_scalar(nEsL[:, :, 1], Es[:, :, DSEQ - 1], -1.0, None, ALU.mult), ["E"], ["E2"])
        slot_names = [f"{sid}:wsl{k}" for sid in range(NSTREAM) for k in range(NSLOT)]
        V(lambda e: e.memset(ti[:, 0:1], 0), [], ["xin", "ti"] + [f"g{i}" for i in range(22)] + slot_names)
        xin = xin_real

        gcol = {"ffn1": lambda c: colsA[:, c, 0:1], "mix": lambda c: colsA[:, c, 1:2],
                "ffn2": lambda c: colsA[:, c, 2:3], "final": lambda c: colsA[:, c, 3:4]}

        def tile_slabs():
            L = []
            def ffn(f):
                r = []
                for s in range(11):
                    r.append((f + "_w1", s)); r.append((f + "_w3", s))
                for s in range(16):
                    r.append((f + "_w2", s))
                return r
            L += ffn("ffn1")
            for s in (0, 1, 4, 5, 2, 3, 6, 7, 8, 9, 10, 11, 12, 13):
                L.append(("w_in", s))
            L += [("ssm_glu_w", 0), ("ssm_glu_w", 1)]
            for s in range(4):
                L.append(("ssm_out_w", s)); L.append(("conv_out_w", s))
            for s in range(4):
                L.append(("w_o", s))
            L += ffn("ffn2")
            return L

        class Stream:
            pass

        SHARED_PREFIX = ("u32_", "um", "sg", "tt", "m0_", "m1_", "r0_", "r1_", "s0_", "s1_", "mean", "msq", "yssm")
        sh = Stream()
        sh.tt = [sb(f"tt{i}", [128, T]) for i in range(2)]
        sh.mrmi = [[sb(f"m{b}_{i}", [128, T]) for i in range(2)] for b in range(2)]
        sh.rri = [[sb(f"r{b}_{i}", [128, T]) for i in range(2)] for b in range(2)]
        sh.srsi = [[sb(f"s{b}_{i}", [128, T], MDT) for i in range(2)] for b in range(2)]
        sh.u32 = sb("u32", [128, 4, T]); sh.um = sb("um", [128, 4, T], MDT)
        sh.sg = sb("sg", [128, 4, T])
        sh.mean = sb("mean", [128, T]); sh.msq = sb("msq", [128, T])
        sh.yssm = [sb(f"yssm{i}", [128, T]) for i in range(2)]
        sh.pt = [sb(f"ptmp{i}", [128, T]) for i in range(2)]

        def make_stream(sid):
            S = Stream()
            S.sid = sid
            n = lambda x: x if x.startswith(SHARED_PREFIX) else f"{sid}:{x}"
            S.n = n
            pre = f"s{sid}_"
            S.xT = sb(pre + "xT", [128, 8, T])
            S.hbuf = sb(pre + "hbuf", [128, 8, T], MDT)
            S.gbuf = sb(pre + "gbuf", [128, 22 * T], MDT)
            S.gb32 = S.gbuf[:].bitcast(F32)
            S.Sre = sb(pre + "Sre", [128, NPAIR]); S.Sim = sb(pre + "Sim", [128, NPAIR])
            S.tt = sh.tt; S.mrmi = sh.mrmi; S.rri = sh.rri; S.srsi = sh.srsi
            S.s5c = sb(pre + "s5c", [128, 4])
            S.u32 = sh.u32; S.um = sh.um
            S.vp = sb(pre + "vp", [128, 4, HALO + T])
            S.vpb = sb(pre + "vpb", [128, 4, HALO + T], MDT)
            S.sg = sh.sg
            S.rstd = sb(pre + "rstd", [128, T]); S.mean = sh.mean; S.msq = sh.msq
            S.sq = [sb(pre + f"sq{i}", [128, T], MDT) for i in range(2)]
            S.ftmp = [sb(pre + f"ftmp{i}", [128, T]) for i in range(2)]
            S.yssm = sh.yssm
            S.wsl = [wslab[:, (sid * NSLOT + k) * SLOT_EL:(sid * NSLOT + k + 1) * SLOT_EL] for k in range(NSLOT)]
            S.pb = ps[4 * sid:4 * sid + 4]
            S.pbn = [f"ps{4 * sid + k}" for k in range(4)]
            S.rot = {"A": 0, "B": 0, "Y": 0, "sq": 0, "ft": 0}
            S.slab_seq = []
            S.state = {"next_load": 0, "cons": 0}
            return S

        def bankk(S, k, Tt=T):
            return S.pb[k][:, 0:Tt], S.pbn[k]

        def hb(S, kind, Tt=T):
            if kind == "A":
                k = 0 + 2 * S.rot["A"]
                S.rot["A"] ^= 1
            else:
                k = 1 + 2 * S.rot["B"]
                S.rot["B"] ^= 1
            return bankk(S, k, Tt)

        def issue_loads(S, upto):
            st = S.state
            while st["next_load"] <= min(upto, len(S.slab_seq) - 1):
                i = st["next_load"]
                nm, s = S.slab_seq[i]
                kc, ncol, ns = winfo[nm]
                slot = i % NSLOT
                dst = S.wsl[slot][:, 0:kc * ncol].rearrange("p (kc n) -> p kc n", kc=kc)
                src = dr["scr_" + nm][s].rearrange("p (kc n) -> p kc n", kc=kc)
                P.dma("sync", dst, src, reads=[f"scr_{nm}_{s}"], writes=[S.n(f"wsl{slot}")], sem=f"w{S.sid}_{slot}")
                st["next_load"] += 1

        def next_slab(S, nm_expect, s_expect):
            st = S.state
            i = st["cons"]
            assert S.slab_seq[i] == (nm_expect, s_expect), (S.slab_seq[i], nm_expect, s_expect)
            issue_loads(S, i + NSLOT - 2)
            st["cons"] += 1
            kc, ncol, ns = winfo[nm_expect]
            slot = i % NSLOT
            return S.wsl[slot][:, 0:kc * ncol].rearrange("p (kc n) -> p kc n", kc=kc), S.n(f"wsl{slot}")

        MMC = 0.14

        def rmsnorm_stats(S, Tt, nchunks, src, src_res, scale):
            n = S.n
            pst, rst = bankk(S, 3, Tt)
            for c in range(nchunks):
                k = S.rot["sq"]; S.rot["sq"] ^= 1
                A(lambda e, c=c, k=k: e.activation(S.sq[k][:, :Tt], src(c), AF.Square), [src_res(c)], [n(f"sq{k}")])
                mm(pst, ones[:], S.sq[k][:, :Tt], c == 0, c == nchunks - 1, ["ones", n(f"sq{k}")], [rst])
            V(lambda e: e.tensor_scalar(S.rstd[:, :Tt], pst, scale, EPS, ALU.mult, ALU.add), [rst], [n("rstd")])
            A(lambda e: e.activation(S.rstd[:, :Tt], S.rstd[:, :Tt], AF.Ln), [n("rstd")], [n("rstd")])
            A(lambda e: e.activation(S.rstd[:, :Tt], S.rstd[:, :Tt], AF.Exp, scale=-0.5), [n("rstd")], [n("rstd")])

        def norm_to_h(S, Tt, which):
            n = S.n
            rmsnorm_stats(S, Tt, 8, lambda c: S.xT[:, c, :Tt], lambda c: n(f"xT{c}"), 1.0 / D)
            for c in range(8):
                V(lambda e, c=c: e.scalar_tensor_tensor(S.hbuf[:, c, :Tt], S.xT[:, c, :Tt], gcol[which](c), S.rstd[:, :Tt],
                                                        ALU.mult, ALU.mult), [n(f"xT{c}"), "colsA", n("rstd")], [n(f"h{c}")])

        def gch(S, i, Tt):
            return S.gbuf[:, i * T:i * T + Tt]

        def ffn(S, Tt, f):
            n = S.n
            norm_to_h(S, Tt, f)
            yield 6.0
            hres = [n(f"h{c}") for c in range(8)]
            for s in range(11):
                w1, r1 = next_slab(S, f + "_w1", s)
                w3, r3 = next_slab(S, f + "_w3", s)
                for i2 in range(2):
                    i = 2 * s + i2
                    pa, ra = hb(S, "A", Tt); pb_, rb = hb(S, "B", Tt)
                    for k in range(8):
                        mm(pa, w1[:, k, i2 * 128:(i2 + 1) * 128], S.hbuf[:, k, :Tt], k == 0, k == 7, [r1] + hres, [ra])
                    for k in range(8):
                        mm(pb_, w3[:, k, i2 * 128:(i2 + 1) * 128], S.hbuf[:, k, :Tt], k == 0, k == 7, [r3] + hres, [rb])
                    kt = S.rot["ft"]; S.rot["ft"] ^= 1
                    A(lambda e, pa=pa, kt=kt: e.activation(S.ftmp[kt][:, :Tt], pa, AF.Silu), [ra], [n(f"ftmp{kt}")])
                    V(lambda e, pb_=pb_, kt=kt, i=i: e.tensor_tensor(gch(S, i, Tt), S.ftmp[kt][:, :Tt], pb_, ALU.mult),
                      [n(f"ftmp{kt}"), rb], [n(f"g{i}")])
                    yield 16 * MMC
            gall = [n(f"g{i}") for i in range(22)]
            for n_ in range(8):
                py, ry = hb(S, "Y", Tt)
                for kh in range(2):
                    w2, r2 = next_slab(S, f + "_w2", 2 * n_ + kh)
                    for k in range(11):
                        mm(py, w2[:, k, :], gch(S, kh * 11 + k, Tt), kh == 0 and k == 0, kh == 1 and k == 10, [r2] + gall, [ry])
                V(lambda e, py=py, n_=n_: e.scalar_tensor_tensor(S.xT[:, n_, :Tt], py, 0.5, S.xT[:, n_, :Tt],
                                                                 ALU.mult, ALU.add), [ry, n(f"xT{n_}")], [n(f"xT{n_}")])
                yield 22 * MMC

        def mixer(S, Tt):
            n = S.n
            norm_to_h(S, Tt, "mix")
            yield 6.0
            yield "lock"
            hres = [n(f"h{c}") for c in range(8)]
            gs = lambda n_: gch(S, n_, Tt)
            gc = lambda n_: gch(S, 8 + n_, Tt)
            cact = lambda c: gch(S, 16 + c, Tt)
            zz = lambda c: S.hbuf[:, c, :Tt]
            z2 = lambda c: S.hbuf[:, 4 + c, :Tt]
            u32, um, vp, sg = S.u32, S.um, S.vp, S.sg

            def proj_chunks(s, kind):
                w, rw = next_slab(S, "w_in", s)
                outs = []
                for i2 in range(2):
                    pa, ra = hb(S, kind, Tt)
                    for k in range(8):
                        mm(pa, w[:, k, i2 * 128:(i2 + 1) * 128], S.hbuf[:, k, :Tt], k == 0, k == 7, [rw] + hres, [ra])
                    outs.append((pa, ra))
                return outs

            for s in (0, 1):
                for i2, (pa, ra) in enumerate(proj_chunks(s, "A")):
                    c = 2 * s + i2
                    A(lambda e, pa=pa, c=c: e.copy(u32[:, c, :Tt], pa), [ra], [n(f"u32_{c}")])
                    G(lambda e, c=c: e.tensor_copy(um[:, c, :Tt], u32[:, c, :Tt]), [n(f"u32_{c}")], [n(f"um{c}")])
                yield 16 * MMC
            for s in (4, 5):
                for i2, (pa, ra) in enumerate(proj_chunks(s, "A")):
                    c = 2 * (s - 4) + i2
                    A(lambda e, pa=pa, c=c: e.activation(sg[:, c, :Tt], pa, AF.Sigmoid), [ra], [n(f"sg{c}")])
                yield 16 * MMC
            for s in (2, 3):
                for i2, (pa, ra) in enumerate(proj_chunks(s, "B")):
                    c = 2 * (s - 2) + i2
                    V(lambda e, pa=pa, c=c: e.tensor_tensor(vp[:, c, HALO:HALO + Tt], pa, sg[:, c, :Tt], ALU.mult),
                      [ra, n(f"sg{c}")], [n(f"vp{c}")])
                yield 16 * MMC
            for s in range(6, 14):
                for i2, (pa, ra) in enumerate(proj_chunks(s, "A")):
                    n_ = 2 * (s - 6) + i2
                    A(lambda e, pa=pa, n_=n_: e.activation(gch(S, n_, Tt), pa, AF.Sigmoid), [ra], [n(f"g{n_}")])
                yield 16 * MMC

            nblk = max(1, Tt // SBLK)
            L = min(SBLK, Tt)
            Sre, Sim, s5c = S.Sre, S.Sim, S.s5c
            tt1, tt2 = S.tt
            rtt = [n("tt0"), n("tt1")]
            pbr, rbr = bankk(S, 0, Tt)
            pbi, rbi = bankk(S, 2, Tt)

            def emit_bu(j):
                q = j // 4
                mm(pbr, WBr[:, j, :], um[:, q, :Tt], True, True, ["WBr", n(f"um{q}")], [rbr])
                mm(pbi, WBi[:, j, :], um[:, q, :Tt], True, True, ["WBi", n(f"um{q}")], [rbi])

            def emit_c(j):
                q, jj = j // 4, j % 4
                par = j % 2
                sr, si = S.srsi[par]
                py, ry = bankk(S, 1 + 2 * (q % 2), Tt)
                mm(py, WCr[:, j, :], sr[:, :Tt], jj == 0, False, ["WCr", n(f"s{par}_0")], [ry])
                mm(py, WCi[:, j, :], si[:, :Tt], False, jj == 3, ["WCi", n(f"s{par}_1")], [ry])
                if jj == 3:
                    k = q % 2
                    V(lambda e, q=q, py=py, k=k: e.scalar_tensor_tensor(S.yssm[k][:, :Tt], u32[:, q, :Tt], colsB[:, q, 0:1], py,
                                                                        ALU.mult, ALU.add), [n(f"u32_{q}"), "colsB", ry], [n(f"yssm{k}")])
                    A(lambda e, q=q, k=k: e.activation(zz(q), S.yssm[k][:, :Tt], AF.Gelu_apprx_tanh), [n(f"yssm{k}")], [n(f"h{q}")])

            v3 = lambda ap: ap[:, :Tt].rearrange("p (b l) -> p b l", l=L)

            def pair_ctx(j):
                par = j % 2
                c = {"j": j, "par": par}
                c["mr"], c["mi"] = S.mrmi[par]
                c["rr"], c["ri"] = S.rri[par]
                c["sr"], c["si"] = S.srsi[par]
                c["rm"] = [n(f"m{par}_0"), n(f"m{par}_1")]
                c["rrs"] = [n(f"r{par}_0"), n(f"r{par}_1")]
                c["rss"] = [n(f"s{par}_0"), n(f"s{par}_1")]
                c["ecb"] = Ec[:, j:j + 1, 0:L].to_broadcast([128, nblk, L])
                c["esb"] = Es[:, j:j + 1, 0:L].to_broadcast([128, nblk, L])
                c["ab"] = amag[:, j:j + 1].to_broadcast([128, L])
                c["ecl"] = Ec[:, j, L - 1:L]; c["esl"] = Es[:, j, L - 1:L]
                return c

            def mod_ops(c):
                rr, ri, mr, mi, ecb, esb, rrs, rm = c["rr"], c["ri"], c["mr"], c["mi"], c["ecb"], c["esb"], c["rrs"], c["rm"]
                return [
                    lambda: V(lambda e: e.tensor_tensor(v3(tt1), v3(rr), ecb, ALU.mult), [rrs[0], "E"], [rtt[0]]),
                    lambda: V(lambda e: e.tensor_tensor(v3(tt2), v3(ri), esb, ALU.mult), [rrs[1], "E"], [rtt[1]]),
                    lambda: V(lambda e: e.tensor_tensor(mr[:, :Tt], tt1[:, :Tt], tt2[:, :Tt], ALU.add), rtt, [rm[0]]),
                    lambda: V(lambda e: e.tensor_tensor(v3(tt1), v3(ri), ecb, ALU.mult), [rrs[1], "E"], [rtt[0]]),
                    lambda: V(lambda e: e.tensor_tensor(v3(tt2), v3(rr), esb, ALU.mult), [rrs[0], "E"], [rtt[1]]),
                    lambda: V(lambda e: e.tensor_tensor(mi[:, :Tt], tt1[:, :Tt], tt2[:, :Tt], ALU.subtract), rtt, [rm[1]]),
                ]

            def scan_ops(c):
                j, par = c["j"], c["par"]
                rr, ri, mr, mi, ab, ecl, esl, rrs, rm = c["rr"], c["ri"], c["mr"], c["mi"], c["ab"], c["ecl"], c["esl"], c["rrs"], c["rm"]
                sc0, sc1 = s5c[:, 2 * par:2 * par + 1], s5c[:, 2 * par + 1:2 * par + 2]
                rsc = n(f"s5c{par}")
                nesl = nEsL[:, j, (0 if L == SBLK else 1):(1 if L == SBLK else 2)]
                ops = []
                for b in range(nblk):
                    c0 = b * L
                    l1 = c0 + L - 1
                    ops += [
                        lambda c0=c0: V(lambda e: e.tensor_tensor_scan(rr[:, c0:c0 + L], ab, mr[:, c0:c0 + L], Sre[:, j:j + 1], ALU.mult, ALU.add),
                                        [rm[0], "amag", n(f"S{j}")], [rrs[0]]),
                        lambda c0=c0: V(lambda e: e.tensor_tensor_scan(ri[:, c0:c0 + L], ab, mi[:, c0:c0 + L], Sim[:, j:j + 1], ALU.mult, ALU.add),
                                        [rm[1], "amag", n(f"S{j}")], [rrs[1]]),
                        lambda l1=l1: A(lambda e: e.activation(sc0, ri[:, l1:l1 + 1], AF.Identity, scale=nesl), [rrs[1], "E2"], [rsc]),
                        lambda l1=l1: A(lambda e: e.activation(sc1, rr[:, l1:l1 + 1], AF.Identity, scale=esl), [rrs[0], "E"], [rsc]),
                        lambda l1=l1: A(lambda e: e.activation(Sre[:, j:j + 1], rr[:, l1:l1 + 1], AF.Identity, scale=ecl, bias=sc0),
                                        [rrs[0], "E", rsc], [n(f"S{j}")]),
                        lambda l1=l1: A(lambda e: e.activation(Sim[:, j:j + 1], ri[:, l1:l1 + 1], AF.Identity, scale=ecl, bias=sc1),
                                        [rrs[1], "E", rsc], [n(f"S{j}")]),
                    ]
                return ops

            def demod(c):
                p1, p2 = sh.pt
                p3, p4 = c["mr"], c["mi"]
                rp = ["ptmp0", "ptmp1"]
                rr, ri, sr, si, ecb, esb, rrs, rss, rm = c["rr"], c["ri"], c["sr"], c["si"], c["ecb"], c["esb"], c["rrs"], c["rss"], c["rm"]
                G(lambda e: e.tensor_tensor(v3(p1), v3(rr), ecb, ALU.mult), [rrs[0], "E"], [rp[0]])
                G(lambda e: e.tensor_tensor(v3(p2), v3(ri), esb, ALU.mult), [rrs[1], "E"], [rp[1]])
                G(lambda e: e.tensor_tensor(v3(p3), v3(ri), ecb, ALU.mult), [rrs[1], "E"], [rm[0]])
                G(lambda e: e.tensor_tensor(v3(p4), v3(rr), esb, ALU.mult), [rrs[0], "E"], [rm[1]])
                G(lambda e: e.tensor_tensor(sr[:, :Tt], p1[:, :Tt], p2[:, :Tt], ALU.subtract), rp, [rss[0]])
                G(lambda e: e.tensor_tensor(si[:, :Tt], p3[:, :Tt], p4[:, :Tt], ALU.add), rm, [rss[1]])

            emit_bu(0)
            prev = None
            for j in range(NPAIR + 1):
                cur = None
                mods = []
                if j < NPAIR:
                    cur = pair_ctx(j)
                    A(lambda e, o=cur["rr"]: e.copy(o[:, :Tt], pbr), [rbr], [cur["rrs"][0]])
                    A(lambda e, o=cur["ri"]: e.copy(o[:, :Tt], pbi), [rbi], [cur["rrs"][1]])
                    if j + 1 < NPAIR:
                        emit_bu(j + 1)
                    mods = mod_ops(cur)
                scans = scan_ops(prev) if prev is not None else []
                lead = min(4, len(scans)) if mods else len(scans)
                for op in scans[:lead]:
                    op()
                rest = scans[lead:]
                while mods or rest:
                    if mods:
                        mods.pop(0)()
                    if rest:
                        rest.pop(0)()
                if prev is not None:
                    if j >= 3:
                        emit_c(j - 3)
                    demod(prev)
                prev = cur
                yield 8.0
            emit_c(NPAIR - 2)
            emit_c(NPAIR - 1)
            yield 1.0

            zres = [n(f"h{c}") for c in range(4)]
            for s in range(2):
                w, rw = next_slab(S, "ssm_glu_w", s)
                for i2 in range(2):
                    n_ = 2 * s + i2
                    pa, ra = hb(S, "A", Tt)
                    for k in range(4):
                        mm(pa, w[:, k, i2 * 128:(i2 + 1) * 128], zz(k), k == 0, k == 3, [rw] + zres, [ra])
                    kt = S.rot["ft"]; S.rot["ft"] ^= 1
                    A(lambda e, pa=pa, kt=kt, n_=n_: e.activation(S.ftmp[kt][:, :Tt], pa, AF.Sigmoid, bias=colsB[:, n_, 1:2]),
                      [ra, "colsB"], [n(f"ftmp{kt}")])
                    V(lambda e, kt=kt, n_=n_: e.tensor_tensor(z2(n_), zz(n_), S.ftmp[kt][:, :Tt], ALU.mult),
                      [n(f"ftmp{kt}"), n(f"h{n_}")], [n(f"h{4 + n_}")])
                yield 8 * MMC + 0.5

            vpb = S.vpb
            for c in range(4):
                A(lambda e, c=c: e.copy(vpb[:, c, 0:HALO + Tt], vp[:, c, 0:HALO + Tt]), [n(f"vp{c}")], [n(f"vpb{c}")])
            for c in range(4):
                pa, ra = hb(S, "A", Tt)
                for jt in range(CW):
                    mm(pa, Dg[:, c, jt, :], vpb[:, c, jt:jt + Tt], jt == 0, jt == CW - 1, ["Dg", n(f"vpb{c}")], [ra])
                A(lambda e, c=c, pa=pa: e.activation(sg[:, c, :Tt], pa, AF.Identity, bias=colsB[:, c, 2:3]), [ra, "colsB"], [n(f"sg{c}")])
                yield CW * MMC
            psm, rsm = bankk(S, 1, Tt)
            pst, rst = bankk(S, 3, Tt)
            for c in range(4):
                A(lambda e, c=c: e.copy(um[:, c, :Tt], sg[:, c, :Tt]), [n(f"sg{c}")], [n(f"um{c}")])
                mm(psm, ones[:], um[:, c, :Tt], c == 0, c == 3, ["ones", n(f"um{c}")], [rsm])
            V(lambda e: e.tensor_scalar(S.mean[:, :Tt], psm, 1.0 / DS, None, ALU.mult), [rsm], [n("mean")])
            for c in range(4):
                k = S.rot["sq"]; S.rot["sq"] ^= 1
                A(lambda e, c=c, k=k: e.activation(S.sq[k][:, :Tt], sg[:, c, :Tt], AF.Square), [n(f"sg{c}")], [n(f"sq{k}")])
                mm(pst, ones[:], S.sq[k][:, :Tt], c == 0, c == 3, ["ones", n(f"sq{k}")], [rst])
            V(lambda e: e.tensor_tensor(S.msq[:, :Tt], S.mean[:, :Tt], S.mean[:, :Tt], ALU.mult), [n("mean")], [n("msq")])
            V(lambda e: e.scalar_tensor_tensor(S.rstd[:, :Tt], pst, 1.0 / DS, S.msq[:, :Tt], ALU.mult, ALU.subtract),
              [rst, n("msq")], [n("rstd")])
            V(lambda e: e.tensor_scalar(S.rstd[:, :Tt], S.rstd[:, :Tt], EPS, None, ALU.add), [n("rstd")], [n("rstd")])
            A(lambda e: e.activation(S.rstd[:, :Tt], S.rstd[:, :Tt], AF.Ln), [n("rstd")], [n("rstd")])
            A(lambda e: e.activation(S.rstd[:, :Tt], S.rstd[:, :Tt], AF.Exp, scale=-0.5), [n("rstd")], [n("rstd")])
            for c in range(4):
                V(lambda e, c=c: e.tensor_tensor(sg[:, c, :Tt], sg[:, c, :Tt], S.mean[:, :Tt], ALU.subtract), [n(f"sg{c}"), n("mean")], [n(f"sg{c}")])
                V(lambda e, c=c: e.tensor_tensor(sg[:, c, :Tt], sg[:, c, :Tt], S.rstd[:, :Tt], ALU.mult), [n(f"sg{c}"), n("rstd")], [n(f"sg{c}")])
                A(lambda e, c=c: e.activation(cact(c), sg[:, c, :Tt], AF.Silu, bias=colsB[:, c, 4:5], scale=colsB[:, c, 3:4]),
                  [n(f"sg{c}"), "colsB"], [n(f"g{16 + c}")])
            yield "unlock"
            yield 8.0
            z2res = [n(f"h{4 + c}") for c in range(4)]
            cres = [n(f"g{16 + c}") for c in range(4)]
            for s in range(4):
                ws, rws = next_slab(S, "ssm_out_w", s)
                wc, rwc = next_slab(S, "conv_out_w", s)
                for i2 in range(2):
                    n_ = 2 * s + i2
                    pa, ra = bankk(S, 1, Tt); pb_, rb = bankk(S, 3, Tt)
                    for k in range(4):
                        mm(pa, ws[:, k, i2 * 128:(i2 + 1) * 128], z2(k), k == 0, k == 3, [rws] + z2res, [ra])
                    for k in range(4):
                        mm(pb_, wc[:, k, i2 * 128:(i2 + 1) * 128], cact(k), k == 0, k == 3, [rwc] + cres, [rb])
                    V(lambda e, pa=pa, n_=n_: e.tensor_tensor(S.ftmp[0][:, :Tt], pa, gs(n_), ALU.mult), [ra, n(f"g{n_}")], [n("ftmp0")])
                    V(lambda e, pb_=pb_, n_=n_: e.tensor_tensor(S.ftmp[1][:, :Tt], pb_, gc(n_), ALU.mult), [rb, n(f"g{8 + n_}")], [n("ftmp1")])
                    V(lambda e, n_=n_: e.tensor_tensor(gs(n_), S.ftmp[0][:, :Tt], S.ftmp[1][:, :Tt], ALU.add), [n("ftmp0"), n("ftmp1")], [n(f"g{n_}")])
                yield 16 * MMC + 1.0
            mres = [n(f"g{n_}") for n_ in range(8)]
            for s in range(4):
                w, rw = next_slab(S, "w_o", s)
                for i2 in range(2):
                    n_ = 2 * s + i2
                    py, ry = hb(S, "Y", Tt)
                    for k in range(8):
                        mm(py, w[:, k, i2 * 128:(i2 + 1) * 128], gs(k), k == 0, k == 7, [rw] + mres, [ry])
                    V(lambda e, py=py, n_=n_: e.tensor_tensor(S.xT[:, n_, :Tt], S.xT[:, n_, :Tt], py, ALU.add), [ry, n(f"xT{n_}")], [n(f"xT{n_}")])
                yield 16 * MMC

        def emit_conv_state(S, Tt, dst_ap):
            n = S.n
            bank = S.pb[0]
            rb = [S.pbn[0]]
            for c in range(4):
                tr(bank[:HALO, c * 128:(c + 1) * 128], S.vp[:, c, Tt:Tt + HALO], 128, [n(f"vp{c}"), "ident"], rb)
            A(lambda e: e.copy(ncst[:, :], bank[:HALO, 0:512]), rb, ["ncst"])
            P.dma("scalar", dst_ap, ncst[:, :], reads=["ncst"], sem="o_nc")

        def emit_ssm_state(S, dst_re, dst_im):
            n = S.n
            bank = S.pb[0]
            rb = [S.pbn[0]]
            Sall = [n(f"S{j}") for j in range(16)]
            tr(bank[:16, 0:128], S.Sre[:, :], 128, Sall + ["ident"], rb)
            tr(bank[:16, 128:256], S.Sim[:, :], 128, Sall + ["ident"], rb)
            A(lambda e: e.copy(sst[:, :], bank[:16, 0:256]), rb, ["sst"])
            P.dma("scalar", dst_re.rearrange("(j gl) p -> j (gl p)", gl=2), sst[:, 0:128], reads=["sst"], sem="o_st")
            P.dma("scalar", dst_im.rearrange("(j gl) p -> j (gl p)", gl=2), sst[:, 128:256], reads=["sst"], sem="o_st")

        def load_x(Tt, src_ap):
            nb = (Tt + 127) // 128
            tb = min(128, Tt)
            P.dma("sync", xin[:tb, 0:nb * 1024].rearrange("p (b d) -> p b d", b=nb),
                  src_ap.rearrange("(b p) d -> p b d", p=tb), writes=["xin"], sem="xin")

        def transpose_in(S, Tt):
            n = S.n
            nb = (Tt + 127) // 128
            tb = min(128, Tt)
            for c in range(8):
                py, ry = hb(S, "A", Tt)
                for b in range(nb):
                    tr(py[:, b * tb:(b + 1) * tb], xin[:tb, b * 1024 + c * 128: b * 1024 + (c + 1) * 128], tb, ["xin", "ident"], [ry])
                A(lambda e, c=c, py=py: e.copy(S.xT[:, c, :Tt], py), [ry], [n(f"xT{c}")])

        def emit_output(S, Tt, dst_ap):
            n = S.n
            nb = (Tt + 127) // 128
            tb = min(128, Tt)
            rmsnorm_stats(S, Tt, 8, lambda c: S.xT[:, c, :Tt], lambda c: n(f"xT{c}"), 1.0 / D)
            for c in range(8):
                V(lambda e, c=c: e.scalar_tensor_tensor(S.xT[:, c, :Tt], S.xT[:, c, :Tt], gcol["final"](c), S.rstd[:, :Tt],
                                                        ALU.mult, ALU.mult), [n(f"xT{c}"), "colsA", n("rstd")], [n(f"xT{c}")])
            bank = S.pb[0]
            rb = [S.pbn[0]]
            for b in range(nb):
                slot = b % 2
                ysl = S.gb32[:tb, slot * 1024:(slot + 1) * 1024]
                yres = [n(f"g{i}") for i in range(slot * 8, slot * 8 + 9)]
                for hf in range(2):
                    for c4 in range(4):
                        c = hf * 4 + c4
                        tr(bank[:tb, c4 * 128:(c4 + 1) * 128], S.xT[:, c, b * tb:(b + 1) * tb], 128, [n(f"xT{c}"), "ident"], rb)
                    A(lambda e, hf=hf, ysl=ysl: e.copy(ysl[:, hf * 512:(hf + 1) * 512], bank[:tb, 0:512]), rb, yres)
                P.dma("scalar", dst_ap[b * tb:(b + 1) * tb, :], ysl, reads=yres, sem=f"o_y{S.sid}_{slot}")

        xin_state = {"owner": None}

        def stream_gen(S, tiles, tps):
            n = S.n
            S.prefetched = False
            for idx, t in enumerate(tiles):
                kind, sq_i, ti_ = t
                if kind == "p":
                    Tt = T
                    src = dr["xp"][sq_i, ti_ * T:(ti_ + 1) * T, :]
                else:
                    Tt = DSEQ
                    src = dr["xs"][:, :]
                first = ti_ == 0
                last = (kind == "s") or (ti_ == tps - 1)
                if not S.prefetched:
                    while xin_state["owner"] is not None:
                        yield 0.5
                    load_x(Tt, src)
                    xin_state["owner"] = S.sid
                transpose_in(S, Tt)
                xin_state["owner"] = None
                S.prefetched = False
                yield 4.0
                if first:
                    Sall = [n(f"S{j}") for j in range(16)]
                    if kind == "p":
                        V(lambda e: e.memset(S.Sre[:], 0.0), [], Sall)
                        V(lambda e: e.memset(S.Sim[:], 0.0), [], Sall)
                        for c in range(4):
                            G(lambda e, c=c: e.memset(S.vp[:, c, 0:HALO], 0.0), [], [n(f"vp{c}")])
                    else:
                        bank = S.pb[0]
                        rb = [S.pbn[0]]
                        for (srcS, dstS) in ((dr["sre"], S.Sre), (dr["sim"], S.Sim)):
                            P.dma("sync", sst[0:16, 0:128], srcS.rearrange("(j gl) p -> j (gl p)", gl=2), writes=["sst"], sem="su")
                            tr(bank[:, 0:16], sst[0:16, 0:128], 16, ["sst", "ident"], rb)
                            A(lambda e, dstS=dstS: e.copy(dstS[:], bank[:, 0:16]), rb, Sall)
                        P.dma("sync", ncst[:, :], dr["cc"], writes=["ncst"], sem="su")
                        for c in range(4):
                            tr(bank[:, c * HALO:(c + 1) * HALO], ncst[:, c * 128:(c + 1) * 128], HALO, ["ncst", "ident"], rb)
                        for c in range(4):
                            A(lambda e, c=c: e.copy(S.vp[:, c, 0:HALO], bank[:, c * HALO:(c + 1) * HALO]), rb, [n(f"vp{c}")])
                yield from ffn(S, Tt, "ffn1")
                yield from mixer(S, Tt)
                if last:
                    if kind == "p":
                        emit_conv_state(S, Tt, dr["ncp"][sq_i])
                        emit_ssm_state(S, dr["nrp"][sq_i], dr["nip"][sq_i])
                    else:
                        emit_conv_state(S, Tt, dr["ncs"])
                        emit_ssm_state(S, dr["nrs"], dr["nis"])
                else:
                    for c in range(4):
                        G(lambda e, c=c, Tt=Tt: e.tensor_copy(S.vp[:, c, 0:HALO], S.vp[:, c, Tt:Tt + HALO]), [n(f"vp{c}")], [n(f"vp{c}")])
                if idx + 1 < len(tiles) and xin_state["owner"] is None:
                    xin_state["owner"] = S.sid
                    S.prefetched = True
                    k2, s2, t2 = tiles[idx + 1]
                    if k2 == "p":
                        load_x(T, dr["xp"][s2, t2 * T:(t2 + 1) * T, :])
                    else:
                        load_x(DSEQ, dr["xs"][:, :])
                yield from ffn(S, Tt, "ffn2")
                if kind == "p":
                    emit_output(S, Tt, dr["yp"][sq_i, ti_ * T:(ti_ + 1) * T, :])
                else:
                    emit_output(S, Tt, dr["ys"])
                yield 4.0

        tps = seq_len // T
        streams = [make_stream(0), make_stream(1)]
        tile_lists = [[], []]
        for sq_i in range(n_prompt_seq):
            sid = 0 if sq_i < (n_prompt_seq + 1) // 2 else 1
            for ti_ in range(tps):
                tile_lists[sid].append(("p", sq_i, ti_))
        if do_sample:
            tile_lists[1].append(("s", 0, 0))
        gens = []
        for S, tl in zip(streams, tile_lists):
            for _ in tl:
                S.slab_seq.extend(tile_slabs())
            gens.append(stream_gen(S, tl, tps))
        clocks = [0.0, 230.0]
        alive = [len(tile_lists[0]) > 0, len(tile_lists[1]) > 0]
        waiting = [False, False]
        owner = None
        while any(alive):
            cands = [k for k in range(2) if alive[k] and not (waiting[k] and owner is not None)]
            i = min(cands, key=lambda k: clocks[k])
            if waiting[i]:
                waiting[i] = False
                owner = i
                clocks[i] = max(clocks[i], clocks[1 - i])
                continue
            try:
                r = next(gens[i])
            except StopIteration:
                alive[i] = False
                if owner == i:
                    owner = None
                continue
            if r == "lock":
                if owner is None:
                    owner = i
                else:
                    waiting[i] = True
            elif r == "unlock":
                owner = None
            else:
                clocks[i] += r
        P.wait_all("sync")
        P.build()
    return nc, P


_CACHE = {}


def kernel(**inputs):
    inp = {k: np.ascontiguousarray(np.asarray(v)) for k, v in inputs.items()}
    if "nc" not in _CACHE:
        _CACHE["nc"] = build_program()[0]
    nc = _CACHE["nc"]
    in_maps = []
    for c in range(NCORES):
        m = {
            "xp": inp["x_prompt"][c * BPC:(c + 1) * BPC],
            "xs": inp["x_sample"][c],
            "sre": inp["state_ssm_re"][0, c],
            "sim": inp["state_ssm_im"][0, c],
            "cc": inp["cache_conv"][0, c],
        }
        for nm, K, N in WEIGHTS:
            m[nm] = inp[nm][0]
        for nm, shp in SMALL:
            m[nm] = inp[nm] if nm == "final_norm" else inp[nm][0]
        in_maps.append({k: np.ascontiguousarray(v, dtype=np.float32) for k, v in m.items()})
    res = run_bass_kernel_spmd(nc, in_maps, core_ids=list(range(NCORES)))
    R = res.results
    cat = lambda k: np.concatenate([r[k] for r in R], axis=0)
    stk = lambda k: np.stack([r[k] for r in R], axis=0)
    y_prompt = cat("yp")
    y_sample = stk("ys")
    nrp = cat("nrp")[None]; nip = cat("nip")[None]; ncp = cat("ncp")[None]
    nrs = stk("nrs")[None]; nis = stk("nis")[None]; ncs = stk("ncs")[None]
    return (y_prompt, y_sample, nrp, nip, ncp, nrs, nis, ncs)
```

```python
import numpy as np
from contextlib import ExitStack
import concourse.bass as bass
import concourse.mybir as mybir
from concourse.bass_utils import run_bass_kernel_spmd

F32 = mybir.dt.float32
BF16 = mybir.dt.bfloat16
I32 = mybir.dt.int32
ALU = mybir.AluOpType
AF = mybir.ActivationFunctionType

D = 1024
DFF = 2816
NIN = 3584
DS = 512
NPAIR = 16
CW = 31
HALO = 30
SEQ = 2048
BPC = 4
NCORES = 8
DSEQ = 16
EPS = 1e-6
TWO_PI = 6.283185307179586
PI = 3.141592653589793

MODE = "bf16"
ENGS = ("sync", "scalar", "vector", "gpsimd", "tensor")
import os
SAME_ENGINE_SYNC = os.environ.get("KSES", "1") == "1"
DBG = os.environ.get("KDBG", "")


class StopBuild(Exception):
    pass


class Prog:
    def __init__(self, nc):
        self.nc = nc
        self.ops = {e: [] for e in ENGS}
        self.count = {}
        self.waited = {e: {} for e in ENGS}
        self.res = {}
        self.sems = {}
        self.ninst = {e: 0 for e in ENGS}

    def _need(self, eng, toks):
        need = {}
        for t in toks:
            if t is None:
                continue
            s, v, te = t
            if te == eng and s.startswith("e_"):
                if eng == "tensor" or not SAME_ENGINE_SYNC:
                    continue
            if need.get(s, 0) < v:
                need[s] = v
        for s, v in need.items():
            if self.waited[eng].get(s, 0) >= v:
                continue
            self.waited[eng][s] = v
            self.ops[eng].append(("wait", s, v))

    def emit(self, eng, fn, reads=(), writes=(), sem=None, inc=1):
        toks = []
        for r in reads:
            st = self.res.get(r)
            if st is not None:
                toks.append(st[0])
        for w in writes:
            st = self.res.get(w)
            if st is not None:
                toks.append(st[0])
                toks.extend(st[1])
        self._need(eng, toks)
        if sem is None:
            sem = "e_" + eng
        v = self.count.get(sem, 0) + inc
        self.count[sem] = v
        tok = (sem, v, eng)
        self.ops[eng].append(("op", fn, sem, inc))
        self.ninst[eng] += 1
        for r in reads:
            st = self.res.setdefault(r, [None, []])
            st[1].append(tok)
        for w in writes:
            self.res[w] = [tok, []]
        return tok

    def dma(self, eng, out, in_, reads=(), writes=(), sem=None, **kw):
        prev = self.count.get("d_" + sem, 0)
        if prev and self.waited[eng].get("d_" + sem, 0) < prev:
            self.waited[eng]["d_" + sem] = prev
            self.ops[eng].append(("wait", "d_" + sem, prev))
        return self.emit(eng, lambda e: e.dma_start(out=out, in_=in_, **kw),
                         reads=reads, writes=writes, sem="d_" + sem, inc=16)

    def wait_all(self, eng):
        for s, v in self.count.items():
            if self.waited[eng].get(s, 0) < v:
                self.waited[eng][s] = v
                self.ops[eng].append(("wait", s, v))

    def build(self):
        nc = self.nc
        with ExitStack() as es:
            for s in self.count:
                self.sems[s] = es.enter_context(nc.semaphore(s))
            block = es.enter_context(nc.Block())
            prog = self

            def runner(name):
                def run(e):
                    for op in prog.ops[name]:
                        if op[0] == "wait":
                            e.wait_ge(prog.sems[op[1]], op[2])
                        else:
                            _, fn, s, inc = op
                            fn(e).then_inc(prog.sems[s], inc)
                return run

            block.sync(runner("sync"))
            block.scalar(runner("scalar"))
            block.vector(runner("vector"))
            block.gpsimd(runner("gpsimd"))
            block.tensor(runner("tensor"))


WEIGHTS = [
    ("ffn1_w1", D, DFF), ("ffn1_w3", D, DFF), ("ffn1_w2", DFF, D),
    ("w_in", D, NIN), ("ssm_glu_w", DS, DS), ("ssm_out_w", DS, D), ("conv_out_w", DS, D),
    ("w_o", D, D),
    ("ffn2_w1", D, DFF), ("ffn2_w3", D, DFF), ("ffn2_w2", DFF, D),
]
WSLAB_NC = {"ffn1_w2": 128, "ffn2_w2": 128}
SMALL = [
    ("ffn1_norm", [D]), ("mix_norm", [D]), ("ffn2_norm", [D]), ("final_norm", [D]),
    ("ssm_d", [DS]), ("ssm_glu_b", [DS]), ("conv_dw_b", [DS]), ("conv_ln_g", [DS]), ("conv_ln_b", [DS]),
    ("conv_dw_w", [CW, DS]),
    ("ssm_lambda_re", [32, 64]), ("ssm_lambda_im", [32, 64]), ("ssm_log_step", [32]),
    ("ssm_b_re", [32, 64, 16]), ("ssm_b_im", [32, 64, 16]),
    ("ssm_c_re", [32, 16, 64]), ("ssm_c_im", [32, 16, 64]),
]


def build_program(n_prompt_seq=BPC, seq_len=SEQ, do_sample=True, mode=MODE, stop_after=None):
    bf = mode == "bf16"
    assert bf
    MDT = BF16
    T = 256
    SBLK = 128
    NSLOT = 4
    SLOT_EL = 2048
    NSTREAM = 2
    nc = bass.Bass("TRN2", target_bir_lowering=False)
    P = Prog(nc)
    dr = {}

    def din(name, shape, dt=F32):
        dr[name] = nc.dram_tensor(name, list(shape), dt, kind="ExternalInput").ap()
        return dr[name]

    def dout(name, shape, dt=F32):
        dr[name] = nc.dram_tensor(name, list(shape), dt, kind="ExternalOutput").ap()
        return dr[name]

    din("xp", [BPC, SEQ, D]); din("xs", [DSEQ, D])
    din("sre", [32, 64]); din("sim", [32, 64]); din("cc", [HALO, DS])
    for nm, K, N in WEIGHTS:
        din(nm, [K, N])
    for nm, shp in SMALL:
        din(nm, shp)
    dout("yp", [BPC, SEQ, D]); dout("ys", [DSEQ, D])
    dout("nrp", [BPC, 32, 64]); dout("nip", [BPC, 32, 64]); dout("ncp", [BPC, HALO, DS])
    dout("nrs", [32, 64]); dout("nis", [32, 64]); dout("ncs", [HALO, DS])

    winfo = {}
    for nm, K, N in WEIGHTS:
        if nm.endswith("_w2"):
            kc, ncol, ns = 11, 128, 16
        else:
            ncol = 256
            kc = K // 128
            ns = N // ncol
        winfo[nm] = (kc, ncol, ns)
        dr["scr_" + nm] = nc.dram_tensor("scr_" + nm, [ns, 128, kc * ncol], BF16, kind="Internal").ap()

    es = ExitStack()
    with es:
        def sb(name, shape, dt=F32):
            return es.enter_context(nc.sbuf_tensor(name, list(shape), dt))

        def psb(name):
            return es.enter_context(nc.psum_tensor(name, [128, 512], F32))

        ps = [psb(f"ps{i}") for i in range(8)]
        ident = sb("ident", [128, 128])
        ones = sb("ones", [128, 128], MDT)
        wslab = sb("wslab", [128, NSTREAM * NSLOT * SLOT_EL], MDT)
        wsl32 = wslab[:].bitcast(F32)
        Ec = sb("Ec", [128, NPAIR, SBLK]); Es = sb("Es", [128, NPAIR, SBLK])
        WBr = sb("WBr", [128, NPAIR, 128], MDT); WBi = sb("WBi", [128, NPAIR, 128], MDT)
        WCr = sb("WCr", [128, NPAIR, 128], MDT); WCi = sb("WCi", [128, NPAIR, 128], MDT)
        amag = sb("amag", [128, NPAIR])
        epscol = sb("epscol", [128, 1])
        nEsL = sb("nEsL", [128, NPAIR, 2])
        colsA = sb("colsA", [128, 8, 4])
        colsB = sb("colsB", [128, 4, 36])
        ncst = sb("ncst", [HALO, DS])
        sst = sb("sst", [16, 256])
        xin = sb("xinr", [128, 2048])
        identb = sb("identb", [128, 128], MDT)
        Dg = sb("Dg", [128, 4, CW, 128], MDT)
        ti = sb("ti", [128, 16], I32)

        def V(fn, r, w): return P.emit("vector", fn, reads=r, writes=w)
        def A(fn, r, w): return P.emit("scalar", fn, reads=r, writes=w)
        def G(fn, r, w): return P.emit("gpsimd", fn, reads=r, writes=w)
        def PE(fn, r, w): return P.emit("tensor", fn, reads=r, writes=w)

        def mm(out, lhsT, rhs, start, stop, r, w):
            return PE(lambda e: e.matmul(out, lhsT=lhsT, rhs=rhs, start=start, stop=stop), r, w)

        def tr(out, in_, n, r, w):
            return PE(lambda e: e.transpose(out, in_, ident[:n, :n]), r, w)

        xin_real = xin
        xin = wsl32[:, 0:3584]
        gb32 = wsl32[:, 3584:7168]
        gres = lambda i: f"g{i}"
        stg = gb32[0:36, 2560:3584]

        G(lambda e: e.memset(ident[:], 0.0), [], ["ident"])
        G(lambda e: e.affine_select(out=ident[:], in_=ident[:], pattern=[[-1, 128]], compare_op=ALU.not_equal,
                                    fill=1.0, base=0, channel_multiplier=1), ["ident"], ["ident"])
        V(lambda e: e.memset(ones[:], 1.0), [], ["ones"])
        V(lambda e: e.memset(epscol[:], EPS), [], ["epscol"])

        def wsrc(nm, s):
            kc, ncol, ns = winfo[nm]
            if nm.endswith("_w2"):
                n_, kh = s // 2, s % 2
                return dr[nm][kh * 1408:(kh + 1) * 1408, n_ * 128:(n_ + 1) * 128].rearrange("(kc p) n -> p kc n", p=128)
            return dr[nm][:, s * ncol:(s + 1) * ncol].rearrange("(kc p) n -> p kc n", p=128)

        if bf:
            for nm, K, N in WEIGHTS:
                kc, ncol, ns = winfo[nm]
                for s in range(ns):
                    P.dma("gpsimd", dr["scr_" + nm][s].rearrange("p (kc n) -> p kc n", kc=kc), wsrc(nm, s),
                          writes=[f"scr_{nm}_{s}"], sem=f"cast{s % 4}")

        STG = [f"g{i}" for i in range(22)]
        for i, nm in enumerate(["ffn1_norm", "mix_norm", "ffn2_norm", "final_norm"]):
            P.dma("sync", stg[i:i + 1, :], dr[nm].rearrange("(o d) -> o d", o=1), writes=STG, sem="su")
        for c in range(8):
            tr(ps[7][:, c * 4:(c + 1) * 4], stg[0:4, c * 128:(c + 1) * 128], 4, STG + ["ident"], ["ps7"])
        V(lambda e: e.tensor_copy(colsA[:].rearrange("p c k -> p (c k)"), ps[7][:, 0:32]), ["ps7"], ["colsA"])
        for i, nm in enumerate(["ssm_d", "ssm_glu_b", "conv_dw_b", "conv_ln_g", "conv_ln_b"]):
            P.dma("sync", stg[i:i + 1, 0:DS], dr[nm].rearrange("(o d) -> o d", o=1), reads=[], writes=STG, sem="su")
        P.dma("sync", stg[5:36, 0:DS], dr["conv_dw_w"], writes=STG, sem="su")
        for c in range(4):
            tr(ps[7][:, c * 36:(c + 1) * 36], stg[0:36, c * 128:(c + 1) * 128], 36, STG + ["ident"], ["ps7"])
        V(lambda e: e.tensor_copy(colsB[:].rearrange("p c k -> p (c k)"), ps[7][:, 0:144]), ["ps7"], ["colsB"])
        gcol = {"ffn1": lambda c: colsA[:, c, 0:1], "mix": lambda c: colsA[:, c, 1:2],
                "ffn2": lambda c: colsA[:, c, 2:3], "final": lambda c: colsA[:, c, 3:4]}
        dcol = lambda c: colsB[:, c, 0:1]
        glubcol = lambda c: colsB[:, c, 1:2]
        dwbcol = lambda c: colsB[:, c, 2:3]
        lngcol = lambda c: colsB[:, c, 3:4]
        lnbcol = lambda c: colsB[:, c, 4:5]
        tapcol = lambda c, j: colsB[:, c, 5 + j:6 + j]

        xs32 = xin
        LR = xs32[:, 0:16]; LI = xs32[:, 16:32]; DT = xs32[:, 32:48]; TH = xs32[:, 48:64]
        AR = xs32[:, 64:80]; AI = xs32[:, 80:96]; KR = xs32[:, 96:112]; KI = xs32[:, 112:128]
        t0 = xs32[:, 128:144]; t1 = xs32[:, 144:160]; t2 = xs32[:, 160:176]; t3 = xs32[:, 176:192]
        SN = xs32[:, 192:208]; CS = xs32[:, 208:224]
        XR = "xin"

        def loadT16(dst, src_ap):
            P.dma("sync", sst[0:16, 0:128], src_ap, writes=["sst"], sem="su")
            tr(ps[7][:, 0:16], sst[0:16, 0:128], 16, ["sst", "ident"], ["ps7"])
            V(lambda e: e.tensor_copy(dst, ps[7][:, 0:16]), ["ps7"], [XR])

        loadT16(LR, dr["ssm_lambda_re"].rearrange("(j gl) p -> j (gl p)", gl=2))
        loadT16(LI, dr["ssm_lambda_im"].rearrange("(j gl) p -> j (gl p)", gl=2))
        lsh = dr["ssm_log_step"].tensor
        for gl in range(2):
            P.dma("sync", xs32[gl * 64:(gl + 1) * 64, 32:48], bass.AP(tensor=lsh, offset=gl, ap=[[0, 64], [2, 16]]),
                  writes=[XR], sem="su", allow_slow_non_contiguous=True)
        A(lambda e: e.activation(DT, DT, AF.Exp), [XR], [XR])
        V(lambda e: e.tensor_tensor(t0, LR, DT, ALU.mult), [XR], [XR])
        A(lambda e: e.activation(amag[:], t0, AF.Exp), [XR], ["amag"])
        V(lambda e: e.tensor_tensor(TH, LI, DT, ALU.mult), [XR], [XR])

        V(lambda e: e.tensor_scalar(t1, TH, 1.0 / TWO_PI, None, ALU.mult), [XR], [XR])
        V(lambda e: e.tensor_copy(ti[:], t1), [XR], ["ti"])
        V(lambda e: e.tensor_copy(t2, ti[:]), ["ti"], [XR])
        V(lambda e: e.scalar_tensor_tensor(t1, t2, -TWO_PI, TH, ALU.mult, ALU.add), [XR], [XR])
        V(lambda e: e.tensor_scalar(t2, t1, PI, TWO_PI, ALU.is_gt, ALU.mult), [XR], [XR])
        V(lambda e: e.tensor_tensor(t1, t1, t2, ALU.subtract), [XR], [XR])
        V(lambda e: e.tensor_scalar(t2, t1, -PI, TWO_PI, ALU.is_lt, ALU.mult), [XR], [XR])
        V(lambda e: e.tensor_tensor(t1, t1, t2, ALU.add), [XR], [XR])
        V(lambda e: e.tensor_scalar(t1, t1, 0.25, None, ALU.mult), [XR], [XR])
        V(lambda e: e.tensor_scalar(t2, t1, PI / 2, None, ALU.add), [XR], [XR])
        A(lambda e: e.activation(SN, t1, AF.Sin), [XR], [XR])
        A(lambda e: e.activation(CS, t2, AF.Sin), [XR], [XR])
        for _ in range(2):
            V(lambda e: e.tensor_tensor(t1, CS, CS, ALU.mult), [XR], [XR])
            V(lambda e: e.tensor_tensor(t2, SN, SN, ALU.mult), [XR], [XR])
            V(lambda e: e.tensor_tensor(t3, CS, SN, ALU.mult), [XR], [XR])
            V(lambda e: e.tensor_tensor(CS, t1, t2, ALU.subtract), [XR], [XR])
            V(lambda e: e.tensor_scalar(SN, t3, 2.0, None, ALU.mult), [XR], [XR])
        V(lambda e: e.tensor_tensor(AR, amag[:], CS, ALU.mult), [XR, "amag"], [XR])
        V(lambda e: e.tensor_tensor(AI, amag[:], SN, ALU.mult), [XR, "amag"], [XR])
        V(lambda e: e.tensor_scalar(t0, AR, -1.0, None, ALU.add), [XR], [XR])
        V(lambda e: e.tensor_tensor(t1, LR, LR, ALU.mult), [XR], [XR])
        V(lambda e: e.tensor_tensor(t2, LI, LI, ALU.mult), [XR], [XR])
        V(lambda e: e.tensor_tensor(t1, t1, t2, ALU.add), [XR], [XR])
        V(lambda e: e.reciprocal(t1, t1), [XR], [XR])
        V(lambda e: e.tensor_tensor(t2, t0, LR, ALU.mult), [XR], [XR])
        V(lambda e: e.tensor_tensor(t3, AI, LI, ALU.mult), [XR], [XR])
        V(lambda e: e.tensor_tensor(t2, t2, t3, ALU.add), [XR], [XR])
        V(lambda e: e.tensor_tensor(KR, t2, t1, ALU.mult), [XR], [XR])
        V(lambda e: e.tensor_tensor(t2, AI, LR, ALU.mult), [XR], [XR])
        V(lambda e: e.tensor_tensor(t3, t0, LI, ALU.mult), [XR], [XR])
        V(lambda e: e.tensor_tensor(t2, t2, t3, ALU.subtract), [XR], [XR])
        V(lambda e: e.tensor_tensor(KI, t2, t1, ALU.mult), [XR], [XR])
        V(lambda e: e.tensor_copy(Ec[:, :, 0:1], CS.unsqueeze(2)), [XR], ["E"])
        V(lambda e: e.tensor_copy(Es[:, :, 0:1], SN.unsqueeze(2)), [XR], ["E"])
        TA = xs32[:, 1024:1024 + 16 * 64].rearrange("p (j m) -> p j m", j=16)
        TB = xs32[:, 2048:2048 + 16 * 64].rearrange("p (j m) -> p j m", j=16)
        n = 1
        while n < SBLK:
            cn = Ec[:, :, n - 1:n].to_broadcast([128, NPAIR, n])
            sn = Es[:, :, n - 1:n].to_broadcast([128, NPAIR, n])
            c_lo = Ec[:, :, 0:n]; s_lo = Es[:, :, 0:n]
            c_hi = Ec[:, :, n:2 * n]; s_hi = Es[:, :, n:2 * n]
            ta = TA[:, :, 0:n]; tb = TB[:, :, 0:n]
            V(lambda e, a=ta, x=c_lo, y=cn: e.tensor_tensor(a, x, y, ALU.mult), ["E"], [XR])
            V(lambda e, a=tb, x=s_lo, y=sn: e.tensor_tensor(a, x, y, ALU.mult), ["E"], [XR])
            V(lambda e, o=c_hi, a=ta, b=tb: e.tensor_tensor(o, a, b, ALU.subtract), [XR, "E"], ["E"])
            V(lambda e, a=ta, x=s_lo, y=cn: e.tensor_tensor(a, x, y, ALU.mult), ["E"], [XR])
            V(lambda e, a=tb, x=c_lo, y=sn: e.tensor_tensor(a, x, y, ALU.mult), ["E"], [XR])
            V(lambda e, o=s_hi, a=ta, b=tb: e.tensor_tensor(o, a, b, ALU.add), [XR, "E"], ["E"])
            n *= 2
        BR = xs32[:, 256:512].rearrange("p (j c) -> p j c", j=16)
        BI = xs32[:, 512:768].rearrange("p (j c) -> p j c", j=16)
        BBR = xs32[:, 768:1024].rearrange("p (j c) -> p j c", j=16)
        BBI = xs32[:, 3072:3328].rearrange("p (j c) -> p j c", j=16)
        P.dma("sync", BR, dr["ssm_b_re"].rearrange("(j gl) p c -> (gl p) j c", gl=2), writes=[XR], sem="su")
        P.dma("sync", BI, dr["ssm_b_im"].rearrange("(j gl) p c -> (gl p) j c", gl=2), writes=[XR], sem="su")
        krb = KR.unsqueeze(2).to_broadcast([128, 16, 16]); kib = KI.unsqueeze(2).to_broadcast([128, 16, 16])
        ta = TA[:, :, 0:16]; tb = TB[:, :, 0:16]
        V(lambda e: e.tensor_tensor(ta, BR, krb, ALU.mult), [XR], [XR])
        V(lambda e: e.tensor_tensor(tb, BI, kib, ALU.mult), [XR], [XR])
        V(lambda e: e.tensor_tensor(BBR, ta, tb, ALU.subtract), [XR], [XR])
        V(lambda e: e.tensor_tensor(ta, BI, krb, ALU.mult), [XR], [XR])
        V(lambda e: e.tensor_tensor(tb, BR, kib, ALU.mult), [XR], [XR])
        V(lambda e: e.tensor_tensor(BBI, ta, tb, ALU.add), [XR], [XR])
        Z = gb32[:, 0:2048].rearrange("p (j r) -> p j r", j=16)
        ZR = [gres(i) for i in range(22)]
        for (BB, WB, nm) in ((BBR, WBr, "WBr"), (BBI, WBi, "WBi")):
            V(lambda e: e.memset(Z, 0.0), [], ZR)
            for jj in range(4):
                for gl in range(2):
                    lo = gl * 64
                    V(lambda e, jj=jj, gl=gl, lo=lo, BB=BB: e.tensor_copy(
                        Z[lo:lo + 64, jj::4, 32 * jj + 16 * gl:32 * jj + 16 * gl + 16], BB[lo:lo + 64, jj::4, :]),
                      [XR], ZR)
            for j in range(16):
                pb = ps[j % 2]
                tr(pb[:, 0:128], Z[:, j, :], 128, ZR + ["ident"], [f"ps{j % 2}"])
                V(lambda e, j=j, pb=pb, WB=WB: e.tensor_copy(WB[:, j, :], pb[:, 0:128]), [f"ps{j % 2}"], [nm])
        CX = gb32[:, 2048:2048 + 512].rearrange("p (q m) -> p q m", q=4)
        for (cname, WC, nm, sgn) in (("ssm_c_re", WCr, "WCr", 1.0), ("ssm_c_im", WCi, "WCi", -1.0)):
            csrc = dr[cname].rearrange("(q g) c p -> (g c) q p", q=4)
            P.dma("sync", CX[:, :, 0:64], csrc, writes=ZR, sem="su")
            P.dma("sync", CX[:, :, 64:128], csrc, writes=[gres(21)], sem="su")
            V(lambda e, WC=WC: e.memset(WC[:], 0.0), [], [nm])
            for q in range(4):
                pb = ps[q % 2]
                tr(pb[:, 0:128], CX[:, q, :], 128, ZR + ["ident"], [f"ps{q % 2}"])
                for jj in range(4):
                    j = 4 * q + jj
                    for gl in range(2):
                        lo = gl * 64
                        c0 = 32 * jj + 16 * gl
                        V(lambda e, WC=WC, j=j, lo=lo, c0=c0, pb=pb, sgn=sgn: e.tensor_scalar(
                            WC[lo:lo + 64, j, c0:c0 + 16], pb[lo:lo + 64, c0:c0 + 16], sgn, None, ALU.mult),
                          [f"ps{q % 2}"], [nm])

        V(lambda e: e.tensor_copy(identb[:], ident[:]), ["ident"], ["identb"])
        for c in range(4):
            for jt in range(CW):
                V(lambda e, c=c, jt=jt: e.tensor_scalar(Dg[:, c, jt, :], identb[:], colsB[:, c, 5 + jt:6 + jt], None, ALU.mult),
                  ["identb", "colsB"], ["Dg"])
        V(lambda e: e.tensor_scalar(nEsL[:, :, 0], Es[:, :, SBLK - 1], -1.0, None, ALU.mult), ["E"], ["E2"])
        V(lambda e: e.tensor_scalar(nEsL[:, :, 1], Es[:, :, DSEQ - 1], -1.0, None, ALU.mult), ["E"], ["E2"])
        slot_names = [f"{sid}:wsl{k}" for sid in range(NSTREAM) for k in range(NSLOT)]
        V(lambda e: e.memset(ti[:, 0:1], 0), [], ["xin", "ti"] + [f"g{i}" for i in range(22)] + slot_names)
        xin = xin_real

        gcol = {"ffn1": lambda c: colsA[:, c, 0:1], "mix": lambda c: colsA[:, c, 1:2],
                "ffn2": lambda c: colsA[:, c, 2:3], "final": lambda c: colsA[:, c, 3:4]}

        def tile_slabs():
            L = []
            def ffn(f):
                r = []
                for s in range(11):
                    r.append((f + "_w1", s)); r.append((f + "_w3", s))
                for s in range(16):
                    r.append((f + "_w2", s))
                return r
            L += ffn("ffn1")
            for s in (0, 1, 4, 5, 2, 3, 6, 7, 8, 9, 10, 11, 12, 13):
                L.append(("w_in", s))
            L += [("ssm_glu_w", 0), ("ssm_glu_w", 1)]
            for s in range(4):
                L.append(("ssm_out_w", s)); L.append(("conv_out_w", s))
            for s in range(4):
                L.append(("w_o", s))
            L += ffn("ffn2")
            return L

        class Stream:
            pass

        SHARED_PREFIX = ("u32_", "um", "sg", "tt", "m0_", "m1_", "r0_", "r1_", "s0_", "s1_", "mean", "msq", "yssm")
        sh = Stream()
        sh.tt = [sb(f"tt{i}", [128, T]) for i in range(2)]
        sh.mrmi = [[sb(f"m{b}_{i}", [128, T]) for i in range(2)] for b in range(2)]
        sh.rri = [[sb(f"r{b}_{i}", [128, T]) for i in range(2)] for b in range(2)]
        sh.srsi = [[sb(f"s{b}_{i}", [128, T], MDT) for i in range(2)] for b in range(2)]
        sh.u32 = sb("u32", [128, 4, T]); sh.um = sb("um", [128, 4, T], MDT)
        sh.sg = sb("sg", [128, 4, T])
        sh.mean = sb("mean", [128, T]); sh.msq = sb("msq", [128, T])
        sh.yssm = [sb(f"yssm{i}", [128, T]) for i in range(2)]
        sh.pt = [sb(f"ptmp{i}", [128, T]) for i in range(2)]

        def make_stream(sid):
            S = Stream()
            S.sid = sid
            n = lambda x: x if x.startswith(SHARED_PREFIX) else f"{sid}:{x}"
            S.n = n
            pre = f"s{sid}_"
            S.xT = sb(pre + "xT", [128, 8, T])
            S.hbuf = sb(pre + "hbuf", [128, 8, T], MDT)
            S.gbuf = sb(pre + "gbuf", [128, 22 * T], MDT)
            S.gb32 = S.gbuf[:].bitcast(F32)
            S.Sre = sb(pre + "Sre", [128, NPAIR]); S.Sim = sb(pre + "Sim", [128, NPAIR])
            S.tt = sh.tt; S.mrmi = sh.mrmi; S.rri = sh.rri; S.srsi = sh.srsi
            S.s5c = sb(pre + "s5c", [128, 4])
            S.u32 = sh.u32; S.um = sh.um
            S.vp = sb(pre + "vp", [128, 4, HALO + T])
            S.vpb = sb(pre + "vpb", [128, 4, HALO + T], MDT)
            S.sg = sh.sg
            S.rstd = sb(pre + "rstd", [128, T]); S.mean = sh.mean; S.msq = sh.msq
            S.sq = [sb(pre + f"sq{i}", [128, T], MDT) for i in range(2)]
            S.ftmp = [sb(pre + f"ftmp{i}", [128, T]) for i in range(2)]
            S.yssm = sh.yssm
            S.wsl = [wslab[:, (sid * NSLOT + k) * SLOT_EL:(sid * NSLOT + k + 1) * SLOT_EL] for k in range(NSLOT)]
            S.pb = ps[4 * sid:4 * sid + 4]
            S.pbn = [f"ps{4 * sid + k}" for k in range(4)]
            S.rot = {"A": 0, "B": 0, "Y": 0, "sq": 0, "ft": 0}
            S.slab_seq = []
            S.state = {"next_load": 0, "cons": 0}
            return S

        def bankk(S, k, Tt=T):
            return S.pb[k][:, 0:Tt], S.pbn[k]

        def hb(S, kind, Tt=T):
            if kind == "A":
                k = 0 + 2 * S.rot["A"]
                S.rot["A"] ^= 1
            else:
                k = 1 + 2 * S.rot["B"]
                S.rot["B"] ^= 1
            return bankk(S, k, Tt)

        def issue_loads(S, upto):
            st = S.state
            while st["next_load"] <= min(upto, len(S.slab_seq) - 1):
                i = st["next_load"]
                nm, s = S.slab_seq[i]
                kc, ncol, ns = winfo[nm]
                slot = i % NSLOT
                dst = S.wsl[slot][:, 0:kc * ncol].rearrange("p (kc n) -> p kc n", kc=kc)
                src = dr["scr_" + nm][s].rearrange("p (kc n) -> p kc n", kc=kc)
                P.dma("sync", dst, src, reads=[f"scr_{nm}_{s}"], writes=[S.n(f"wsl{slot}")], sem=f"w{S.sid}_{slot}")
                st["next_load"] += 1

        def next_slab(S, nm_expect, s_expect):
            st = S.state
            i = st["cons"]
            assert S.slab_seq[i] == (nm_expect, s_expect), (S.slab_seq[i], nm_expect, s_expect)
            issue_loads(S, i + NSLOT - 2)
            st["cons"] += 1
            kc, ncol, ns = winfo[nm_expect]
            slot = i % NSLOT
            return S.wsl[slot][:, 0:kc * ncol].rearrange("p (kc n) -> p kc n", kc=kc), S.n(f"wsl{slot}")

        MMC = 0.14

        def rmsnorm_stats(S, Tt, nchunks, src, src_res, scale):
            n = S.n
            pst, rst = bankk(S, 3, Tt)
            for c in range(nchunks):
                k = S.rot["sq"]; S.rot["sq"] ^= 1
                A(lambda e, c=c, k=k: e.activation(S.sq[k][:, :Tt], src(c), AF.Square), [src_res(c)], [n(f"sq{k}")])
                mm(pst, ones[:], S.sq[k][:, :Tt], c == 0, c == nchunks - 1, ["ones", n(f"sq{k}")], [rst])
            A(lambda e: e.activation(S.rstd[:, :Tt], pst, AF.Ln, bias=epscol[:, 0:1], scale=scale), [rst, "epscol"], [n("rstd")])
            A(lambda e: e.activation(S.rstd[:, :Tt], S.rstd[:, :Tt], AF.Exp, scale=-0.5), [n("rstd")], [n("rstd")])

        def norm_to_h(S, Tt, which):
            n = S.n
            rmsnorm_stats(S, Tt, 8, lambda c: S.xT[:, c, :Tt], lambda c: n(f"xT{c}"), 1.0 / D)
            for c in range(8):
                V(lambda e, c=c: e.scalar_tensor_tensor(S.hbuf[:, c, :Tt], S.xT[:, c, :Tt], gcol[which](c), S.rstd[:, :Tt],
                                                        ALU.mult, ALU.mult), [n(f"xT{c}"), "colsA", n("rstd")], [n(f"h{c}")])

        def gch(S, i, Tt):
            return S.gbuf[:, i * T:i * T + Tt]

        def ffn(S, Tt, f):
            n = S.n
            norm_to_h(S, Tt, f)
            yield 6.0
            hres = [n(f"h{c}") for c in range(8)]
            for s in range(11):
                w1, r1 = next_slab(S, f + "_w1", s)
                w3, r3 = next_slab(S, f + "_w3", s)
                for i2 in range(2):
                    i = 2 * s + i2
                    pa, ra = hb(S, "A", Tt); pb_, rb = hb(S, "B", Tt)
                    for k in range(8):
                        mm(pa, w1[:, k, i2 * 128:(i2 + 1) * 128], S.hbuf[:, k, :Tt], k == 0, k == 7, [r1] + hres, [ra])
                    for k in range(8):
                        mm(pb_, w3[:, k, i2 * 128:(i2 + 1) * 128], S.hbuf[:, k, :Tt], k == 0, k == 7, [r3] + hres, [rb])
                    kt = S.rot["ft"]; S.rot["ft"] ^= 1
                    A(lambda e, pa=pa, kt=kt: e.activation(S.ftmp[kt][:, :Tt], pa, AF.Silu), [ra], [n(f"ftmp{kt}")])
                    V(lambda e, pb_=pb_, kt=kt, i=i: e.tensor_tensor(gch(S, i, Tt), S.ftmp[kt][:, :Tt], pb_, ALU.mult),
                      [n(f"ftmp{kt}"), rb], [n(f"g{i}")])
                    yield 16 * MMC
            gall = [n(f"g{i}") for i in range(22)]
            for n_ in range(8):
                py, ry = hb(S, "Y", Tt)
                for kh in range(2):
                    w2, r2 = next_slab(S, f + "_w2", 2 * n_ + kh)
                    for k in range(11):
                        mm(py, w2[:, k, :], gch(S, kh * 11 + k, Tt), kh == 0 and k == 0, kh == 1 and k == 10, [r2] + gall, [ry])
                V(lambda e, py=py, n_=n_: e.scalar_tensor_tensor(S.xT[:, n_, :Tt], py, 0.5, S.xT[:, n_, :Tt],
                                                                 ALU.mult, ALU.add), [ry, n(f"xT{n_}")], [n(f"xT{n_}")])
                yield 22 * MMC

        def mixer(S, Tt):
            n = S.n
            norm_to_h(S, Tt, "mix")
            yield 6.0
            yield "lock"
            hres = [n(f"h{c}") for c in range(8)]
            gs = lambda n_: gch(S, n_, Tt)
            gc = lambda n_: gch(S, 8 + n_, Tt)
            cact = lambda c: gch(S, 16 + c, Tt)
            zz = lambda c: S.hbuf[:, c, :Tt]
            z2 = lambda c: S.hbuf[:, 4 + c, :Tt]
            u32, um, vp, sg = S.u32, S.um, S.vp, S.sg

            def proj_chunks(s, kind):
                w, rw = next_slab(S, "w_in", s)
                outs = []
                for i2 in range(2):
                    pa, ra = hb(S, kind, Tt)
                    for k in range(8):
                        mm(pa, w[:, k, i2 * 128:(i2 + 1) * 128], S.hbuf[:, k, :Tt], k == 0, k == 7, [rw] + hres, [ra])
                    outs.append((pa, ra))
                return outs

            for s in (0, 1):
                for i2, (pa, ra) in enumerate(proj_chunks(s, "A")):
                    c = 2 * s + i2
                    A(lambda e, pa=pa, c=c: e.copy(u32[:, c, :Tt], pa), [ra], [n(f"u32_{c}")])
                    G(lambda e, c=c: e.tensor_copy(um[:, c, :Tt], u32[:, c, :Tt]), [n(f"u32_{c}")], [n(f"um{c}")])
                yield 16 * MMC
            for s in (4, 5):
                for i2, (pa, ra) in enumerate(proj_chunks(s, "A")):
                    c = 2 * (s - 4) + i2
                    A(lambda e, pa=pa, c=c: e.activation(sg[:, c, :Tt], pa, AF.Sigmoid), [ra], [n(f"sg{c}")])
                yield 16 * MMC
            for s in (2, 3):
                for i2, (pa, ra) in enumerate(proj_chunks(s, "B")):
                    c = 2 * (s - 2) + i2
                    V(lambda e, pa=pa, c=c: e.tensor_tensor(vp[:, c, HALO:HALO + Tt], pa, sg[:, c, :Tt], ALU.mult),
                      [ra, n(f"sg{c}")], [n(f"vp{c}")])
                yield 16 * MMC
            for s in range(6, 14):
                for i2, (pa, ra) in enumerate(proj_chunks(s, "A")):
                    n_ = 2 * (s - 6) + i2
                    A(lambda e, pa=pa, n_=n_: e.activation(gch(S, n_, Tt), pa, AF.Sigmoid), [ra], [n(f"g{n_}")])
                yield 16 * MMC

            nblk = max(1, Tt // SBLK)
            L = min(SBLK, Tt)
            Sre, Sim, s5c = S.Sre, S.Sim, S.s5c
            tt1, tt2 = S.tt
            rtt = [n("tt0"), n("tt1")]
            pbr, rbr = bankk(S, 0, Tt)
            pbi, rbi = bankk(S, 2, Tt)

            def emit_bu(j):
                q = j // 4
                mm(pbr, WBr[:, j, :], um[:, q, :Tt], True, True, ["WBr", n(f"um{q}")], [rbr])
                mm(pbi, WBi[:, j, :], um[:, q, :Tt], True, True, ["WBi", n(f"um{q}")], [rbi])

            def emit_c(j):
                q, jj = j // 4, j % 4
                par = j % 2
                sr, si = S.srsi[par]
                py, ry = bankk(S, 1 + 2 * (q % 2), Tt)
                mm(py, WCr[:, j, :], sr[:, :Tt], jj == 0, False, ["WCr", n(f"s{par}_0")], [ry])
                mm(py, WCi[:, j, :], si[:, :Tt], False, jj == 3, ["WCi", n(f"s{par}_1")], [ry])
                if jj == 3:
                    k = q % 2
                    V(lambda e, q=q, py=py, k=k: e.scalar_tensor_tensor(S.yssm[k][:, :Tt], u32[:, q, :Tt], colsB[:, q, 0:1], py,
                                                                        ALU.mult, ALU.add), [n(f"u32_{q}"), "colsB", ry], [n(f"yssm{k}")])
                    A(lambda e, q=q, k=k: e.activation(zz(q), S.yssm[k][:, :Tt], AF.Gelu_apprx_tanh), [n(f"yssm{k}")], [n(f"h{q}")])

            v3 = lambda ap: ap[:, :Tt].rearrange("p (b l) -> p b l", l=L)

            def pair_ctx(j):
                par = j % 2
                c = {"j": j, "par": par}
                c["mr"], c["mi"] = S.mrmi[par]
                c["rr"], c["ri"] = S.rri[par]
                c["sr"], c["si"] = S.srsi[par]
                c["rm"] = [n(f"m{par}_0"), n(f"m{par}_1")]
                c["rrs"] = [n(f"r{par}_0"), n(f"r{par}_1")]
                c["rss"] = [n(f"s{par}_0"), n(f"s{par}_1")]
                c["ecb"] = Ec[:, j:j + 1, 0:L].to_broadcast([128, nblk, L])
                c["esb"] = Es[:, j:j + 1, 0:L].to_broadcast([128, nblk, L])
                c["ab"] = amag[:, j:j + 1].to_broadcast([128, L])
                c["ecl"] = Ec[:, j, L - 1:L]; c["esl"] = Es[:, j, L - 1:L]
                return c

            def mod_ops(c):
                rr, ri, mr, mi, ecb, esb, rrs, rm = c["rr"], c["ri"], c["mr"], c["mi"], c["ecb"], c["esb"], c["rrs"], c["rm"]
                return [
                    lambda: V(lambda e: e.tensor_tensor(v3(tt1), v3(rr), ecb, ALU.mult), [rrs[0], "E"], [rtt[0]]),
                    lambda: V(lambda e: e.tensor_tensor(v3(tt2), v3(ri), esb, ALU.mult), [rrs[1], "E"], [rtt[1]]),
                    lambda: V(lambda e: e.tensor_tensor(mr[:, :Tt], tt1[:, :Tt], tt2[:, :Tt], ALU.add), rtt, [rm[0]]),
                    lambda: V(lambda e: e.tensor_tensor(v3(tt1), v3(ri), ecb, ALU.mult), [rrs[1], "E"], [rtt[0]]),
                    lambda: V(lambda e: e.tensor_tensor(v3(tt2), v3(rr), esb, ALU.mult), [rrs[0], "E"], [rtt[1]]),
                    lambda: V(lambda e: e.tensor_tensor(mi[:, :Tt], tt1[:, :Tt], tt2[:, :Tt], ALU.subtract), rtt, [rm[1]]),
                ]

            def scan_ops(c):
                j, par = c["j"], c["par"]
                rr, ri, mr, mi, ab, ecl, esl, rrs, rm = c["rr"], c["ri"], c["mr"], c["mi"], c["ab"], c["ecl"], c["esl"], c["rrs"], c["rm"]
                sc0, sc1 = s5c[:, 2 * par:2 * par + 1], s5c[:, 2 * par + 1:2 * par + 2]
                rsc = n(f"s5c{par}")
                nesl = nEsL[:, j, (0 if L == SBLK else 1):(1 if L == SBLK else 2)]
                ops = []
                for b in range(nblk):
                    c0 = b * L
                    l1 = c0 + L - 1
                    ops += [
                        lambda c0=c0: V(lambda e: e.tensor_tensor_scan(rr[:, c0:c0 + L], ab, mr[:, c0:c0 + L], Sre[:, j:j + 1], ALU.mult, ALU.add),
                                        [rm[0], "amag", n(f"S{j}")], [rrs[0]]),
                        lambda c0=c0: V(lambda e: e.tensor_tensor_scan(ri[:, c0:c0 + L], ab, mi[:, c0:c0 + L], Sim[:, j:j + 1], ALU.mult, ALU.add),
                                        [rm[1], "amag", n(f"S{j}")], [rrs[1]]),
                        lambda l1=l1: A(lambda e: e.activation(sc0, ri[:, l1:l1 + 1], AF.Identity, scale=nesl), [rrs[1], "E2"], [rsc]),
                        lambda l1=l1: A(lambda e: e.activation(sc1, rr[:, l1:l1 + 1], AF.Identity, scale=esl), [rrs[0], "E"], [rsc]),
                        lambda l1=l1: A(lambda e: e.activation(Sre[:, j:j + 1], rr[:, l1:l1 + 1], AF.Identity, scale=ecl, bias=sc0),
                                        [rrs[0], "E", rsc], [n(f"S{j}")]),
                        lambda l1=l1: A(lambda e: e.activation(Sim[:, j:j + 1], ri[:, l1:l1 + 1], AF.Identity, scale=ecl, bias=sc1),
                                        [rrs[1], "E", rsc], [n(f"S{j}")]),
                    ]
                return ops

            def demod(c):
                p1, p2 = sh.pt
                p3, p4 = c["mr"], c["mi"]
                rp = ["ptmp0", "ptmp1"]
                rr, ri, sr, si, ecb, esb, rrs, rss, rm = c["rr"], c["ri"], c["sr"], c["si"], c["ecb"], c["esb"], c["rrs"], c["rss"], c["rm"]
                G(lambda e: e.tensor_tensor(v3(p1), v3(rr), ecb, ALU.mult), [rrs[0], "E"], [rp[0]])
                G(lambda e: e.tensor_tensor(v3(p2), v3(ri), esb, ALU.mult), [rrs[1], "E"], [rp[1]])
                G(lambda e: e.tensor_tensor(v3(p3), v3(ri), ecb, ALU.mult), [rrs[1], "E"], [rm[0]])
                G(lambda e: e.tensor_tensor(v3(p4), v3(rr), esb, ALU.mult), [rrs[0], "E"], [rm[1]])
                G(lambda e: e.tensor_tensor(sr[:, :Tt], p1[:, :Tt], p2[:, :Tt], ALU.subtract), rp, [rss[0]])
                G(lambda e: e.tensor_tensor(si[:, :Tt], p3[:, :Tt], p4[:, :Tt], ALU.add), rm, [rss[1]])

            emit_bu(0)
            prev = None
            for j in range(NPAIR + 1):
                cur = None
                mods = []
                if j < NPAIR:
                    cur = pair_ctx(j)
                    A(lambda e, o=cur["rr"]: e.copy(o[:, :Tt], pbr), [rbr], [cur["rrs"][0]])
                    A(lambda e, o=cur["ri"]: e.copy(o[:, :Tt], pbi), [rbi], [cur["rrs"][1]])
                    if j + 1 < NPAIR:
                        emit_bu(j + 1)
                    mods = mod_ops(cur)
                scans = scan_ops(prev) if prev is not None else []
                lead = min(4, len(scans)) if mods else len(scans)
                for op in scans[:lead]:
                    op()
                rest = scans[lead:]
                while mods or rest:
                    if mods:
                        mods.pop(0)()
                    if rest:
                        rest.pop(0)()
                if prev is not None:
                    if j >= 3:
                        emit_c(j - 3)
                    demod(prev)
                prev = cur
                yield 8.0
            emit_c(NPAIR - 2)
            emit_c(NPAIR - 1)
            yield 1.0

            zres = [n(f"h{c}") for c in range(4)]
            for s in range(2):
                w, rw = next_slab(S, "ssm_glu_w", s)
                for i2 in range(2):
                    n_ = 2 * s + i2
                    pa, ra = hb(S, "A", Tt)
                    for k in range(4):
                        mm(pa, w[:, k, i2 * 128:(i2 + 1) * 128], zz(k), k == 0, k == 3, [rw] + zres, [ra])
                    kt = S.rot["ft"]; S.rot["ft"] ^= 1
                    A(lambda e, pa=pa, kt=kt, n_=n_: e.activation(S.ftmp[kt][:, :Tt], pa, AF.Sigmoid, bias=colsB[:, n_, 1:2]),
                      [ra, "colsB"], [n(f"ftmp{kt}")])
                    V(lambda e, kt=kt, n_=n_: e.tensor_tensor(z2(n_), zz(n_), S.ftmp[kt][:, :Tt], ALU.mult),
                      [n(f"ftmp{kt}"), n(f"h{n_}")], [n(f"h{4 + n_}")])
                yield 8 * MMC + 0.5

            vpb = S.vpb
            for c in range(4):
                A(lambda e, c=c: e.copy(vpb[:, c, 0:HALO + Tt], vp[:, c, 0:HALO + Tt]), [n(f"vp{c}")], [n(f"vpb{c}")])
            for c in range(4):
                pa, ra = hb(S, "A", Tt)
                for jt in range(CW):
                    mm(pa, Dg[:, c, jt, :], vpb[:, c, jt:jt + Tt], jt == 0, jt == CW - 1, ["Dg", n(f"vpb{c}")], [ra])
                A(lambda e, c=c, pa=pa: e.activation(sg[:, c, :Tt], pa, AF.Identity, bias=colsB[:, c, 2:3]), [ra, "colsB"], [n(f"sg{c}")])
                yield CW * MMC
            psm, rsm = bankk(S, 1, Tt)
            pst, rst = bankk(S, 3, Tt)
            for c in range(4):
                A(lambda e, c=c: e.copy(um[:, c, :Tt], sg[:, c, :Tt]), [n(f"sg{c}")], [n(f"um{c}")])
                mm(psm, ones[:], um[:, c, :Tt], c == 0, c == 3, ["ones", n(f"um{c}")], [rsm])
            V(lambda e: e.tensor_scalar(S.mean[:, :Tt], psm, 1.0 / DS, None, ALU.mult), [rsm], [n("mean")])
            for c in range(4):
                k = S.rot["sq"]; S.rot["sq"] ^= 1
                A(lambda e, c=c, k=k: e.activation(S.sq[k][:, :Tt], sg[:, c, :Tt], AF.Square), [n(f"sg{c}")], [n(f"sq{k}")])
                mm(pst, ones[:], S.sq[k][:, :Tt], c == 0, c == 3, ["ones", n(f"sq{k}")], [rst])
            V(lambda e: e.tensor_tensor(S.msq[:, :Tt], S.mean[:, :Tt], S.mean[:, :Tt], ALU.mult), [n("mean")], [n("msq")])
            V(lambda e: e.scalar_tensor_tensor(S.rstd[:, :Tt], pst, 1.0 / DS, S.msq[:, :Tt], ALU.mult, ALU.subtract),
              [rst, n("msq")], [n("rstd")])
            V(lambda e: e.tensor_scalar(S.rstd[:, :Tt], S.rstd[:, :Tt], EPS, None, ALU.add), [n("rstd")], [n("rstd")])
            A(lambda e: e.activation(S.rstd[:, :Tt], S.rstd[:, :Tt], AF.Ln), [n("rstd")], [n("rstd")])
            A(lambda e: e.activation(S.rstd[:, :Tt], S.rstd[:, :Tt], AF.Exp, scale=-0.5), [n("rstd")], [n("rstd")])
            for c in range(4):
                V(lambda e, c=c: e.tensor_tensor(sg[:, c, :Tt], sg[:, c, :Tt], S.mean[:, :Tt], ALU.subtract), [n(f"sg{c}"), n("mean")], [n(f"sg{c}")])
                V(lambda e, c=c: e.tensor_tensor(sg[:, c, :Tt], sg[:, c, :Tt], S.rstd[:, :Tt], ALU.mult), [n(f"sg{c}"), n("rstd")], [n(f"sg{c}")])
                A(lambda e, c=c: e.activation(cact(c), sg[:, c, :Tt], AF.Silu, bias=colsB[:, c, 4:5], scale=colsB[:, c, 3:4]),
                  [n(f"sg{c}"), "colsB"], [n(f"g{16 + c}")])
            yield "unlock"
            yield 8.0
            z2res = [n(f"h{4 + c}") for c in range(4)]
            cres = [n(f"g{16 + c}") for c in range(4)]
            for s in range(4):
                ws, rws = next_slab(S, "ssm_out_w", s)
                wc, rwc = next_slab(S, "conv_out_w", s)
                for i2 in range(2):
                    n_ = 2 * s + i2
                    pa, ra = bankk(S, 1, Tt); pb_, rb = bankk(S, 3, Tt)
                    for k in range(4):
                        mm(pa, ws[:, k, i2 * 128:(i2 + 1) * 128], z2(k), k == 0, k == 3, [rws] + z2res, [ra])
                    for k in range(4):
                        mm(pb_, wc[:, k, i2 * 128:(i2 + 1) * 128], cact(k), k == 0, k == 3, [rwc] + cres, [rb])
                    V(lambda e, pa=pa, n_=n_: e.tensor_tensor(S.ftmp[0][:, :Tt], pa, gs(n_), ALU.mult), [ra, n(f"g{n_}")], [n("ftmp0")])
                    V(lambda e, pb_=pb_, n_=n_: e.tensor_tensor(S.ftmp[1][:, :Tt], pb_, gc(n_), ALU.mult), [rb, n(f"g{8 + n_}")], [n("ftmp1")])
                    V(lambda e, n_=n_: e.tensor_tensor(gs(n_), S.ftmp[0][:, :Tt], S.ftmp[1][:, :Tt], ALU.add), [n("ftmp0"), n("ftmp1")], [n(f"g{n_}")])
                yield 16 * MMC + 1.0
            mres = [n(f"g{n_}") for n_ in range(8)]
            for s in range(4):
                w, rw = next_slab(S, "w_o", s)
                for i2 in range(2):
                    n_ = 2 * s + i2
                    py, ry = hb(S, "Y", Tt)
                    for k in range(8):
                        mm(py, w[:, k, i2 * 128:(i2 + 1) * 128], gs(k), k == 0, k == 7, [rw] + mres, [ry])
                    V(lambda e, py=py, n_=n_: e.tensor_tensor(S.xT[:, n_, :Tt], S.xT[:, n_, :Tt], py, ALU.add), [ry, n(f"xT{n_}")], [n(f"xT{n_}")])
                yield 16 * MMC

        def emit_conv_state(S, Tt, dst_ap):
            n = S.n
            bank = S.pb[0]
            rb = [S.pbn[0]]
            for c in range(4):
                tr(bank[:HALO, c * 128:(c + 1) * 128], S.vp[:, c, Tt:Tt + HALO], 128, [n(f"vp{c}"), "ident"], rb)
            A(lambda e: e.copy(ncst[:, :], bank[:HALO, 0:512]), rb, ["ncst"])
            P.dma("scalar", dst_ap, ncst[:, :], reads=["ncst"], sem="o_nc")

        def emit_ssm_state(S, dst_re, dst_im):
            n = S.n
            bank = S.pb[0]
            rb = [S.pbn[0]]
            Sall = [n(f"S{j}") for j in range(16)]
            tr(bank[:16, 0:128], S.Sre[:, :], 128, Sall + ["ident"], rb)
            tr(bank[:16, 128:256], S.Sim[:, :], 128, Sall + ["ident"], rb)
            A(lambda e: e.copy(sst[:, :], bank[:16, 0:256]), rb, ["sst"])
            P.dma("scalar", dst_re.rearrange("(j gl) p -> j (gl p)", gl=2), sst[:, 0:128], reads=["sst"], sem="o_st")
            P.dma("scalar", dst_im.rearrange("(j gl) p -> j (gl p)", gl=2), sst[:, 128:256], reads=["sst"], sem="o_st")

        def load_x(Tt, src_ap):
            nb = (Tt + 127) // 128
            tb = min(128, Tt)
            P.dma("sync", xin[:tb, 0:nb * 1024].rearrange("p (b d) -> p b d", b=nb),
                  src_ap.rearrange("(b p) d -> p b d", p=tb), writes=["xin"], sem="xin")

        def transpose_in(S, Tt):
            n = S.n
            nb = (Tt + 127) // 128
            tb = min(128, Tt)
            for c in range(8):
                py, ry = hb(S, "A", Tt)
                for b in range(nb):
                    tr(py[:, b * tb:(b + 1) * tb], xin[:tb, b * 1024 + c * 128: b * 1024 + (c + 1) * 128], tb, ["xin", "ident"], [ry])
                A(lambda e, c=c, py=py: e.copy(S.xT[:, c, :Tt], py), [ry], [n(f"xT{c}")])

        def emit_output(S, Tt, dst_ap):
            n = S.n
            nb = (Tt + 127) // 128
            tb = min(128, Tt)
            rmsnorm_stats(S, Tt, 8, lambda c: S.xT[:, c, :Tt], lambda c: n(f"xT{c}"), 1.0 / D)
            for c in range(8):
                V(lambda e, c=c: e.scalar_tensor_tensor(S.xT[:, c, :Tt], S.xT[:, c, :Tt], gcol["final"](c), S.rstd[:, :Tt],
                                                        ALU.mult, ALU.mult), [n(f"xT{c}"), "colsA", n("rstd")], [n(f"xT{c}")])
            bank = S.pb[0]
            rb = [S.pbn[0]]
            for b in range(nb):
                slot = b % 2
                ysl = S.gb32[:tb, slot * 1024:(slot + 1) * 1024]
                yres = [n(f"g{i}") for i in range(slot * 8, slot * 8 + 9)]
                for hf in range(2):
                    for c4 in range(4):
                        c = hf * 4 + c4
                        tr(bank[:tb, c4 * 128:(c4 + 1) * 128], S.xT[:, c, b * tb:(b + 1) * tb], 128, [n(f"xT{c}"), "ident"], rb)
                    A(lambda e, hf=hf, ysl=ysl: e.copy(ysl[:, hf * 512:(hf + 1) * 512], bank[:tb, 0:512]), rb, yres)
                P.dma("scalar", dst_ap[b * tb:(b + 1) * tb, :], ysl, reads=yres, sem=f"o_y{S.sid}_{slot}")

        xin_state = {"owner": None}

        def stream_gen(S, tiles, tps):
            n = S.n
            S.prefetched = False
            for idx, t in enumerate(tiles):
                kind, sq_i, ti_ = t
                if kind == "p":
                    Tt = T
                    src = dr["xp"][sq_i, ti_ * T:(ti_ + 1) * T, :]
                else:
                    Tt = DSEQ
                    src = dr["xs"][:, :]
                first = ti_ == 0
                last = (kind == "s") or (ti_ == tps - 1)
                if not S.prefetched:
                    while xin_state["owner"] is not None:
                        yield 0.5
                    load_x(Tt, src)
                    xin_state["owner"] = S.sid
                transpose_in(S, Tt)
                xin_state["owner"] = None
                S.prefetched = False
                yield 4.0
                if first:
                    Sall = [n(f"S{j}") for j in range(16)]
                    if kind == "p":
                        V(lambda e: e.memset(S.Sre[:], 0.0), [], Sall)
                        V(lambda e: e.memset(S.Sim[:], 0.0), [], Sall)
                        for c in range(4):
                            G(lambda e, c=c: e.memset(S.vp[:, c, 0:HALO], 0.0), [], [n(f"vp{c}")])
                    else:
                        bank = S.pb[0]
                        rb = [S.pbn[0]]
                        for (srcS, dstS) in ((dr["sre"], S.Sre), (dr["sim"], S.Sim)):
                            P.dma("sync", sst[0:16, 0:128], srcS.rearrange("(j gl) p -> j (gl p)", gl=2), writes=["sst"], sem="su")
                            tr(bank[:, 0:16], sst[0:16, 0:128], 16, ["sst", "ident"], rb)
                            A(lambda e, dstS=dstS: e.copy(dstS[:], bank[:, 0:16]), rb, Sall)
                        P.dma("sync", ncst[:, :], dr["cc"], writes=["ncst"], sem="su")
                        for c in range(4):
                            tr(bank[:, c * HALO:(c + 1) * HALO], ncst[:, c * 128:(c + 1) * 128], HALO, ["ncst", "ident"], rb)
                        for c in range(4):
                            A(lambda e, c=c: e.copy(S.vp[:, c, 0:HALO], bank[:, c * HALO:(c + 1) * HALO]), rb, [n(f"vp{c}")])
                yield from ffn(S, Tt, "ffn1")
                yield from mixer(S, Tt)
                if last:
                    if kind == "p":
                        emit_conv_state(S, Tt, dr["ncp"][sq_i])
                        emit_ssm_state(S, dr["nrp"][sq_i], dr["nip"][sq_i])
                    else:
                        emit_conv_state(S, Tt, dr["ncs"])
                        emit_ssm_state(S, dr["nrs"], dr["nis"])
                else:
                    for c in range(4):
                        G(lambda e, c=c, Tt=Tt: e.tensor_copy(S.vp[:, c, 0:HALO], S.vp[:, c, Tt:Tt + HALO]), [n(f"vp{c}")], [n(f"vp{c}")])
                if idx + 1 < len(tiles) and xin_state["owner"] is None:
                    xin_state["owner"] = S.sid
                    S.prefetched = True
                    k2, s2, t2 = tiles[idx + 1]
                    if k2 == "p":
                        load_x(T, dr["xp"][s2, t2 * T:(t2 + 1) * T, :])
                    else:
                        load_x(DSEQ, dr["xs"][:, :])
                yield from ffn(S, Tt, "ffn2")
                if kind == "p":
                    emit_output(S, Tt, dr["yp"][sq_i, ti_ * T:(ti_ + 1) * T, :])
                else:
                    emit_output(S, Tt, dr["ys"])
                yield 4.0

        tps = seq_len // T
        streams = [make_stream(0), make_stream(1)]
        tile_lists = [[], []]
        for sq_i in range(n_prompt_seq):
            sid = 0 if sq_i < (n_prompt_seq + 1) // 2 else 1
            for ti_ in range(tps):
                tile_lists[sid].append(("p", sq_i, ti_))
        if do_sample:
            tile_lists[1].append(("s", 0, 0))
        gens = []
        for S, tl in zip(streams, tile_lists):
            for _ in tl:
                S.slab_seq.extend(tile_slabs())
            gens.append(stream_gen(S, tl, tps))
        clocks = [0.0, 230.0]
        alive = [len(tile_lists[0]) > 0, len(tile_lists[1]) > 0]
        waiting = [False, False]
        owner = None
        while any(alive):
            cands = [k for k in range(2) if alive[k] and not (waiting[k] and owner is not None)]
            i = min(cands, key=lambda k: clocks[k])
            if waiting[i]:
                waiting[i] = False
                owner = i
                clocks[i] = max(clocks[i], clocks[1 - i])
                continue
            try:
                r = next(gens[i])
            except StopIteration:
                alive[i] = False
                if owner == i:
                    owner = None
                continue
            if r == "lock":
                if owner is None:
                    owner = i
                else:
                    waiting[i] = True
            elif r == "unlock":
                owner = None
            else:
                clocks[i] += r
        P.wait_all("sync")
        P.build()
    return nc, P


_CACHE = {}


def kernel(**inputs):
    inp = {k: np.ascontiguousarray(np.asarray(v)) for k, v in inputs.items()}
    if "nc" not in _CACHE:
        _CACHE["nc"] = build_program()[0]
    nc = _CACHE["nc"]
    in_maps = []
    for c in range(NCORES):
        m = {
            "xp": inp["x_prompt"][c * BPC:(c + 1) * BPC],
            "xs": inp["x_sample"][c],
            "sre": inp["state_ssm_re"][0, c],
            "sim": inp["state_ssm_im"][0, c],
            "cc": inp["cache_conv"][0, c],
        }
        for nm, K, N in WEIGHTS:
            m[nm] = inp[nm][0]
        for nm, shp in SMALL:
            m[nm] = inp[nm] if nm == "final_norm" else inp[nm][0]
        in_maps.append({k: np.ascontiguousarray(v, dtype=np.float32) for k, v in m.items()})
    res = run_bass_kernel_spmd(nc, in_maps, core_ids=list(range(NCORES)))
    R = res.results
    cat = lambda k: np.concatenate([r[k] for r in R], axis=0)
    stk = lambda k: np.stack([r[k] for r in R], axis=0)
    y_prompt = cat("yp")
    y_sample = stk("ys")
    nrp = cat("nrp")[None]; nip = cat("nip")[None]; ncp = cat("ncp")[None]
    nrs = stk("nrs")[None]; nis = stk("nis")[None]; ncs = stk("ncs")[None]
    return (y_prompt, y_sample, nrp, nip, ncp, nrs, nis, ncs)
```

```python
import numpy as np
from contextlib import ExitStack
import concourse.bass as bass
import concourse.mybir as mybir
from concourse.bass_utils import run_bass_kernel_spmd

F32 = mybir.dt.float32
BF16 = mybir.dt.bfloat16
I32 = mybir.dt.int32
ALU = mybir.AluOpType
AF = mybir.ActivationFunctionType

D = 1024
DFF = 2816
NIN = 3584
DS = 512
NPAIR = 16
CW = 31
HALO = 30
SEQ = 2048
BPC = 4
NCORES = 8
DSEQ = 16
EPS = 1e-6
TWO_PI = 6.283185307179586
PI = 3.141592653589793

MODE = "bf16"
ENGS = ("sync", "scalar", "vector", "gpsimd", "tensor")
import os
SAME_ENGINE_SYNC = os.environ.get("KSES", "1") == "1"
DBG = os.environ.get("KDBG", "")


class StopBuild(Exception):
    pass


class Prog:
    def __init__(self, nc):
        self.nc = nc
        self.ops = {e: [] for e in ENGS}
        self.count = {}
        self.waited = {e: {} for e in ENGS}
        self.res = {}
        self.sems = {}
        self.ninst = {e: 0 for e in ENGS}

    def _need(self, eng, toks):
        need = {}
        for t in toks:
            if t is None:
                continue
            s, v, te = t
            if te == eng and s.startswith("e_"):
                if eng == "tensor" or not SAME_ENGINE_SYNC:
                    continue
            if need.get(s, 0) < v:
                need[s] = v
        for s, v in need.items():
            if self.waited[eng].get(s, 0) >= v:
                continue
            self.waited[eng][s] = v
            self.ops[eng].append(("wait", s, v))

    def emit(self, eng, fn, reads=(), writes=(), sem=None, inc=1):
        toks = []
        for r in reads:
            st = self.res.get(r)
            if st is not None:
                toks.append(st[0])
        for w in writes:
            st = self.res.get(w)
            if st is not None:
                toks.append(st[0])
                toks.extend(st[1])
        self._need(eng, toks)
        if sem is None:
            sem = "e_" + eng
        v = self.count.get(sem, 0) + inc
        self.count[sem] = v
        tok = (sem, v, eng)
        self.ops[eng].append(("op", fn, sem, inc))
        self.ninst[eng] += 1
        for r in reads:
            st = self.res.setdefault(r, [None, []])
            st[1].append(tok)
        for w in writes:
            self.res[w] = [tok, []]
        return tok

    def dma(self, eng, out, in_, reads=(), writes=(), sem=None, **kw):
        prev = self.count.get("d_" + sem, 0)
        if prev and self.waited[eng].get("d_" + sem, 0) < prev:
            self.waited[eng]["d_" + sem] = prev
            self.ops[eng].append(("wait", "d_" + sem, prev))
        return self.emit(eng, lambda e: e.dma_start(out=out, in_=in_, **kw),
                         reads=reads, writes=writes, sem="d_" + sem, inc=16)

    def wait_all(self, eng):
        for s, v in self.count.items():
            if self.waited[eng].get(s, 0) < v:
                self.waited[eng][s] = v
                self.ops[eng].append(("wait", s, v))

    def build(self):
        nc = self.nc
        with ExitStack() as es:
            for s in self.count:
                self.sems[s] = es.enter_context(nc.semaphore(s))
            block = es.enter_context(nc.Block())
            prog = self

            def runner(name):
                def run(e):
                    for op in prog.ops[name]:
                        if op[0] == "wait":
                            e.wait_ge(prog.sems[op[1]], op[2])
                        else:
                            _, fn, s, inc = op
                            fn(e).then_inc(prog.sems[s], inc)
                return run

            block.sync(runner("sync"))
            block.scalar(runner("scalar"))
            block.vector(runner("vector"))
            block.gpsimd(runner("gpsimd"))
            block.tensor(runner("tensor"))


WEIGHTS = [
    ("ffn1_w1", D, DFF), ("ffn1_w3", D, DFF), ("ffn1_w2", DFF, D),
    ("w_in", D, NIN), ("ssm_glu_w", DS, DS), ("ssm_out_w", DS, D), ("conv_out_w", DS, D),
    ("w_o", D, D),
    ("ffn2_w1", D, DFF), ("ffn2_w3", D, DFF), ("ffn2_w2", DFF, D),
]
WSLAB_NC = {"ffn1_w2": 128, "ffn2_w2": 128}
SMALL = [
    ("ffn1_norm", [D]), ("mix_norm", [D]), ("ffn2_norm", [D]), ("final_norm", [D]),
    ("ssm_d", [DS]), ("ssm_glu_b", [DS]), ("conv_dw_b", [DS]), ("conv_ln_g", [DS]), ("conv_ln_b", [DS]),
    ("conv_dw_w", [CW, DS]),
    ("ssm_lambda_re", [32, 64]), ("ssm_lambda_im", [32, 64]), ("ssm_log_step", [32]),
    ("ssm_b_re", [32, 64, 16]), ("ssm_b_im", [32, 64, 16]),
    ("ssm_c_re", [32, 16, 64]), ("ssm_c_im", [32, 16, 64]),
]


def build_program(n_prompt_seq=BPC, seq_len=SEQ, do_sample=True, mode=MODE, stop_after=None):
    bf = mode == "bf16"
    assert bf
    MDT = BF16
    T = 256
    SBLK = 128
    NSLOT = 4
    SLOT_EL = 2048
    NSTREAM = 2
    nc = bass.Bass("TRN2", target_bir_lowering=False)
    P = Prog(nc)
    dr = {}

    def din(name, shape, dt=F32):
        dr[name] = nc.dram_tensor(name, list(shape), dt, kind="ExternalInput").ap()
        return dr[name]

    def dout(name, shape, dt=F32):
        dr[name] = nc.dram_tensor(name, list(shape), dt, kind="ExternalOutput").ap()
        return dr[name]

    din("xp", [BPC, SEQ, D]); din("xs", [DSEQ, D])
    din("sre", [32, 64]); din("sim", [32, 64]); din("cc", [HALO, DS])
    for nm, K, N in WEIGHTS:
        din(nm, [K, N])
    for nm, shp in SMALL:
        din(nm, shp)
    dout("yp", [BPC, SEQ, D]); dout("ys", [DSEQ, D])
    dout("nrp", [BPC, 32, 64]); dout("nip", [BPC, 32, 64]); dout("ncp", [BPC, HALO, DS])
    dout("nrs", [32, 64]); dout("nis", [32, 64]); dout("ncs", [HALO, DS])

    winfo = {}
    for nm, K, N in WEIGHTS:
        if nm.endswith("_w2"):
            kc, ncol, ns = 11, 128, 16
        else:
            ncol = 256
            kc = K // 128
            ns = N // ncol
        winfo[nm] = (kc, ncol, ns)
        dr["scr_" + nm] = nc.dram_tensor("scr_" + nm, [ns, 128, kc * ncol], BF16, kind="Internal").ap()

    es = ExitStack()
    with es:
        def sb(name, shape, dt=F32):
            return es.enter_context(nc.sbuf_tensor(name, list(shape), dt))

        def psb(name):
            return es.enter_context(nc.psum_tensor(name, [128, 512], F32))

        ps = [psb(f"ps{i}") for i in range(8)]
        ident = sb("ident", [128, 128])
        ones = sb("ones", [128, 128], MDT)
        wslab = sb("wslab", [128, NSTREAM * NSLOT * SLOT_EL], MDT)
        wsl32 = wslab[:].bitcast(F32)
        Ec = sb("Ec", [128, NPAIR, SBLK]); Es = sb("Es", [128, NPAIR, SBLK])
        WBr = sb("WBr", [128, NPAIR, 128], MDT); WBi = sb("WBi", [128, NPAIR, 128], MDT)
        WCr = sb("WCr", [128, NPAIR, 128], MDT); WCi = sb("WCi", [128, NPAIR, 128], MDT)
        amag = sb("amag", [128, NPAIR])
        nEsL = sb("nEsL", [128, NPAIR, 2])
        colsA = sb("colsA", [128, 8, 4])
        colsB = sb("colsB", [128, 4, 36])
        ncst = sb("ncst", [HALO, DS])
        sst = sb("sst", [16, 256])
        xin = sb("xinr", [128, 2048])
        identb = sb("identb", [128, 128], MDT)
        Dg = sb("Dg", [128, 4, CW, 128], MDT)
        ti = sb("ti", [128, 16], I32)

        def V(fn, r, w): return P.emit("vector", fn, reads=r, writes=w)
        def A(fn, r, w): return P.emit("scalar", fn, reads=r, writes=w)
        def G(fn, r, w): return P.emit("gpsimd", fn, reads=r, writes=w)
        def PE(fn, r, w): return P.emit("tensor", fn, reads=r, writes=w)

        def mm(out, lhsT, rhs, start, stop, r, w):
            return PE(lambda e: e.matmul(out, lhsT=lhsT, rhs=rhs, start=start, stop=stop), r, w)

        def tr(out, in_, n, r, w):
            return PE(lambda e: e.transpose(out, in_, ident[:n, :n]), r, w)

        xin_real = xin
        xin = wsl32[:, 0:3584]
        gb32 = wsl32[:, 3584:7168]
        gres = lambda i: f"g{i}"
        stg = gb32[0:36, 2560:3584]

        G(lambda e: e.memset(ident[:], 0.0), [], ["ident"])
        G(lambda e: e.affine_select(out=ident[:], in_=ident[:], pattern=[[-1, 128]], compare_op=ALU.not_equal,
                                    fill=1.0, base=0, channel_multiplier=1), ["ident"], ["ident"])
        V(lambda e: e.memset(ones[:], 1.0), [], ["ones"])

        def wsrc(nm, s):
            kc, ncol, ns = winfo[nm]
            if nm.endswith("_w2"):
                n_, kh = s // 2, s % 2
                return dr[nm][kh * 1408:(kh + 1) * 1408, n_ * 128:(n_ + 1) * 128].rearrange("(kc p) n -> p kc n", p=128)
            return dr[nm][:, s * ncol:(s + 1) * ncol].rearrange("(kc p) n -> p kc n", p=128)

        if bf:
            for nm, K, N in WEIGHTS:
                kc, ncol, ns = winfo[nm]
                for s in range(ns):
                    P.dma("gpsimd", dr["scr_" + nm][s].rearrange("p (kc n) -> p kc n", kc=kc), wsrc(nm, s),
                          writes=[f"scr_{nm}_{s}"], sem=f"cast{s % 4}")

        STG = [f"g{i}" for i in range(22)]
        for i, nm in enumerate(["ffn1_norm", "mix_norm", "ffn2_norm", "final_norm"]):
            P.dma("sync", stg[i:i + 1, :], dr[nm].rearrange("(o d) -> o d", o=1), writes=STG, sem="su")
        for c in range(8):
            tr(ps[7][:, c * 4:(c + 1) * 4], stg[0:4, c * 128:(c + 1) * 128], 4, STG + ["ident"], ["ps7"])
        V(lambda e: e.tensor_copy(colsA[:].rearrange("p c k -> p (c k)"), ps[7][:, 0:32]), ["ps7"], ["colsA"])
        for i, nm in enumerate(["ssm_d", "ssm_glu_b", "conv_dw_b", "conv_ln_g", "conv_ln_b"]):
            P.dma("sync", stg[i:i + 1, 0:DS], dr[nm].rearrange("(o d) -> o d", o=1), reads=[], writes=STG, sem="su")
        P.dma("sync", stg[5:36, 0:DS], dr["conv_dw_w"], writes=STG, sem="su")
        for c in range(4):
            tr(ps[7][:, c * 36:(c + 1) * 36], stg[0:36, c * 128:(c + 1) * 128], 36, STG + ["ident"], ["ps7"])
        V(lambda e: e.tensor_copy(colsB[:].rearrange("p c k -> p (c k)"), ps[7][:, 0:144]), ["ps7"], ["colsB"])
        gcol = {"ffn1": lambda c: colsA[:, c, 0:1], "mix": lambda c: colsA[:, c, 1:2],
                "ffn2": lambda c: colsA[:, c, 2:3], "final": lambda c: colsA[:, c, 3:4]}
        dcol = lambda c: colsB[:, c, 0:1]
        glubcol = lambda c: colsB[:, c, 1:2]
        dwbcol = lambda c: colsB[:, c, 2:3]
        lngcol = lambda c: colsB[:, c, 3:4]
        lnbcol = lambda c: colsB[:, c, 4:5]
        tapcol = lambda c, j: colsB[:, c, 5 + j:6 + j]

        xs32 = xin
        LR = xs32[:, 0:16]; LI = xs32[:, 16:32]; DT = xs32[:, 32:48]; TH = xs32[:, 48:64]
        AR = xs32[:, 64:80]; AI = xs32[:, 80:96]; KR = xs32[:, 96:112]; KI = xs32[:, 112:128]
        t0 = xs32[:, 128:144]; t1 = xs32[:, 144:160]; t2 = xs32[:, 160:176]; t3 = xs32[:, 176:192]
        SN = xs32[:, 192:208]; CS = xs32[:, 208:224]
        XR = "xin"

        def loadT16(dst, src_ap):
            P.dma("sync", sst[0:16, 0:128], src_ap, writes=["sst"], sem="su")
            tr(ps[7][:, 0:16], sst[0:16, 0:128], 16, ["sst", "ident"], ["ps7"])
            V(lambda e: e.tensor_copy(dst, ps[7][:, 0:16]), ["ps7"], [XR])

        loadT16(LR, dr["ssm_lambda_re"].rearrange("(j gl) p -> j (gl p)", gl=2))
        loadT16(LI, dr["ssm_lambda_im"].rearrange("(j gl) p -> j (gl p)", gl=2))
        lsh = dr["ssm_log_step"].tensor
        for gl in range(2):
            P.dma("sync", xs32[gl * 64:(gl + 1) * 64, 32:48], bass.AP(tensor=lsh, offset=gl, ap=[[0, 64], [2, 16]]),
                  writes=[XR], sem="su", allow_slow_non_contiguous=True)
        A(lambda e: e.activation(DT, DT, AF.Exp), [XR], [XR])
        V(lambda e: e.tensor_tensor(t0, LR, DT, ALU.mult), [XR], [XR])
        A(lambda e: e.activation(amag[:], t0, AF.Exp), [XR], ["amag"])
        V(lambda e: e.tensor_tensor(TH, LI, DT, ALU.mult), [XR], [XR])

        V(lambda e: e.tensor_scalar(t1, TH, 1.0 / TWO_PI, None, ALU.mult), [XR], [XR])
        V(lambda e: e.tensor_copy(ti[:], t1), [XR], ["ti"])
        V(lambda e: e.tensor_copy(t2, ti[:]), ["ti"], [XR])
        V(lambda e: e.scalar_tensor_tensor(t1, t2, -TWO_PI, TH, ALU.mult, ALU.add), [XR], [XR])
        V(lambda e: e.tensor_scalar(t2, t1, PI, TWO_PI, ALU.is_gt, ALU.mult), [XR], [XR])
        V(lambda e: e.tensor_tensor(t1, t1, t2, ALU.subtract), [XR], [XR])
        V(lambda e: e.tensor_scalar(t2, t1, -PI, TWO_PI, ALU.is_lt, ALU.mult), [XR], [XR])
        V(lambda e: e.tensor_tensor(t1, t1, t2, ALU.add), [XR], [XR])
        V(lambda e: e.tensor_scalar(t1, t1, 0.25, None, ALU.mult), [XR], [XR])
        V(lambda e: e.tensor_scalar(t2, t1, PI / 2, None, ALU.add), [XR], [XR])
        A(lambda e: e.activation(SN, t1, AF.Sin), [XR], [XR])
        A(lambda e: e.activation(CS, t2, AF.Sin), [XR], [XR])
        for _ in range(2):
            V(lambda e: e.tensor_tensor(t1, CS, CS, ALU.mult), [XR], [XR])
            V(lambda e: e.tensor_tensor(t2, SN, SN, ALU.mult), [XR], [XR])
            V(lambda e: e.tensor_tensor(t3, CS, SN, ALU.mult), [XR], [XR])
            V(lambda e: e.tensor_tensor(CS, t1, t2, ALU.subtract), [XR], [XR])
            V(lambda e: e.tensor_scalar(SN, t3, 2.0, None, ALU.mult), [XR], [XR])
        V(lambda e: e.tensor_tensor(AR, amag[:], CS, ALU.mult), [XR, "amag"], [XR])
        V(lambda e: e.tensor_tensor(AI, amag[:], SN, ALU.mult), [XR, "amag"], [XR])
        V(lambda e: e.tensor_scalar(t0, AR, -1.0, None, ALU.add), [XR], [XR])
        V(lambda e: e.tensor_tensor(t1, LR, LR, ALU.mult), [XR], [XR])
        V(lambda e: e.tensor_tensor(t2, LI, LI, ALU.mult), [XR], [XR])
        V(lambda e: e.tensor_tensor(t1, t1, t2, ALU.add), [XR], [XR])
        V(lambda e: e.reciprocal(t1, t1), [XR], [XR])
        V(lambda e: e.tensor_tensor(t2, t0, LR, ALU.mult), [XR], [XR])
        V(lambda e: e.tensor_tensor(t3, AI, LI, ALU.mult), [XR], [XR])
        V(lambda e: e.tensor_tensor(t2, t2, t3, ALU.add), [XR], [XR])
        V(lambda e: e.tensor_tensor(KR, t2, t1, ALU.mult), [XR], [XR])
        V(lambda e: e.tensor_tensor(t2, AI, LR, ALU.mult), [XR], [XR])
        V(lambda e: e.tensor_tensor(t3, t0, LI, ALU.mult), [XR], [XR])
        V(lambda e: e.tensor_tensor(t2, t2, t3, ALU.subtract), [XR], [XR])
        V(lambda e: e.tensor_tensor(KI, t2, t1, ALU.mult), [XR], [XR])
        V(lambda e: e.tensor_copy(Ec[:, :, 0:1], CS.unsqueeze(2)), [XR], ["E"])
        V(lambda e: e.tensor_copy(Es[:, :, 0:1], SN.unsqueeze(2)), [XR], ["E"])
        TA = xs32[:, 1024:1024 + 16 * 64].rearrange("p (j m) -> p j m", j=16)
        TB = xs32[:, 2048:2048 + 16 * 64].rearrange("p (j m) -> p j m", j=16)
        n = 1
        while n < SBLK:
            cn = Ec[:, :, n - 1:n].to_broadcast([128, NPAIR, n])
            sn = Es[:, :, n - 1:n].to_broadcast([128, NPAIR, n])
            c_lo = Ec[:, :, 0:n]; s_lo = Es[:, :, 0:n]
            c_hi = Ec[:, :, n:2 * n]; s_hi = Es[:, :, n:2 * n]
            ta = TA[:, :, 0:n]; tb = TB[:, :, 0:n]
            V(lambda e, a=ta, x=c_lo, y=cn: e.tensor_tensor(a, x, y, ALU.mult), ["E"], [XR])
            V(lambda e, a=tb, x=s_lo, y=sn: e.tensor_tensor(a, x, y, ALU.mult), ["E"], [XR])
            V(lambda e, o=c_hi, a=ta, b=tb: e.tensor_tensor(o, a, b, ALU.subtract), [XR, "E"], ["E"])
            V(lambda e, a=ta, x=s_lo, y=cn: e.tensor_tensor(a, x, y, ALU.mult), ["E"], [XR])
            V(lambda e, a=tb, x=c_lo, y=sn: e.tensor_tensor(a, x, y, ALU.mult), ["E"], [XR])
            V(lambda e, o=s_hi, a=ta, b=tb: e.tensor_tensor(o, a, b, ALU.add), [XR, "E"], ["E"])
            n *= 2
        BR = xs32[:, 256:512].rearrange("p (j c) -> p j c", j=16)
        BI = xs32[:, 512:768].rearrange("p (j c) -> p j c", j=16)
        BBR = xs32[:, 768:1024].rearrange("p (j c) -> p j c", j=16)
        BBI = xs32[:, 3072:3328].rearrange("p (j c) -> p j c", j=16)
        P.dma("sync", BR, dr["ssm_b_re"].rearrange("(j gl) p c -> (gl p) j c", gl=2), writes=[XR], sem="su")
        P.dma("sync", BI, dr["ssm_b_im"].rearrange("(j gl) p c -> (gl p) j c", gl=2), writes=[XR], sem="su")
        krb = KR.unsqueeze(2).to_broadcast([128, 16, 16]); kib = KI.unsqueeze(2).to_broadcast([128, 16, 16])
        ta = TA[:, :, 0:16]; tb = TB[:, :, 0:16]
        V(lambda e: e.tensor_tensor(ta, BR, krb, ALU.mult), [XR], [XR])
        V(lambda e: e.tensor_tensor(tb, BI, kib, ALU.mult), [XR], [XR])
        V(lambda e: e.tensor_tensor(BBR, ta, tb, ALU.subtract), [XR], [XR])
        V(lambda e: e.tensor_tensor(ta, BI, krb, ALU.mult), [XR], [XR])
        V(lambda e: e.tensor_tensor(tb, BR, kib, ALU.mult), [XR], [XR])
        V(lambda e: e.tensor_tensor(BBI, ta, tb, ALU.add), [XR], [XR])
        Z = gb32[:, 0:2048].rearrange("p (j r) -> p j r", j=16)
        ZR = [gres(i) for i in range(22)]
        for (BB, WB, nm) in ((BBR, WBr, "WBr"), (BBI, WBi, "WBi")):
            V(lambda e: e.memset(Z, 0.0), [], ZR)
            for jj in range(4):
                for gl in range(2):
                    lo = gl * 64
                    V(lambda e, jj=jj, gl=gl, lo=lo, BB=BB: e.tensor_copy(
                        Z[lo:lo + 64, jj::4, 32 * jj + 16 * gl:32 * jj + 16 * gl + 16], BB[lo:lo + 64, jj::4, :]),
                      [XR], ZR)
            for j in range(16):
                pb = ps[j % 2]
                tr(pb[:, 0:128], Z[:, j, :], 128, ZR + ["ident"], [f"ps{j % 2}"])
                V(lambda e, j=j, pb=pb, WB=WB: e.tensor_copy(WB[:, j, :], pb[:, 0:128]), [f"ps{j % 2}"], [nm])
        CX = gb32[:, 2048:2048 + 512].rearrange("p (q m) -> p q m", q=4)
        for (cname, WC, nm, sgn) in (("ssm_c_re", WCr, "WCr", 1.0), ("ssm_c_im", WCi, "WCi", -1.0)):
            csrc = dr[cname].rearrange("(q g) c p -> (g c) q p", q=4)
            P.dma("sync", CX[:, :, 0:64], csrc, writes=ZR, sem="su")
            P.dma("sync", CX[:, :, 64:128], csrc, writes=[gres(21)], sem="su")
            V(lambda e, WC=WC: e.memset(WC[:], 0.0), [], [nm])
            for q in range(4):
                pb = ps[q % 2]
                tr(pb[:, 0:128], CX[:, q, :], 128, ZR + ["ident"], [f"ps{q % 2}"])
                for jj in range(4):
                    j = 4 * q + jj
                    for gl in range(2):
                        lo = gl * 64
                        c0 = 32 * jj + 16 * gl
                        V(lambda e, WC=WC, j=j, lo=lo, c0=c0, pb=pb, sgn=sgn: e.tensor_scalar(
                            WC[lo:lo + 64, j, c0:c0 + 16], pb[lo:lo + 64, c0:c0 + 16], sgn, None, ALU.mult),
                          [f"ps{q % 2}"], [nm])

        V(lambda e: e.tensor_copy(identb[:], ident[:]), ["ident"], ["identb"])
        for c in range(4):
            for jt in range(CW):
                V(lambda e, c=c, jt=jt: e.tensor_scalar(Dg[:, c, jt, :], identb[:], colsB[:, c, 5 + jt:6 + jt], None, ALU.mult),
                  ["identb", "colsB"], ["Dg"])
        V(lambda e: e.tensor_scalar(nEsL[:, :, 0], Es[:, :, SBLK - 1], -1.0, None, ALU.mult), ["E"], ["E2"])
        V(lambda e: e.tensor_scalar(nEsL[:, :, 1], Es[:, :, DSEQ - 1], -1.0, None, ALU.mult), ["E"], ["E2"])
        slot_names = [f"{sid}:wsl{k}" for sid in range(NSTREAM) for k in range(NSLOT)]
        V(lambda e: e.memset(ti[:, 0:1], 0), [], ["xin", "ti"] + [f"g{i}" for i in range(22)] + slot_names)
        xin = xin_real

        gcol = {"ffn1": lambda c: colsA[:, c, 0:1], "mix": lambda c: colsA[:, c, 1:2],
                "ffn2": lambda c: colsA[:, c, 2:3], "final": lambda c: colsA[:, c, 3:4]}

        def tile_slabs():
            L = []
            def ffn(f):
                r = []
                for s in range(11):
                    r.append((f + "_w1", s)); r.append((f + "_w3", s))
                for s in range(16):
                    r.append((f + "_w2", s))
                return r
            L += ffn("ffn1")
            for s in (0, 1, 4, 5, 2, 3, 6, 7, 8, 9, 10, 11, 12, 13):
                L.append(("w_in", s))
            L += [("ssm_glu_w", 0), ("ssm_glu_w", 1)]
            for s in range(4):
                L.append(("ssm_out_w", s)); L.append(("conv_out_w", s))
            for s in range(4):
                L.append(("w_o", s))
            L += ffn("ffn2")
            return L

        class Stream:
            pass

        SHARED_PREFIX = ("u32_", "um", "sg", "tt", "m0_", "m1_", "r0_", "r1_", "s0_", "s1_", "mean", "msq", "yssm")
        sh = Stream()
        sh.tt = [sb(f"tt{i}", [128, T]) for i in range(2)]
        sh.mrmi = [[sb(f"m{b}_{i}", [128, T]) for i in range(2)] for b in range(2)]
        sh.rri = [[sb(f"r{b}_{i}", [128, T]) for i in range(2)] for b in range(2)]
        sh.srsi = [[sb(f"s{b}_{i}", [128, T], MDT) for i in range(2)] for b in range(2)]
        sh.u32 = sb("u32", [128, 4, T]); sh.um = sb("um", [128, 4, T], MDT)
        sh.sg = sb("sg", [128, 4, T])
        sh.mean = sb("mean", [128, T]); sh.msq = sb("msq", [128, T])
        sh.yssm = [sb(f"yssm{i}", [128, T]) for i in range(2)]
        sh.pt = [sb(f"ptmp{i}", [128, T]) for i in range(2)]

        def make_stream(sid):
            S = Stream()
            S.sid = sid
            n = lambda x: x if x.startswith(SHARED_PREFIX) else f"{sid}:{x}"
            S.n = n
            pre = f"s{sid}_"
            S.xT = sb(pre + "xT", [128, 8, T])
            S.hbuf = sb(pre + "hbuf", [128, 8, T], MDT)
            S.gbuf = sb(pre + "gbuf", [128, 22 * T], MDT)
            S.gb32 = S.gbuf[:].bitcast(F32)
            S.Sre = sb(pre + "Sre", [128, NPAIR]); S.Sim = sb(pre + "Sim", [128, NPAIR])
            S.tt = sh.tt; S.mrmi = sh.mrmi; S.rri = sh.rri; S.srsi = sh.srsi
            S.s5c = sb(pre + "s5c", [128, 4])
            S.u32 = sh.u32; S.um = sh.um
            S.vp = sb(pre + "vp", [128, 4, HALO + T])
            S.vpb = sb(pre + "vpb", [128, 4, HALO + T], MDT)
            S.sg = sh.sg
            S.rstd = sb(pre + "rstd", [128, T]); S.mean = sh.mean; S.msq = sh.msq
            S.sq = [sb(pre + f"sq{i}", [128, T], MDT) for i in range(2)]
            S.ftmp = [sb(pre + f"ftmp{i}", [128, T]) for i in range(2)]
            S.yssm = sh.yssm
            S.wsl = [wslab[:, (sid * NSLOT + k) * SLOT_EL:(sid * NSLOT + k + 1) * SLOT_EL] for k in range(NSLOT)]
            S.pb = ps[4 * sid:4 * sid + 4]
            S.pbn = [f"ps{4 * sid + k}" for k in range(4)]
            S.rot = {"A": 0, "B": 0, "Y": 0, "sq": 0, "ft": 0}
            S.slab_seq = []
            S.state = {"next_load": 0, "cons": 0}
            return S

        def bankk(S, k, Tt=T):
            return S.pb[k][:, 0:Tt], S.pbn[k]

        def hb(S, kind, Tt=T):
            if kind == "A":
                k = 0 + 2 * S.rot["A"]
                S.rot["A"] ^= 1
            else:
                k = 1 + 2 * S.rot["B"]
                S.rot["B"] ^= 1
            return bankk(S, k, Tt)

        def issue_loads(S, upto):
            st = S.state
            while st["next_load"] <= min(upto, len(S.slab_seq) - 1):
                i = st["next_load"]
                nm, s = S.slab_seq[i]
                kc, ncol, ns = winfo[nm]
                slot = i % NSLOT
                dst = S.wsl[slot][:, 0:kc * ncol].rearrange("p (kc n) -> p kc n", kc=kc)
                src = dr["scr_" + nm][s].rearrange("p (kc n) -> p kc n", kc=kc)
                P.dma("sync", dst, src, reads=[f"scr_{nm}_{s}"], writes=[S.n(f"wsl{slot}")], sem=f"w{S.sid}_{slot}")
                st["next_load"] += 1

        def next_slab(S, nm_expect, s_expect, hold=2):
            st = S.state
            i = st["cons"]
            assert S.slab_seq[i] == (nm_expect, s_expect), (S.slab_seq[i], nm_expect, s_expect)
            issue_loads(S, i + NSLOT - hold)
            st["cons"] += 1
            kc, ncol, ns = winfo[nm_expect]
            slot = i % NSLOT
            return S.wsl[slot][:, 0:kc * ncol].rearrange("p (kc n) -> p kc n", kc=kc), S.n(f"wsl{slot}")

        MMC = 0.14

        def rmsnorm_stats(S, Tt, nchunks, src, src_res, scale):
            n = S.n
            pst, rst = bankk(S, 3, Tt)
            for c in range(nchunks):
                k = S.rot["sq"]; S.rot["sq"] ^= 1
                A(lambda e, c=c, k=k: e.activation(S.sq[k][:, :Tt], src(c), AF.Square), [src_res(c)], [n(f"sq{k}")])
                mm(pst, ones[:], S.sq[k][:, :Tt], c == 0, c == nchunks - 1, ["ones", n(f"sq{k}")], [rst])
            V(lambda e: e.tensor_scalar(S.rstd[:, :Tt], pst, scale, EPS, ALU.mult, ALU.add), [rst], [n("rstd")])
            A(lambda e: e.activation(S.rstd[:, :Tt], S.rstd[:, :Tt], AF.Ln), [n("rstd")], [n("rstd")])
            A(lambda e: e.activation(S.rstd[:, :Tt], S.rstd[:, :Tt], AF.Exp, scale=-0.5), [n("rstd")], [n("rstd")])

        def norm_to_h(S, Tt, which):
            n = S.n
            rmsnorm_stats(S, Tt, 8, lambda c: S.xT[:, c, :Tt], lambda c: n(f"xT{c}"), 1.0 / D)
            for c in range(8):
                V(lambda e, c=c: e.scalar_tensor_tensor(S.hbuf[:, c, :Tt], S.xT[:, c, :Tt], gcol[which](c), S.rstd[:, :Tt],
                                                        ALU.mult, ALU.mult), [n(f"xT{c}"), "colsA", n("rstd")], [n(f"h{c}")])

        def gch(S, i, Tt):
            return S.gbuf[:, i * T:i * T + Tt]

        def ffn(S, Tt, f):
            n = S.n
            norm_to_h(S, Tt, f)
            yield 6.0
            hres = [n(f"h{c}") for c in range(8)]
            for s in range(11):
                w1, r1 = next_slab(S, f + "_w1", s)
                w3, r3 = next_slab(S, f + "_w3", s)
                for i2 in range(2):
                    i = 2 * s + i2
                    pa, ra = hb(S, "A", Tt); pb_, rb = hb(S, "B", Tt)
                    for k in range(8):
                        mm(pa, w1[:, k, i2 * 128:(i2 + 1) * 128], S.hbuf[:, k, :Tt], k == 0, k == 7, [r1] + hres, [ra])
                    for k in range(8):
                        mm(pb_, w3[:, k, i2 * 128:(i2 + 1) * 128], S.hbuf[:, k, :Tt], k == 0, k == 7, [r3] + hres, [rb])
                    kt = S.rot["ft"]; S.rot["ft"] ^= 1
                    A(lambda e, pa=pa, kt=kt: e.activation(S.ftmp[kt][:, :Tt], pa, AF.Silu), [ra], [n(f"ftmp{kt}")])
                    V(lambda e, pb_=pb_, kt=kt, i=i: e.tensor_tensor(gch(S, i, Tt), S.ftmp[kt][:, :Tt], pb_, ALU.mult),
                      [n(f"ftmp{kt}"), rb], [n(f"g{i}")])
                    yield 16 * MMC
            gall = [n(f"g{i}") for i in range(22)]
            for n_ in range(8):
                py, ry = hb(S, "Y", Tt)
                for kh in range(2):
                    w2, r2 = next_slab(S, f + "_w2", 2 * n_ + kh, hold=1)
                    for k in range(11):
                        mm(py, w2[:, k, :], gch(S, kh * 11 + k, Tt), kh == 0 and k == 0, kh == 1 and k == 10, [r2] + gall, [ry])
                V(lambda e, py=py, n_=n_: e.scalar_tensor_tensor(S.xT[:, n_, :Tt], py, 0.5, S.xT[:, n_, :Tt],
                                                                 ALU.mult, ALU.add), [ry, n(f"xT{n_}")], [n(f"xT{n_}")])
                yield 22 * MMC

        def mixer(S, Tt):
            n = S.n
            norm_to_h(S, Tt, "mix")
            yield 6.0
            yield "lock"
            hres = [n(f"h{c}") for c in range(8)]
            gs = lambda n_: gch(S, n_, Tt)
            gc = lambda n_: gch(S, 8 + n_, Tt)
            cact = lambda c: gch(S, 16 + c, Tt)
            zz = lambda c: S.hbuf[:, c, :Tt]
            z2 = lambda c: S.hbuf[:, 4 + c, :Tt]
            u32, um, vp, sg = S.u32, S.um, S.vp, S.sg

            def proj_chunks(s, kind):
                w, rw = next_slab(S, "w_in", s, hold=1)
                outs = []
                for i2 in range(2):
                    pa, ra = hb(S, kind, Tt)
                    for k in range(8):
                        mm(pa, w[:, k, i2 * 128:(i2 + 1) * 128], S.hbuf[:, k, :Tt], k == 0, k == 7, [rw] + hres, [ra])
                    outs.append((pa, ra))
                return outs

            for s in (0, 1):
                for i2, (pa, ra) in enumerate(proj_chunks(s, "A")):
                    c = 2 * s + i2
                    A(lambda e, pa=pa, c=c: e.copy(u32[:, c, :Tt], pa), [ra], [n(f"u32_{c}")])
                    G(lambda e, c=c: e.tensor_copy(um[:, c, :Tt], u32[:, c, :Tt]), [n(f"u32_{c}")], [n(f"um{c}")])
                yield 16 * MMC
            for s in (4, 5):
                for i2, (pa, ra) in enumerate(proj_chunks(s, "A")):
                    c = 2 * (s - 4) + i2
                    A(lambda e, pa=pa, c=c: e.activation(sg[:, c, :Tt], pa, AF.Sigmoid), [ra], [n(f"sg{c}")])
                yield 16 * MMC
            for s in (2, 3):
                for i2, (pa, ra) in enumerate(proj_chunks(s, "B")):
                    c = 2 * (s - 2) + i2
                    V(lambda e, pa=pa, c=c: e.tensor_tensor(vp[:, c, HALO:HALO + Tt], pa, sg[:, c, :Tt], ALU.mult),
                      [ra, n(f"sg{c}")], [n(f"vp{c}")])
                yield 16 * MMC
            for s in range(6, 14):
                for i2, (pa, ra) in enumerate(proj_chunks(s, "A")):
                    n_ = 2 * (s - 6) + i2
                    A(lambda e, pa=pa, n_=n_: e.activation(gch(S, n_, Tt), pa, AF.Sigmoid), [ra], [n(f"g{n_}")])
                yield 16 * MMC

            nblk = max(1, Tt // SBLK)
            L = min(SBLK, Tt)
            Sre, Sim, s5c = S.Sre, S.Sim, S.s5c
            tt1, tt2 = S.tt
            rtt = [n("tt0"), n("tt1")]
            pbr, rbr = bankk(S, 0, Tt)
            pbi, rbi = bankk(S, 2, Tt)

            def emit_bu(j):
                q = j // 4
                mm(pbr, WBr[:, j, :], um[:, q, :Tt], True, True, ["WBr", n(f"um{q}")], [rbr])
                mm(pbi, WBi[:, j, :], um[:, q, :Tt], True, True, ["WBi", n(f"um{q}")], [rbi])

            def emit_c(j):
                q, jj = j // 4, j % 4
                par = j % 2
                sr, si = S.srsi[par]
                py, ry = bankk(S, 1 + 2 * (q % 2), Tt)
                mm(py, WCr[:, j, :], sr[:, :Tt], jj == 0, False, ["WCr", n(f"s{par}_0")], [ry])
                mm(py, WCi[:, j, :], si[:, :Tt], False, jj == 3, ["WCi", n(f"s{par}_1")], [ry])
                if jj == 3:
                    k = q % 2
                    V(lambda e, q=q, py=py, k=k: e.scalar_tensor_tensor(S.yssm[k][:, :Tt], u32[:, q, :Tt], colsB[:, q, 0:1], py,
                                                                        ALU.mult, ALU.add), [n(f"u32_{q}"), "colsB", ry], [n(f"yssm{k}")])
                    A(lambda e, q=q, k=k: e.activation(zz(q), S.yssm[k][:, :Tt], AF.Gelu_apprx_tanh), [n(f"yssm{k}")], [n(f"h{q}")])

            v3 = lambda ap: ap[:, :Tt].rearrange("p (b l) -> p b l", l=L)

            def pair_ctx(j):
                par = j % 2
                c = {"j": j, "par": par}
                c["mr"], c["mi"] = S.mrmi[par]
                c["rr"], c["ri"] = S.rri[par]
                c["sr"], c["si"] = S.srsi[par]
                c["rm"] = [n(f"m{par}_0"), n(f"m{par}_1")]
                c["rrs"] = [n(f"r{par}_0"), n(f"r{par}_1")]
                c["rss"] = [n(f"s{par}_0"), n(f"s{par}_1")]
                c["ecb"] = Ec[:, j:j + 1, 0:L].to_broadcast([128, nblk, L])
                c["esb"] = Es[:, j:j + 1, 0:L].to_broadcast([128, nblk, L])
                c["ab"] = amag[:, j:j + 1].to_broadcast([128, L])
                c["ecl"] = Ec[:, j, L - 1:L]; c["esl"] = Es[:, j, L - 1:L]
                return c

            def mod_ops(c):
                rr, ri, mr, mi, ecb, esb, rrs, rm = c["rr"], c["ri"], c["mr"], c["mi"], c["ecb"], c["esb"], c["rrs"], c["rm"]
                return [
                    lambda: V(lambda e: e.tensor_tensor(v3(tt1), v3(rr), ecb, ALU.mult), [rrs[0], "E"], [rtt[0]]),
                    lambda: V(lambda e: e.tensor_tensor(v3(tt2), v3(ri), esb, ALU.mult), [rrs[1], "E"], [rtt[1]]),
                    lambda: V(lambda e: e.tensor_tensor(mr[:, :Tt], tt1[:, :Tt], tt2[:, :Tt], ALU.add), rtt, [rm[0]]),
                    lambda: V(lambda e: e.tensor_tensor(v3(tt1), v3(ri), ecb, ALU.mult), [rrs[1], "E"], [rtt[0]]),
                    lambda: V(lambda e: e.tensor_tensor(v3(tt2), v3(rr), esb, ALU.mult), [rrs[0], "E"], [rtt[1]]),
                    lambda: V(lambda e: e.tensor_tensor(mi[:, :Tt], tt1[:, :Tt], tt2[:, :Tt], ALU.subtract), rtt, [rm[1]]),
                ]

            def scan_ops(c):
                j, par = c["j"], c["par"]
                rr, ri, mr, mi, ab, ecl, esl, rrs, rm = c["rr"], c["ri"], c["mr"], c["mi"], c["ab"], c["ecl"], c["esl"], c["rrs"], c["rm"]
                sc0, sc1 = s5c[:, 2 * par:2 * par + 1], s5c[:, 2 * par + 1:2 * par + 2]
                rsc = n(f"s5c{par}")
                nesl = nEsL[:, j, (0 if L == SBLK else 1):(1 if L == SBLK else 2)]
                ops = []
                for b in range(nblk):
                    c0 = b * L
                    l1 = c0 + L - 1
                    ops += [
                        lambda c0=c0: V(lambda e: e.tensor_tensor_scan(rr[:, c0:c0 + L], ab, mr[:, c0:c0 + L], Sre[:, j:j + 1], ALU.mult, ALU.add),
                                        [rm[0], "amag", n(f"S{j}")], [rrs[0]]),
                        lambda c0=c0: V(lambda e: e.tensor_tensor_scan(ri[:, c0:c0 + L], ab, mi[:, c0:c0 + L], Sim[:, j:j + 1], ALU.mult, ALU.add),
                                        [rm[1], "amag", n(f"S{j}")], [rrs[1]]),
                        lambda l1=l1: A(lambda e: e.activation(sc0, ri[:, l1:l1 + 1], AF.Identity, scale=nesl), [rrs[1], "E2"], [rsc]),
                        lambda l1=l1: A(lambda e: e.activation(sc1, rr[:, l1:l1 + 1], AF.Identity, scale=esl), [rrs[0], "E"], [rsc]),
                        lambda l1=l1: A(lambda e: e.activation(Sre[:, j:j + 1], rr[:, l1:l1 + 1], AF.Identity, scale=ecl, bias=sc0),
                                        [rrs[0], "E", rsc], [n(f"S{j}")]),
                        lambda l1=l1: A(lambda e: e.activation(Sim[:, j:j + 1], ri[:, l1:l1 + 1], AF.Identity, scale=ecl, bias=sc1),
                                        [rrs[1], "E", rsc], [n(f"S{j}")]),
                    ]
                return ops

            def demod(c):
                p1, p2 = sh.pt
                p3, p4 = c["mr"], c["mi"]
                rp = ["ptmp0", "ptmp1"]
                rr, ri, sr, si, ecb, esb, rrs, rss, rm = c["rr"], c["ri"], c["sr"], c["si"], c["ecb"], c["esb"], c["rrs"], c["rss"], c["rm"]
                G(lambda e: e.tensor_tensor(v3(p1), v3(rr), ecb, ALU.mult), [rrs[0], "E"], [rp[0]])
                G(lambda e: e.tensor_tensor(v3(p2), v3(ri), esb, ALU.mult), [rrs[1], "E"], [rp[1]])
                G(lambda e: e.tensor_tensor(v3(p3), v3(ri), ecb, ALU.mult), [rrs[1], "E"], [rm[0]])
                G(lambda e: e.tensor_tensor(v3(p4), v3(rr), esb, ALU.mult), [rrs[0], "E"], [rm[1]])
                G(lambda e: e.tensor_tensor(sr[:, :Tt], p1[:, :Tt], p2[:, :Tt], ALU.subtract), rp, [rss[0]])
                G(lambda e: e.tensor_tensor(si[:, :Tt], p3[:, :Tt], p4[:, :Tt], ALU.add), rm, [rss[1]])

            emit_bu(0)
            prev = None
            for j in range(NPAIR + 1):
                cur = None
                mods = []
                if j < NPAIR:
                    cur = pair_ctx(j)
                    A(lambda e, o=cur["rr"]: e.copy(o[:, :Tt], pbr), [rbr], [cur["rrs"][0]])
                    A(lambda e, o=cur["ri"]: e.copy(o[:, :Tt], pbi), [rbi], [cur["rrs"][1]])
                    if j + 1 < NPAIR:
                        emit_bu(j + 1)
                    mods = mod_ops(cur)
                scans = scan_ops(prev) if prev is not None else []
                lead = min(4, len(scans)) if mods else len(scans)
                for op in scans[:lead]:
                    op()
                rest = scans[lead:]
                while mods or rest:
                    if mods:
                        mods.pop(0)()
                    if rest:
                        rest.pop(0)()
                if prev is not None:
                    if j >= 3:
                        emit_c(j - 3)
                    demod(prev)
                prev = cur
                yield 8.0
            emit_c(NPAIR - 2)
            emit_c(NPAIR - 1)
            yield 1.0

            zres = [n(f"h{c}") for c in range(4)]
            for s in range(2):
                w, rw = next_slab(S, "ssm_glu_w", s, hold=1)
                for i2 in range(2):
                    n_ = 2 * s + i2
                    pa, ra = hb(S, "A", Tt)
                    for k in range(4):
                        mm(pa, w[:, k, i2 * 128:(i2 + 1) * 128], zz(k), k == 0, k == 3, [rw] + zres, [ra])
                    kt = S.rot["ft"]; S.rot["ft"] ^= 1
                    A(lambda e, pa=pa, kt=kt, n_=n_: e.activation(S.ftmp[kt][:, :Tt], pa, AF.Sigmoid, bias=colsB[:, n_, 1:2]),
                      [ra, "colsB"], [n(f"ftmp{kt}")])
                    V(lambda e, kt=kt, n_=n_: e.tensor_tensor(z2(n_), zz(n_), S.ftmp[kt][:, :Tt], ALU.mult),
                      [n(f"ftmp{kt}"), n(f"h{n_}")], [n(f"h{4 + n_}")])
                yield 8 * MMC + 0.5

            vpb = S.vpb
            for c in range(4):
                A(lambda e, c=c: e.copy(vpb[:, c, 0:HALO + Tt], vp[:, c, 0:HALO + Tt]), [n(f"vp{c}")], [n(f"vpb{c}")])
            for c in range(4):
                pa, ra = hb(S, "A", Tt)
                for jt in range(CW):
                    mm(pa, Dg[:, c, jt, :], vpb[:, c, jt:jt + Tt], jt == 0, jt == CW - 1, ["Dg", n(f"vpb{c}")], [ra])
                A(lambda e, c=c, pa=pa: e.activation(sg[:, c, :Tt], pa, AF.Identity, bias=colsB[:, c, 2:3]), [ra, "colsB"], [n(f"sg{c}")])
                yield CW * MMC
            psm, rsm = bankk(S, 1, Tt)
            pst, rst = bankk(S, 3, Tt)
            for c in range(4):
                A(lambda e, c=c: e.copy(um[:, c, :Tt], sg[:, c, :Tt]), [n(f"sg{c}")], [n(f"um{c}")])
                mm(psm, ones[:], um[:, c, :Tt], c == 0, c == 3, ["ones", n(f"um{c}")], [rsm])
            V(lambda e: e.tensor_scalar(S.mean[:, :Tt], psm, 1.0 / DS, None, ALU.mult), [rsm], [n("mean")])
            for c in range(4):
                k = S.rot["sq"]; S.rot["sq"] ^= 1
                A(lambda e, c=c, k=k: e.activation(S.sq[k][:, :Tt], sg[:, c, :Tt], AF.Square), [n(f"sg{c}")], [n(f"sq{k}")])
                mm(pst, ones[:], S.sq[k][:, :Tt], c == 0, c == 3, ["ones", n(f"sq{k}")], [rst])
            V(lambda e: e.tensor_tensor(S.msq[:, :Tt], S.mean[:, :Tt], S.mean[:, :Tt], ALU.mult), [n("mean")], [n("msq")])
            V(lambda e: e.scalar_tensor_tensor(S.rstd[:, :Tt], pst, 1.0 / DS, S.msq[:, :Tt], ALU.mult, ALU.subtract),
              [rst, n("msq")], [n("rstd")])
            V(lambda e: e.tensor_scalar(S.rstd[:, :Tt], S.rstd[:, :Tt], EPS, None, ALU.add), [n("rstd")], [n("rstd")])
            A(lambda e: e.activation(S.rstd[:, :Tt], S.rstd[:, :Tt], AF.Ln), [n("rstd")], [n("rstd")])
            A(lambda e: e.activation(S.rstd[:, :Tt], S.rstd[:, :Tt], AF.Exp, scale=-0.5), [n("rstd")], [n("rstd")])
            for c in range(4):
                V(lambda e, c=c: e.tensor_tensor(sg[:, c, :Tt], sg[:, c, :Tt], S.mean[:, :Tt], ALU.subtract), [n(f"sg{c}"), n("mean")], [n(f"sg{c}")])
                V(lambda e, c=c: e.tensor_tensor(sg[:, c, :Tt], sg[:, c, :Tt], S.rstd[:, :Tt], ALU.mult), [n(f"sg{c}"), n("rstd")], [n(f"sg{c}")])
                A(lambda e, c=c: e.activation(cact(c), sg[:, c, :Tt], AF.Silu, bias=colsB[:, c, 4:5], scale=colsB[:, c, 3:4]),
                  [n(f"sg{c}"), "colsB"], [n(f"g{16 + c}")])
            yield "unlock"
            yield 8.0
            z2res = [n(f"h{4 + c}") for c in range(4)]
            cres = [n(f"g{16 + c}") for c in range(4)]
            for s in range(4):
                ws, rws = next_slab(S, "ssm_out_w", s)
                wc, rwc = next_slab(S, "conv_out_w", s)
                for i2 in range(2):
                    n_ = 2 * s + i2
                    pa, ra = bankk(S, 1, Tt); pb_, rb = bankk(S, 3, Tt)
                    for k in range(4):
                        mm(pa, ws[:, k, i2 * 128:(i2 + 1) * 128], z2(k), k == 0, k == 3, [rws] + z2res, [ra])
                    for k in range(4):
                        mm(pb_, wc[:, k, i2 * 128:(i2 + 1) * 128], cact(k), k == 0, k == 3, [rwc] + cres, [rb])
                    V(lambda e, pa=pa, n_=n_: e.tensor_tensor(S.ftmp[0][:, :Tt], pa, gs(n_), ALU.mult), [ra, n(f"g{n_}")], [n("ftmp0")])
                    V(lambda e, pb_=pb_, n_=n_: e.tensor_tensor(S.ftmp[1][:, :Tt], pb_, gc(n_), ALU.mult), [rb, n(f"g{8 + n_}")], [n("ftmp1")])
                    V(lambda e, n_=n_: e.tensor_tensor(gs(n_), S.ftmp[0][:, :Tt], S.ftmp[1][:, :Tt], ALU.add), [n("ftmp0"), n("ftmp1")], [n(f"g{n_}")])
                yield 16 * MMC + 1.0
            mres = [n(f"g{n_}") for n_ in range(8)]
            for s in range(4):
                w, rw = next_slab(S, "w_o", s, hold=1)
                for i2 in range(2):
                    n_ = 2 * s + i2
                    py, ry = hb(S, "Y", Tt)
                    for k in range(8):
                        mm(py, w[:, k, i2 * 128:(i2 + 1) * 128], gs(k), k == 0, k == 7, [rw] + mres, [ry])
                    V(lambda e, py=py, n_=n_: e.tensor_tensor(S.xT[:, n_, :Tt], S.xT[:, n_, :Tt], py, ALU.add), [ry, n(f"xT{n_}")], [n(f"xT{n_}")])
                yield 16 * MMC

        def emit_conv_state(S, Tt, dst_ap):
            n = S.n
            bank = S.pb[0]
            rb = [S.pbn[0]]
            for c in range(4):
                tr(bank[:HALO, c * 128:(c + 1) * 128], S.vp[:, c, Tt:Tt + HALO], 128, [n(f"vp{c}"), "ident"], rb)
            A(lambda e: e.copy(ncst[:, :], bank[:HALO, 0:512]), rb, ["ncst"])
            P.dma("scalar", dst_ap, ncst[:, :], reads=["ncst"], sem="o_nc")

        def emit_ssm_state(S, dst_re, dst_im):
            n = S.n
            bank = S.pb[0]
            rb = [S.pbn[0]]
            Sall = [n(f"S{j}") for j in range(16)]
            tr(bank[:16, 0:128], S.Sre[:, :], 128, Sall + ["ident"], rb)
            tr(bank[:16, 128:256], S.Sim[:, :], 128, Sall + ["ident"], rb)
            A(lambda e: e.copy(sst[:, :], bank[:16, 0:256]), rb, ["sst"])
            P.dma("scalar", dst_re.rearrange("(j gl) p -> j (gl p)", gl=2), sst[:, 0:128], reads=["sst"], sem="o_st")
            P.dma("scalar", dst_im.rearrange("(j gl) p -> j (gl p)", gl=2), sst[:, 128:256], reads=["sst"], sem="o_st")

        def load_x(Tt, src_ap):
            nb = (Tt + 127) // 128
            tb = min(128, Tt)
            P.dma("sync", xin[:tb, 0:nb * 1024].rearrange("p (b d) -> p b d", b=nb),
                  src_ap.rearrange("(b p) d -> p b d", p=tb), writes=["xin"], sem="xin")

        def transpose_in(S, Tt):
            n = S.n
            nb = (Tt + 127) // 128
            tb = min(128, Tt)
            for c in range(8):
                py, ry = hb(S, "A", Tt)
                for b in range(nb):
                    tr(py[:, b * tb:(b + 1) * tb], xin[:tb, b * 1024 + c * 128: b * 1024 + (c + 1) * 128], tb, ["xin", "ident"], [ry])
                A(lambda e, c=c, py=py: e.copy(S.xT[:, c, :Tt], py), [ry], [n(f"xT{c}")])

        def emit_output(S, Tt, dst_ap):
            n = S.n
            nb = (Tt + 127) // 128
            tb = min(128, Tt)
            rmsnorm_stats(S, Tt, 8, lambda c: S.xT[:, c, :Tt], lambda c: n(f"xT{c}"), 1.0 / D)
            for c in range(8):
                V(lambda e, c=c: e.scalar_tensor_tensor(S.xT[:, c, :Tt], S.xT[:, c, :Tt], gcol["final"](c), S.rstd[:, :Tt],
                                                        ALU.mult, ALU.mult), [n(f"xT{c}"), "colsA", n("rstd")], [n(f"xT{c}")])
            bank = S.pb[0]
            rb = [S.pbn[0]]
            for b in range(nb):
                slot = b % 2
                ysl = S.gb32[:tb, slot * 1024:(slot + 1) * 1024]
                yres = [n(f"g{i}") for i in range(slot * 8, slot * 8 + 9)]
                for hf in range(2):
                    for c4 in range(4):
                        c = hf * 4 + c4
                        tr(bank[:tb, c4 * 128:(c4 + 1) * 128], S.xT[:, c, b * tb:(b + 1) * tb], 128, [n(f"xT{c}"), "ident"], rb)
                    A(lambda e, hf=hf, ysl=ysl: e.copy(ysl[:, hf * 512:(hf + 1) * 512], bank[:tb, 0:512]), rb, yres)
                P.dma("scalar", dst_ap[b * tb:(b + 1) * tb, :], ysl, reads=yres, sem=f"o_y{S.sid}_{slot}")

        xin_state = {"owner": None}

        def stream_gen(S, tiles, tps):
            n = S.n
            S.prefetched = False
            for idx, t in enumerate(tiles):
                kind, sq_i, ti_ = t
                if kind == "p":
                    Tt = T
                    src = dr["xp"][sq_i, ti_ * T:(ti_ + 1) * T, :]
                else:
                    Tt = DSEQ
                    src = dr["xs"][:, :]
                first = ti_ == 0
                last = (kind == "s") or (ti_ == tps - 1)
                if not S.prefetched:
                    while xin_state["owner"] is not None:
                        yield 0.5
                    load_x(Tt, src)
                    xin_state["owner"] = S.sid
                transpose_in(S, Tt)
                xin_state["owner"] = None
                S.prefetched = False
                yield 4.0
                if first:
                    Sall = [n(f"S{j}") for j in range(16)]
                    if kind == "p":
                        V(lambda e: e.memset(S.Sre[:], 0.0), [], Sall)
                        V(lambda e: e.memset(S.Sim[:], 0.0), [], Sall)
                        for c in range(4):
                            G(lambda e, c=c: e.memset(S.vp[:, c, 0:HALO], 0.0), [], [n(f"vp{c}")])
                    else:
                        bank = S.pb[0]
                        rb = [S.pbn[0]]
                        for (srcS, dstS) in ((dr["sre"], S.Sre), (dr["sim"], S.Sim)):
                            P.dma("sync", sst[0:16, 0:128], srcS.rearrange("(j gl) p -> j (gl p)", gl=2), writes=["sst"], sem="su")
                            tr(bank[:, 0:16], sst[0:16, 0:128], 16, ["sst", "ident"], rb)
                            A(lambda e, dstS=dstS: e.copy(dstS[:], bank[:, 0:16]), rb, Sall)
                        P.dma("sync", ncst[:, :], dr["cc"], writes=["ncst"], sem="su")
                        for c in range(4):
                            tr(bank[:, c * HALO:(c + 1) * HALO], ncst[:, c * 128:(c + 1) * 128], HALO, ["ncst", "ident"], rb)
                        for c in range(4):
                            A(lambda e, c=c: e.copy(S.vp[:, c, 0:HALO], bank[:, c * HALO:(c + 1) * HALO]), rb, [n(f"vp{c}")])
                yield from ffn(S, Tt, "ffn1")
                yield from mixer(S, Tt)
                if last:
                    if kind == "p":
                        emit_conv_state(S, Tt, dr["ncp"][sq_i])
                        emit_ssm_state(S, dr["nrp"][sq_i], dr["nip"][sq_i])
                    else:
                        emit_conv_state(S, Tt, dr["ncs"])
                        emit_ssm_state(S, dr["nrs"], dr["nis"])
                else:
                    for c in range(4):
                        G(lambda e, c=c, Tt=Tt: e.tensor_copy(S.vp[:, c, 0:HALO], S.vp[:, c, Tt:Tt + HALO]), [n(f"vp{c}")], [n(f"vp{c}")])
                if idx + 1 < len(tiles) and xin_state["owner"] is None:
                    xin_state["owner"] = S.sid
                    S.prefetched = True
                    k2, s2, t2 = tiles[idx + 1]
                    if k2 == "p":
                        load_x(T, dr["xp"][s2, t2 * T:(t2 + 1) * T, :])
                    else:
                        load_x(DSEQ, dr["xs"][:, :])
                yield from ffn(S, Tt, "ffn2")
                if kind == "p":
                    emit_output(S, Tt, dr["yp"][sq_i, ti_ * T:(ti_ + 1) * T, :])
                else:
                    emit_output(S, Tt, dr["ys"])
                yield 4.0

        tps = seq_len // T
        streams = [make_stream(0), make_stream(1)]
        tile_lists = [[], []]
        for sq_i in range(n_prompt_seq):
            sid = 0 if sq_i < (n_prompt_seq + 1) // 2 else 1
            for ti_ in range(tps):
                tile_lists[sid].append(("p", sq_i, ti_))
        if do_sample:
            tile_lists[1].append(("s", 0, 0))
        gens = []
        for S, tl in zip(streams, tile_lists):
            for _ in tl:
                S.slab_seq.extend(tile_slabs())
            gens.append(stream_gen(S, tl, tps))
        clocks = [0.0, 230.0]
        alive = [len(tile_lists[0]) > 0, len(tile_lists[1]) > 0]
        waiting = [False, False]
        owner = None
        while any(alive):
            cands = [k for k in range(2) if alive[k] and not (waiting[k] and owner is not None)]
            i = min(cands, key=lambda k: clocks[k])
            if waiting[i]:
                waiting[i] = False
                owner = i
                clocks[i] = max(clocks[i], clocks[1 - i])
                continue
            try:
                r = next(gens[i])
            except StopIteration:
                alive[i] = False
                if owner == i:
                    owner = None
                continue
            if r == "lock":
                if owner is None:
                    owner = i
                else:
                    waiting[i] = True
            elif r == "unlock":
                owner = None
            else:
                clocks[i] += r
        P.wait_all("sync")
        P.build()
    return nc, P


_CACHE = {}


def kernel(**inputs):
    inp = {k: np.ascontiguousarray(np.asarray(v)) for k, v in inputs.items()}
    if "nc" not in _CACHE:
        _CACHE["nc"] = build_program()[0]
    nc = _CACHE["nc"]
    in_maps = []
    for c in range(NCORES):
        m = {
            "xp": inp["x_prompt"][c * BPC:(c + 1) * BPC],
            "xs": inp["x_sample"][c],
            "sre": inp["state_ssm_re"][0, c],
            "sim": inp["state_ssm_im"][0, c],
            "cc": inp["cache_conv"][0, c],
        }
        for nm, K, N in WEIGHTS:
            m[nm] = inp[nm][0]
        for nm, shp in SMALL:
            m[nm] = inp[nm] if nm == "final_norm" else inp[nm][0]
        in_maps.append({k: np.ascontiguousarray(v, dtype=np.float32) for k, v in m.items()})
    res = run_bass_kernel_spmd(nc, in_maps, core_ids=list(range(NCORES)))
    R = res.results
    cat = lambda k: np.concatenate([r[k] for r in R], axis=0)
    stk = lambda k: np.stack([r[k] for r in R], axis=0)
    y_prompt = cat("yp")
    y_sample = stk("ys")
    nrp = cat("nrp")[None]; nip = cat("nip")[None]; ncp = cat("ncp")[None]
    nrs = stk("nrs")[None]; nis = stk("nis")[None]; ncs = stk("ncs")[None]
    return (y_prompt, y_sample, nrp, nip, ncp, nrs, nis, ncs)
```
